# Optimizing a Trainium2 kernel written in Bass

```python
import jax, jax.numpy as jnp
from jax import lax
import numpy as np

D_MODEL = 1024
BATCH = 4
SEQ = 8192
DEPTH = 1

GRID_W = 64
MEM_TOKENS = 256
MLSTM_HEADS = 4
MLSTM_QK_DIM = 64
MLSTM_V_DIM = 128
MLSTM_CHUNK = 64
MLSTM_CONV = 5
NA_HEADS = 8
NA_HEAD_DIM = 32
NA_WIN_ROWS = 8
NA_WIN_COLS = 16
MEM_HEADS = 4
MEM_HEAD_DIM = 64
N_BRANCH = 3
D_FF = 4 * D_MODEL
EPS = 1e-6

MLSTM_QK_W = MLSTM_HEADS * MLSTM_QK_DIM
MLSTM_V_W = MLSTM_HEADS * MLSTM_V_DIM
NA_W = NA_HEADS * NA_HEAD_DIM
MEM_W = MEM_HEADS * MEM_HEAD_DIM
IN_SPLITS = (2 * MLSTM_QK_W, MLSTM_V_W, MLSTM_V_W, 2 * MLSTM_HEADS, 2 * MLSTM_HEADS,
             3 * NA_W, MEM_W, N_BRANCH * D_MODEL)
D_IN = sum(IN_SPLITS)

kernel_name = 'hybrid_mlstm_natten_memxattn_block'


def rmsnorm(x, g):
    xf = x.astype(jnp.float32)
    y = xf * lax.rsqrt(jnp.mean(xf * xf, axis=-1, keepdims=True) + EPS)
    return (y * g.astype(jnp.float32)).astype(x.dtype)


def short_conv(x, w, b):
    c = x.shape[-1]
    y = lax.conv_general_dilated(
        x, w[:, None, :].astype(x.dtype), window_strides=(1,),
        padding=[(MLSTM_CONV // 2, MLSTM_CONV // 2)],
        dimension_numbers=('NWC', 'WIO', 'NWC'), feature_group_count=c)
    return y + b.astype(x.dtype)


def _mlstm_chunk_step(carry, inp):
    c_st, n_st, m_st = carry
    q, k, v, ig, lf = inp
    L = q.shape[-2]
    tril = jnp.tril(jnp.ones((L, L), dtype=bool))
    b = jnp.cumsum(lf, axis=-1)
    d_log = jnp.where(tril, b[..., :, None] - b[..., None, :] + ig[..., None, :], -jnp.inf)
    inter = b + m_st[..., None]
    m_t = jnp.maximum(inter, jnp.max(d_log, axis=-1))
    s = jnp.einsum('bhtd,bhsd->bhts', q, k) * jnp.exp(d_log - m_t[..., None])
    w_inter = jnp.exp(inter - m_t)
    num = (jnp.einsum('bhts,bhsv->bhtv', s, v)
           + w_inter[..., None] * jnp.einsum('bhvd,bhtd->bhtv', c_st, q))
    den = jnp.sum(s, axis=-1) + w_inter * jnp.einsum('bhd,bhtd->bht', n_st, q)
    h = num / jnp.maximum(jnp.abs(den), jnp.exp(-m_t))[..., None]
    b_last = b[..., -1]
    w_log = b_last[..., None] - b + ig
    m_new = jnp.maximum(b_last + m_st, jnp.max(w_log, axis=-1))
    decay = jnp.exp(b_last + m_st - m_new)
    ws = jnp.exp(w_log - m_new[..., None])
    c_new = decay[..., None, None] * c_st + jnp.einsum('bhs,bhsv,bhsd->bhvd', ws, v, k)
    n_new = decay[..., None] * n_st + jnp.einsum('bhs,bhsd->bhd', ws, k)
    return (c_new, n_new, m_new), h


def mlstm_chunkwise(q, k, v, ig, lf):
    bsz, s, nh, dk = q.shape
    dv = v.shape[-1]
    nc = s // MLSTM_CHUNK

    def chunks(a):
        a = a.reshape((bsz, nc, MLSTM_CHUNK, nh) + a.shape[3:])
        return jnp.swapaxes(jnp.moveaxis(a, 1, 0), 2, 3)

    init = (jnp.zeros((bsz, nh, dv, dk), jnp.float32),
            jnp.zeros((bsz, nh, dk), jnp.float32),
            jnp.full((bsz, nh), -jnp.inf, jnp.float32))
    _, hs = lax.scan(_mlstm_chunk_step, init,
                     (chunks(q), chunks(k), chunks(v), chunks(ig), chunks(lf)))
    return jnp.swapaxes(jnp.moveaxis(hs, 0, 1), 2, 3).reshape(bsz, s, nh, dv)


def neighbourhood_attention(q, k, v, rpb):
    bsz, s, nh, dh = q.shape
    rows = s // GRID_W
    win_r = min(NA_WIN_ROWS, rows)
    grid = lambda a: a.reshape(bsz, rows, GRID_W, nh, dh)
    qg, kg, vg = grid(q), grid(k), grid(v)
    row_start = jnp.clip(jnp.arange(rows) - win_r // 2, 0, rows - win_r)
    cols = jnp.arange(GRID_W)
    col_idx = (jnp.clip(cols - NA_WIN_COLS // 2, 0, GRID_W - NA_WIN_COLS)[:, None]
               + jnp.arange(NA_WIN_COLS))
    col_off = col_idx - cols[:, None] + (NA_WIN_COLS - 1)

    def one_row(r):
        rs = row_start[r]
        q_row = lax.dynamic_index_in_dim(qg, r, axis=1, keepdims=False)
        k_win = lax.dynamic_slice_in_dim(kg, rs, win_r, axis=1)[:, :, col_idx]
        v_win = lax.dynamic_slice_in_dim(vg, rs, win_r, axis=1)[:, :, col_idx]
        row_off = rs + jnp.arange(win_r) - r + (NA_WIN_ROWS - 1)
        bias = rpb[:, row_off[None, :, None], col_off[:, None, :]]
        sc = (jnp.einsum('bchd,bicjhd->bhcij', q_row, k_win).astype(jnp.float32)
              + bias.astype(jnp.float32)[None])
        p = jax.nn.softmax(sc.reshape(bsz, nh, GRID_W, win_r * NA_WIN_COLS), axis=-1)
        p = p.reshape(sc.shape).astype(v.dtype)
        return jnp.einsum('bhcij,bicjhd->bchd', p, v_win)

    out = lax.map(one_row, jnp.arange(rows))
    return jnp.moveaxis(out, 0, 1).reshape(bsz, s, nh * dh)


def memory_cross_attention(q, mem_n, w_mem_kv, g_mem_q, g_mem_k):
    bsz, s, _ = q.shape
    m = mem_n.shape[1]
    k, v = jnp.split(mem_n @ w_mem_kv, 2, axis=-1)
    q = rmsnorm(q.reshape(bsz, s, MEM_HEADS, MEM_HEAD_DIM), g_mem_q) * (MEM_HEAD_DIM ** -0.5)
    k = rmsnorm(k.reshape(bsz, m, MEM_HEADS, MEM_HEAD_DIM), g_mem_k)
    v = v.reshape(bsz, m, MEM_HEADS, MEM_HEAD_DIM)
    sc = jnp.einsum('bshd,bmhd->bhsm', q, k).astype(jnp.float32)
    p = jax.nn.softmax(sc, axis=-1).astype(v.dtype)
    return jnp.einsum('bhsm,bmhd->bshd', p, v).reshape(bsz, s, MEM_W)


def hybrid_layer(x, mem, g_mix, w_in, conv_w, conv_b, b_igate, b_fgate, g_mlstm, w_proj_mlstm,
                 g_na_q, g_na_k, rpb, w_proj_na, g_mem, w_mem_kv, g_mem_q, g_mem_k,
                 w_proj_mem, w_out, g_ffn, w_up, w_down):
    bsz, s, _ = x.shape
    h = rmsnorm(x, g_mix)
    z = h @ w_in
    split_idx = np.cumsum(np.array(IN_SPLITS))[:-1]
    qk_m, v_m, o_m, i_pre, f_pre, qkv_na, q_mem, gate_pre = jnp.split(z, split_idx, axis=-1)

    qk_m = jax.nn.silu(short_conv(qk_m, conv_w, conv_b))
    q_m, k_m = jnp.split(qk_m, 2, axis=-1)
    q_m = q_m.reshape(bsz, s, MLSTM_HEADS, MLSTM_QK_DIM).astype(jnp.float32) * (MLSTM_QK_DIM ** -0.5)
    k_m = k_m.reshape(bsz, s, MLSTM_HEADS, MLSTM_QK_DIM).astype(jnp.float32)
    vv = v_m.reshape(bsz, s, MLSTM_HEADS, MLSTM_V_DIM).astype(jnp.float32)
    i_log = i_pre.reshape(bsz, s, 2, MLSTM_HEADS).astype(jnp.float32) + b_igate.astype(jnp.float32)
    f_log = jax.nn.log_sigmoid(f_pre.reshape(bsz, s, 2, MLSTM_HEADS).astype(jnp.float32)
                               + b_fgate.astype(jnp.float32))
    flip = lambda a: jnp.flip(a, axis=1)
    h_fwd = mlstm_chunkwise(q_m, k_m, vv, i_log[:, :, 0], f_log[:, :, 0])
    h_bwd = flip(mlstm_chunkwise(flip(q_m), flip(k_m), flip(vv),
                                 flip(i_log[:, :, 1]), flip(f_log[:, :, 1])))
    h_m = rmsnorm(h_fwd + h_bwd, g_mlstm.reshape(MLSTM_HEADS, MLSTM_V_DIM))
    h_m = (h_m.reshape(bsz, s, MLSTM_V_W) * jax.nn.sigmoid(o_m.astype(jnp.float32))).astype(x.dtype)

    q_na, k_na, v_na = jnp.split(qkv_na.reshape(bsz, s, 3, NA_HEADS, NA_HEAD_DIM), 3, axis=2)
    q_na = rmsnorm(q_na[:, :, 0], g_na_q) * (NA_HEAD_DIM ** -0.5)
    k_na = rmsnorm(k_na[:, :, 0], g_na_k)
    h_na = neighbourhood_attention(q_na, k_na, v_na[:, :, 0], rpb)

    h_mem = memory_cross_attention(q_mem, rmsnorm(mem, g_mem), w_mem_kv, g_mem_q, g_mem_k)

    gates = jax.nn.sigmoid(gate_pre).reshape(bsz, s, N_BRANCH, D_MODEL)
    y = (gates[:, :, 0] * (h_m @ w_proj_mlstm)
         + gates[:, :, 1] * (h_na @ w_proj_na)
         + gates[:, :, 2] * (h_mem @ w_proj_mem))
    x = x + y @ w_out

    u = rmsnorm(x, g_ffn) @ w_up
    return x + jnp.square(jax.nn.relu(u)) @ w_down


def setup_inputs(seed: int = 0) -> dict:
    key = jax.random.key(seed)
    ks = jax.random.split(key, 24)
    nrm = lambda k, shape, scale: jax.random.normal(k, shape, jnp.float32) * scale
    gain = lambda k, shape: 1.0 + 0.1 * jax.random.normal(k, shape, jnp.float32)
    L = DEPTH
    f_bias = (jnp.linspace(3.0, 6.0, MLSTM_HEADS, dtype=jnp.float32)[None, None, :]
              + nrm(ks[8], (L, 2, MLSTM_HEADS), 0.1))
    return {
        'x': nrm(ks[0], (BATCH, SEQ, D_MODEL), 1.0),
        'mem': nrm(ks[1], (BATCH, MEM_TOKENS, D_MODEL), 1.0),
        'g_mix': gain(ks[2], (L, D_MODEL)),
        'w_in': nrm(ks[3], (L, D_MODEL, D_IN), D_MODEL ** -0.5),
        'conv_w': nrm(ks[4], (L, MLSTM_CONV, 2 * MLSTM_QK_W), MLSTM_CONV ** -0.5),
        'conv_b': nrm(ks[5], (L, 2 * MLSTM_QK_W), 0.02),
        'b_igate': nrm(ks[6], (L, 2, MLSTM_HEADS), 0.1),
        'b_fgate': f_bias,
        'g_mlstm': gain(ks[7], (L, MLSTM_V_W)),
        'w_proj_mlstm': nrm(ks[9], (L, MLSTM_V_W, D_MODEL), MLSTM_V_W ** -0.5),
        'g_na_q': gain(ks[10], (L, NA_HEAD_DIM)),
        'g_na_k': gain(ks[11], (L, NA_HEAD_DIM)),
        'rpb': nrm(ks[12], (L, NA_HEADS, 2 * NA_WIN_ROWS - 1, 2 * NA_WIN_COLS - 1), 0.1),
        'w_proj_na': nrm(ks[13], (L, NA_W, D_MODEL), NA_W ** -0.5),
        'g_mem': gain(ks[14], (L, D_MODEL)),
        'w_mem_kv': nrm(ks[15], (L, D_MODEL, 2 * MEM_W), D_MODEL ** -0.5),
        'g_mem_q': gain(ks[16], (L, MEM_HEAD_DIM)),
        'g_mem_k': gain(ks[17], (L, MEM_HEAD_DIM)),
        'w_proj_mem': nrm(ks[18], (L, MEM_W, D_MODEL), MEM_W ** -0.5),
        'w_out': nrm(ks[19], (L, D_MODEL, D_MODEL), D_MODEL ** -0.5),
        'g_ffn': gain(ks[20], (L, D_MODEL)),
        'w_up': nrm(ks[21], (L, D_MODEL, D_FF), D_MODEL ** -0.5),
        'w_down': nrm(ks[22], (L, D_FF, D_MODEL), D_FF ** -0.5),
    }


def reference(x, mem, g_mix, w_in, conv_w, conv_b, b_igate, b_fgate, g_mlstm, w_proj_mlstm,
              g_na_q, g_na_k, rpb, w_proj_na, g_mem, w_mem_kv, g_mem_q, g_mem_k,
              w_proj_mem, w_out, g_ffn, w_up, w_down):
    for l in range(DEPTH):
        x = hybrid_layer(x, mem, g_mix[l], w_in[l], conv_w[l], conv_b[l], b_igate[l], b_fgate[l],
                         g_mlstm[l], w_proj_mlstm[l], g_na_q[l], g_na_k[l], rpb[l], w_proj_na[l],
                         g_mem[l], w_mem_kv[l], g_mem_q[l], g_mem_k[l], w_proj_mem[l], w_out[l],
                         g_ffn[l], w_up[l], w_down[l])
    return x
```

```python
import math
from contextlib import ExitStack
import numpy as np
import concourse.bass as bass
import concourse.mybir as mybir
from concourse.bass_utils import run_bass_kernel_spmd

F32 = mybir.dt.float32
BF16 = mybir.dt.bfloat16
AF = mybir.ActivationFunctionType
ALU = mybir.AluOpType
AX = mybir.AxisListType

ENGS = ["pe", "act", "dve", "pool", "sp"]
SAME_DIST = 3
SEM_LIMIT = 30000


class V:
    __slots__ = ("t", "ap")

    def __init__(self, t, ap):
        self.t = t
        self.ap = ap

    def __getitem__(self, k):
        return V(self.t, self.ap[k])

    def rearrange(self, pattern_, **kw):
        return V(self.t, self.ap.rearrange(pattern_, **kw))

    def bc(self, shape):
        return V(self.t, self.ap.to_broadcast(list(shape)))

    def bitcast(self, dt):
        return V(self.t, self.ap.bitcast(dt))


class TT:
    def __init__(self, name, ap, lo=0, hi=0, region=None):
        self.name = name
        self.ap = ap
        self.w = None
        self.r = {}
        self.dsem = None
        self.dcount = 0
        self.overlaps = []
        self.lo, self.hi = lo, hi
        if region is not None:
            for o in region:
                if o.lo < hi and lo < o.hi:
                    o.overlaps.append(self)
                    self.overlaps.append(o)
            region.append(self)

    def __getitem__(self, k):
        return V(self, self.ap[k])

    def v(self):
        return V(self, self.ap)


class Op:
    __slots__ = ("eng", "idx", "fn", "waits", "needed", "is_dma", "semkey", "semval", "cnt")

    def __init__(self, eng, idx, fn, is_dma):
        self.eng = eng
        self.idx = idx
        self.fn = fn
        self.waits = []
        self.needed = False
        self.is_dma = is_dma
        self.semkey = None
        self.semval = None
        self.cnt = None


class Prog:
    def __init__(self, nc):
        self.nc = nc
        self.ops = {e: [] for e in ENGS}
        self.waited = {e: {} for e in ENGS}
        self.next_dsem = 0
        self.stack = ExitStack()

    def sbuf(self, name, shape, dt):
        h = self.stack.enter_context(self.nc.sbuf_tensor("sb_" + name, list(shape), dt))
        return TT(name, h[:])

    def psum(self, name, shape, dt):
        h = self.stack.enter_context(self.nc.psum_tensor("ps_" + name, list(shape), dt))
        return TT(name, h[:])

    def dram(self, name, ap):
        return TT("@" + name, ap)

    def _add(self, eng, fn, reads, writes, dma_tile=None):
        ops = self.ops[eng]
        op = Op(eng, len(ops), fn, dma_tile is not None)
        rt = [x.t for x in reads]
        wt = [x.t for x in writes]
        cand = []
        for t in rt:
            if t.w is not None:
                cand.append(t.w)
            for o in t.overlaps:
                if o.w is not None:
                    cand.append(o.w)
        for t in wt:
            if t.w is not None:
                cand.append(t.w)
            cand.extend(t.r.values())
            for o in t.overlaps:
                if o.w is not None:
                    cand.append(o.w)
                cand.extend(o.r.values())
        wd = self.waited[eng]
        deps = {}
        for d in cand:
            if d.is_dma:
                key, val = d.semkey, d.semval
            else:
                key, val = d.eng, d.idx
                if d.eng == eng and (eng == "pe" or op.idx - d.idx > SAME_DIST):
                    continue
            if wd.get(key, -1) >= val:
                continue
            if key not in deps or deps[key][1] < val:
                deps[key] = (d, val)
        for key, (d, val) in deps.items():
            wd[key] = val
            d.needed = True
            op.waits.append(d)
        if dma_tile is not None:
            if dma_tile.dsem is None:
                dma_tile.dsem = {}
                dma_tile.dcount = {}
            if eng not in dma_tile.dsem:
                dma_tile.dsem[eng] = self.next_dsem
                dma_tile.dcount[eng] = 0
                self.next_dsem += 1
            dma_tile.dcount[eng] += 1
            op.semkey = ("d", dma_tile.dsem[eng])
            op.semval = 16 * dma_tile.dcount[eng]
            rkey = op.semkey
        else:
            rkey = eng
        for t in rt:
            t.r[rkey] = op
        for t in wt:
            t.w = op
            t.r = {}
        ops.append(op)
        return op

    def mm(self, out, lhsT, rhs, start=True, stop=True):
        self._add("pe", lambda e: e.matmul(out.ap, lhsT.ap, rhs.ap, start=start, stop=stop),
                  [lhsT, rhs] + ([] if start else [out]), [out])

    def transpose(self, out, in_, ident):
        self._add("pe", lambda e: e.transpose(out.ap, in_.ap, ident.ap), [in_, ident], [out])

    def act(self, out, in_, func, bias=None, scale=None, accum_out=None):
        reads = [in_]
        kw = {}
        if bias is not None:
            if isinstance(bias, V):
                reads.append(bias)
                kw["bias"] = bias.ap
            else:
                kw["bias"] = bias
        if scale is not None:
            if isinstance(scale, V):
                reads.append(scale)
                kw["scale"] = scale.ap
            else:
                kw["scale"] = scale
        writes = [out]
        if accum_out is not None:
            writes.append(accum_out)
            kw["accum_out"] = accum_out.ap
        self._add("act", lambda e: e.activation(out.ap, in_.ap, func, **kw), reads, writes)

    def tt(self, out, in0, in1, op, eng="dve"):
        self._add(eng, lambda e: e.tensor_tensor(out.ap, in0.ap, in1.ap, op), [in0, in1], [out])

    def ts(self, out, in0, s1, op0, s2=None, op1=None, eng="dve"):
        reads = [in0]
        a1 = s1
        if isinstance(s1, V):
            reads.append(s1)
            a1 = s1.ap
        a2 = s2
        if isinstance(s2, V):
            reads.append(s2)
            a2 = s2.ap
        if op1 is None:
            self._add(eng, lambda e: e.tensor_scalar(out.ap, in0.ap, a1, None, op0), reads, [out])
        else:
            self._add(eng, lambda e: e.tensor_scalar(out.ap, in0.ap, a1, a2, op0, op1), reads, [out])

    def stt(self, out, in0, scalar, in1, op0, op1, eng="dve"):
        reads = [in0, in1]
        a = scalar
        if isinstance(scalar, V):
            reads.append(scalar)
            a = scalar.ap
        self._add(eng, lambda e: e.scalar_tensor_tensor(out.ap, in0.ap, a, in1.ap, op0, op1), reads, [out])

    def copy(self, out, in_, eng="dve"):
        if eng == "act":
            self._add(eng, lambda e: e.copy(out.ap, in_.ap), [in_], [out])
        else:
            self._add(eng, lambda e: e.tensor_copy(out.ap, in_.ap), [in_], [out])

    def memset(self, out, val, eng="dve"):
        self._add(eng, lambda e: e.memset(out.ap, val), [], [out])

    def reduce(self, out, in_, op, axis=AX.X, eng="dve"):
        self._add(eng, lambda e: e.tensor_reduce(out.ap, in_.ap, axis, op), [in_], [out])

    def recip(self, out, in_):
        self._add("dve", lambda e: e.reciprocal(out.ap, in_.ap), [in_], [out])

    def dma(self, out, in_, q="sp", sem_tile=None):
        if sem_tile is None:
            sem_tile = out.t if out.t.name[0] != "@" else in_.t
        op = self._add(q, None, [in_], [out], dma_tile=sem_tile)
        op.fn = (out.ap, in_.ap)
        return op

    def emit(self, final_wait_ops=()):
        nc = self.nc
        st = self.stack
        esems = {}
        for e in ENGS:
            c = 0
            for op in self.ops[e]:
                if not op.is_dma and op.needed:
                    c += 1
                    op.cnt = c
            esems[e] = [st.enter_context(nc.semaphore("s_%s_%d" % (e, i))) for i in range(c // SEM_LIMIT + 1)]
        dsems = [st.enter_context(nc.semaphore("d_%d" % i)) for i in range(self.next_dsem)]

        def semof(d):
            if d.is_dma:
                return dsems[d.semkey[1]], d.semval
            k = (d.cnt - 1) // SEM_LIMIT
            return esems[d.eng][k], d.cnt - k * SEM_LIMIT

        block = st.enter_context(nc.Block())
        engattr = {"pe": "tensor", "act": "scalar", "dve": "vector", "pool": "gpsimd", "sp": "sync"}
        prog = self

        def make(ename):
            def body(eng):
                for op in prog.ops[ename]:
                    for d in op.waits:
                        s, v = semof(d)
                        eng.wait_ge(s, v)
                    if op.is_dma:
                        o, i = op.fn
                        eng.dma_start(out=o, in_=i).then_inc(dsems[op.semkey[1]], 16)
                    else:
                        ins = op.fn(eng)
                        if op.needed:
                            s, v = semof(op)
                            ins.then_inc(s, 1)
                if ename == "sp":
                    for d in final_wait_ops:
                        s, v = semof(d)
                        eng.wait_ge(s, v)
            return body

        for e in ENGS:
            getattr(block, engattr[e])(make(e))

    def close(self):
        self.stack.close()


D = 1024
S = 8192
NB = 4
HALF = 4096
TILE = 512
NT = HALF // TILE
NCH = TILE // 64
DFF = 4096
EPS = 1e-6
NEG = -80.0
LN8 = math.log(0.125)

O_QK, O_V, O_O, O_I, O_F, O_NA, O_MQ, O_G = 0, 512, 1024, 1536, 1544, 1552, 2320, 2576
XO_W = 256 + HALF + 256
XP_W = HALF + 4

SLOT = 4352
NSLOT = 4


def _chunks():
    ch = [("memq", 8 * 256), ("na_q", 8 * 256), ("na_kv", 8 * 512), ("qk_m", 8 * 512), ("v_m", 8 * 512),
          ("o_m", 8 * 512), ("g16", 8 * 16)]
    ch += [("mrg%d" % j, SLOT) for j in range(8)]
    ch += [("out%d" % i, 8 * 512) for i in range(2)]
    ch += [("up%d" % i, 8 * 512) for i in range(8)]
    ch += [("dn%d" % j, 32 * 128) for j in range(8)]
    off = {}
    o = 0
    for n, l in ch:
        off[n] = (o, l)
        o += l
    return ch, off, o


CHUNKS, CHOFF, WTOT = _chunks()

CST_FIELDS = [("g_mix8", 8), ("g_ffn8", 8), ("g_mem8", 8), ("convw_m", 40), ("convb_m", 8),
              ("convw_po", 10), ("convw_pp", 10), ("convb_p", 2), ("gbias", 16), ("g_mlstm", 512),
              ("g_naqk", 512), ("g_memq", 1), ("g_memk", 1), ("ident", 128), ("ones", 128),
              ("triA", 64), ("triB", 64), ("mask4A", 256), ("mask4B", 256), ("sut128", 128)]
CSTOFF = {}
_o = 0
for _n, _l in CST_FIELDS:
    CSTOFF[_n] = (_o, _l)
    _o += _l
NCST = _o
NT2 = 14 * 8 * 64


def _kmaj(W):
    K, N = W.shape
    kc = K // 128
    return np.ascontiguousarray(W.reshape(kc, 128, N).transpose(1, 0, 2)).reshape(128, kc * N)


def _host_prep(inp):
    f32 = np.float32
    w_in = np.asarray(inp["w_in"][0], f32)
    maps = []
    wall0 = np.zeros((128, WTOT), f32)

    def put(name, arr):
        o, l = CHOFF[name]
        assert arr.shape == (128, l), (name, arr.shape, l)
        wall0[:, o:o + l] = arr

    put("memq", _kmaj(w_in[:, O_MQ:O_MQ + 256]))
    put("na_q", _kmaj(w_in[:, O_NA:O_NA + 256]))
    put("na_kv", _kmaj(w_in[:, O_NA + 256:O_NA + 768]))
    put("qk_m", _kmaj(w_in[:, O_QK:O_QK + 512]))
    put("v_m", _kmaj(w_in[:, O_V:O_V + 512]))
    put("o_m", _kmaj(w_in[:, O_O:O_O + 512]))
    wpm = np.asarray(inp["w_proj_mlstm"][0], f32)
    wpn = np.asarray(inp["w_proj_na"][0], f32)
    wpe = np.asarray(inp["w_proj_mem"][0], f32)
    for j in range(8):
        cols = np.concatenate([O_G + b * 1024 + j * 128 + np.arange(128) for b in range(3)])
        a = _kmaj(w_in[:, cols])
        b_ = _kmaj(wpm[:, j * 128:(j + 1) * 128])
        c_ = _kmaj(wpn[:, j * 128:(j + 1) * 128])
        d_ = np.zeros((128, 4 * 128), f32)
        d_[0:64] = wpe[:, j * 128:(j + 1) * 128].reshape(4, 64, 128).transpose(1, 0, 2).reshape(64, 512)
        put("mrg%d" % j, np.concatenate([a, b_, c_, d_], axis=1))
    w_out = np.asarray(inp["w_out"][0], f32)
    for i in range(2):
        put("out%d" % i, _kmaj(w_out[:, i * 512:(i + 1) * 512]))
    w_up = np.asarray(inp["w_up"][0], f32)
    for i in range(8):
        put("up%d" % i, _kmaj(w_up[:, i * 512:(i + 1) * 512]))
    w_dn = np.asarray(inp["w_down"][0], f32)
    for j in range(8):
        put("dn%d" % j, _kmaj(w_dn[:, j * 128:(j + 1) * 128]))

    conv_w = np.asarray(inp["conv_w"][0], f32)
    conv_b = np.asarray(inp["conv_b"][0], f32)
    b_i = np.asarray(inp["b_igate"][0], f32)
    b_f = np.asarray(inp["b_fgate"][0], f32)
    rpb = np.asarray(inp["rpb"][0], f32)
    ar = np.arange(64)
    tri = (ar[:, None] <= ar[None, :]).astype(f32)

    walls, csts, t2s = [], [], []
    for half in range(2):
        dA, dB = half, 1 - half
        wall = wall0.copy()
        gcols = np.concatenate([O_I + dA * 4 + np.arange(4), O_F + dA * 4 + np.arange(4),
                                O_I + dB * 4 + np.arange(4), O_F + dB * 4 + np.arange(4)])
        o, l = CHOFF["g16"]
        wall[:, o:o + l] = _kmaj(w_in[:, gcols])
        walls.append(wall)

        cst = np.zeros((128, NCST), f32)

        def cput(name, arr, rows=128):
            o, l = CSTOFF[name]
            cst[0:rows, o:o + l] = arr

        cput("g_mix8", np.asarray(inp["g_mix"][0], f32).reshape(8, 128).T)
        cput("g_ffn8", np.asarray(inp["g_ffn"][0], f32).reshape(8, 128).T)
        cput("g_mem8", np.asarray(inp["g_mem"][0], f32).reshape(8, 128).T)
        cw_own = conv_w if half == 0 else conv_w[::-1]
        cw_par = cw_own[::-1]
        cput("convw_m", cw_own.T.reshape(8, 64, 5).transpose(1, 0, 2).reshape(64, 40), rows=64)
        cput("convb_m", conv_b.reshape(8, 64).T, rows=64)
        cput("convw_po", cw_own.T[256:512].reshape(2, 128, 5).transpose(1, 0, 2).reshape(128, 10))
        cput("convw_pp", cw_par.T[256:512].reshape(2, 128, 5).transpose(1, 0, 2).reshape(128, 10))
        cput("convb_p", conv_b[256:512].reshape(2, 128).T)
        gb = np.concatenate([b_i[dA], b_f[dA], b_i[dB], b_f[dB]])
        cput("gbias", np.tile(gb[None, :], (128, 1)))
        cput("g_mlstm", np.tile(np.asarray(inp["g_mlstm"][0], f32)[None, :], (128, 1)))
        gq = np.asarray(inp["g_na_q"][0], f32)
        gk = np.asarray(inp["g_na_k"][0], f32)
        cput("g_naqk", np.tile(np.concatenate([np.tile(gq, 8), np.tile(gk, 8)])[None, :], (128, 1)))
        cput("g_memq", np.asarray(inp["g_mem_q"][0], f32)[:, None], rows=64)
        cput("g_memk", np.asarray(inp["g_mem_k"][0], f32)[:, None], rows=64)
        cput("ident", np.eye(128, dtype=f32))
        cput("ones", np.ones((128, 128), f32))
        cput("triA", tri, rows=64)
        cput("triB", tri.T, rows=64)
        cput("mask4A", np.tile(tri, (1, 4)), rows=64)
        cput("mask4B", np.tile(tri.T, (1, 4)), rows=64)
        a128 = np.arange(128)
        cput("sut128", (a128[:, None] > a128[None, :]).astype(f32))
        csts.append(cst)

        t2 = np.full((128, 14, 8, 64), NEG, f32)
        kj = np.arange(64)[:, None]
        cq = np.arange(64)[None, :]
        if half == 0:
            cg, kg = cq, kj
        else:
            cg, kg = 63 - cq, 63 - kj
        cs = np.clip(cg - 8, 0, 48)
        colok = (kg >= cs) & (kg <= cs + 15)
        dcg = kg - cg
        for slot in range(14):
            if slot < 10:
                d = slot - 5
                edge = False
            else:
                d = slot - 10 + 3
                edge = True
            for ko in range(2):
                dr = d + ko
                drg = dr if half == 0 else -dr
                if edge:
                    ok = abs(drg) <= 7
                else:
                    ok = (-4 <= drg <= 3)
                if not ok:
                    continue
                idx = np.clip(dcg + 15, 0, 30)
                for h in range(8):
                    vals = rpb[h, drg + 7][idx]
                    t2[ko * 64:(ko + 1) * 64, slot, h, :] = np.where(colok, vals, NEG)
        t2s.append(t2.reshape(128, NT2))

    x = np.asarray(inp["x"], f32)
    mem = np.asarray(inp["mem"], f32)
    for b in range(NB):
        memT = np.ascontiguousarray(mem[b].T)
        for half in range(2):
            seq = x[b] if half == 0 else x[b][::-1]
            xo = np.zeros((D, XO_W), f32)
            xo[:, 256:] = seq[0:HALF + 256].T
            xp = np.zeros((D, XP_W), f32)
            xp[:, 2:] = seq[::-1][0:HALF + 2].T
            maps.append({"xo": xo, "xp": xp, "memT": memT, "wall": walls[half], "cst": csts[half],
                         "t2": t2s[half], "w_mem_kv": np.asarray(inp["w_mem_kv"][0], f32)})
    return maps


def _assemble(results):
    out = np.zeros((NB, S, D), np.float32)
    for b in range(NB):
        for half in range(2):
            o = results[2 * b + half]["outT"]
            loc = o.T
            if half == 0:
                out[b, 0:HALF] = loc
            else:
                out[b, HALF:] = loc[::-1]
    return out


DEBUG = False
RBYTES = 63488


def build_program(n_main_tiles=NT, do_pre=True, debug=False):
    nc = bass.Bass("TRN2", target_bir_lowering=False)
    P = Prog(nc)

    def din(name, shape):
        return P.dram(name, nc.dram_tensor(name, shape, F32, kind="ExternalInput").ap())

    xo_d = din("xo", [D, XO_W])
    xp_d = din("xp", [D, XP_W])
    memT_d = din("memT", [D, 256])
    wall_d = din("wall", [128, WTOT])
    cst_d = din("cst", [128, NCST])
    t2_d = din("t2", [128, NT2])
    wmkv_d = din("w_mem_kv", [D, 512])
    outT_d = P.dram("outT", nc.dram_tensor("outT", [D, HALF], F32, kind="ExternalOutput").ap())
    wbf_ap = nc.dram_tensor("wbf", [128, WTOT], BF16).ap()
    NGRP = 4
    wbf_grp = [P.dram("wbf%d" % i, wbf_ap) for i in range(NGRP)]
    cstart_ap = nc.dram_tensor("cstart", [NT, 64, 516], F32).ap()
    cstart_d = [P.dram("cstart%d" % i, cstart_ap[i]) for i in range(NT)]
    dbg = {}
    if debug:
        for nm, rows in (("d_hmem", 256), ("d_hna", 256), ("d_hm", 512), ("d_y", 1024), ("d_x1", 1024)):
            dbg[nm] = P.dram(nm, nc.dram_tensor(nm, [rows, HALF], F32, kind="ExternalOutput").ap())
        dbg["d_c"] = P.dram("d_c", nc.dram_tensor("d_c", [64, 516], F32, kind="ExternalOutput").ap())
        dbg["d_misc"] = P.dram("d_misc", nc.dram_tensor("d_misc", [128, 4096], F32, kind="ExternalOutput").ap())

    xo_v = V(xo_d, xo_d.ap.rearrange("(kc p) t -> p kc t", p=128))
    xp_v = V(xp_d, xp_d.ap.rearrange("(kc p) t -> p kc t", p=128))

    cst = P.sbuf("cst", [128, NCST], F32)

    def cf(name, rows=128):
        o, l = CSTOFF[name]
        return cst[0:rows, o:o + l]

    identb = P.sbuf("identb", [128, 128], BF16)
    onesb = P.sbuf("onesb", [128, 128], BF16)
    t2 = P.sbuf("t2", [128, NT2], BF16)
    t2v = t2.v().rearrange("p (s h c) -> p s h c", s=14, h=8, c=64)
    ring = [P.sbuf("ring%d" % i, [128, SLOT], BF16) for i in range(NSLOT)]
    xs = [P.sbuf("xs%d" % i, [128, 8, 258], F32) for i in range(2)]
    sq = P.sbuf("sq", [128, 8, 258], BF16)
    rstd = P.sbuf("rstd", [128, 512], F32)
    hT = P.sbuf("hT", [128, 8, 516], BF16)
    hh = P.sbuf("hh", [128, 8, 256], BF16)
    kmT = P.sbuf("kmT", [64, 4, 256], BF16)
    vm_aug = P.sbuf("vm_aug", [128, 2, 4, 65], BF16)
    Cst = [P.sbuf("C%d" % i, [64, 516], F32) for i in range(2)]
    Cb = [P.sbuf("Cb%d" % i, [64, 516], BF16) for i in range(2)]
    hmT = P.sbuf("hmT", [128, 4, 512], BF16)
    hnaT = P.sbuf("hnaT", [128, 2, 512], BF16)
    hmemT = P.sbuf("hmemT", [64, 4, 512], BF16)
    yTs = [P.sbuf("yT%d" % i_, [128, 512], BF16) for i_ in range(8)]
    yaccs = [P.sbuf("yacc%d" % i_, [128, 512], F32) for i_ in range(2)]
    tmpys = [P.sbuf("tmpy%d" % i_, [128, 512], F32) for i_ in range(2)]
    gss = [P.sbuf("gs%d" % i_, [128, 512], F32) for i_ in range(2)]
    sqxs = [P.sbuf("sqx%d" % i_, [128, 512], BF16) for i_ in range(2)]
    rls = [P.sbuf("rl%d" % i_, [128, 512], F32) for i_ in range(2)]
    ost = [P.sbuf("ost%d" % i, [128, 512], F32) for i in range(2)]
    dstage = P.sbuf("dstage", [128, 512], F32) if debug else None
    R = P.sbuf("R", [128, RBYTES // 2], BF16)
    region = []

    def carve(name, off, parts, shape, dt):
        n = 1
        for s_ in shape:
            n *= s_
        nb = n * (4 if dt == F32 else 2)
        assert off % 4 == 0 and off + nb <= RBYTES, (name, off, nb)
        ap = R.ap[0:parts, off // 2:(off + nb) // 2]
        if dt == F32:
            ap = ap.bitcast(F32)
        if len(shape) == 2:
            ap = ap.rearrange("p (a b) -> p a b", a=shape[0], b=shape[1])
        elif len(shape) == 3:
            ap = ap.rearrange("p (a b c) -> p a b c", a=shape[0], b=shape[1], c=shape[2])
        return TT(name, ap, off, off + nb, region)

    class Lay:
        def __init__(self):
            self.o = 0

        def __call__(self, name, parts, shape, dt):
            t = carve(name, self.o, parts, shape, dt)
            self.o = t.hi + (-t.hi) % 4
            return t

    L = Lay()
    qnaT = L("qnaT", 32, [8, 512], BF16)
    knaT = L("knaT", 32, [8, 1024], BF16)
    vna = L("vna", 128, [8, 8, 33], BF16)
    sqns = [L("sqn%d" % i_, 128, [16, 32], F32) for i_ in range(2)]
    tmpns = [L("tmpn%d" % i_, 128, [16, 32], F32) for i_ in range(2)]
    qkns = [L("qkn%d" % i_, 128, [16, 32], BF16) for i_ in range(2)]
    ssns = [L("ssn%d" % i_, 128, [16], F32) for i_ in range(2)]
    hna_tok = L("hna_tok", 128, [8, 32], BF16)
    rdn = L("rdn", 128, [8], F32)
    mem_off = L.o
    PTn = L("PTn", 128, [5, 8, 128], BF16)
    sbns = [L("sbn%d" % i_, 128, [8, 128], F32) for i_ in range(2)]
    L = Lay()
    L.o = mem_off
    memq = L("memq", 64, [4, 512], BF16)
    PTm = L("PTm", 128, [4, 2, 512], BF16)
    hmem_toks = [L("hmem_tok%d" % i_, 128, [4, 64], BF16) for i_ in range(2)]
    sqfs = [L("sqf%d" % i_, 64, [512], F32) for i_ in range(2)]
    rss = [L("rs%d" % i_, 64, [512], F32) for i_ in range(2)]
    rdms = [L("rdm%d" % i_, 128, [4], F32) for i_ in range(2)]
    L = Lay()
    qkT = L("qkT", 64, [8, 512], BF16)
    ktok = L("ktok", 64, [8, 256], BF16)
    vaug = L("vaug", 64, [8, 4, 129], BF16)
    osig = L("osig", 64, [8, 512], BF16)
    hacc = L("hacc", 64, [8, 512], F32)
    zqs = [L("zq%d" % i_, 64, [516], F32) for i_ in range(2)]
    accqs = [L("accq%d" % i_, 64, [512], F32) for i_ in range(2)]
    zq, accq = zqs[0], accqs[0]
    Gt = L("Gt", 64, [8, 16], F32)
    Et = L("Et", 64, [8, 8], F32)
    Lt = L("Lt", 64, [8, 8], F32)
    tmpu = L("tmpu", 64, [8, 8], F32)
    Ut = L("Ut", 64, [8, 8], F32)
    Wt = L("Wt", 64, [8, 8], F32)
    DECt = L("DECt", 64, [8, 8], F32)
    IWt = L("IWt", 64, [8, 8], F32)
    PMs = [L("PM%d" % i_, 64, [256], BF16) for i_ in range(2)]
    vus = [L("vu%d" % i_, 64, [4, 129], BF16) for i_ in range(2)]
    t1s = [L("t1%d" % i_, 64, [4], F32) for i_ in range(2)]
    rrs = [L("rr%d" % i_, 64, [4], F32) for i_ in range(2)]
    tmphs = [L("tmph%d" % i_, 64, [2, 128], F32) for i_ in range(2)]
    ssh = L("ssh", 64, [4], F32)
    hm_tok = L("hm_tok", 64, [512], BF16)
    L = Lay()
    PB = []
    for i_ in range(2):
        PB.append(dict(
            hT=L("p_hT%d" % i_, 128, [8, 516], BF16),
            zk=L("zk%d" % i_, 128, [2, 516], F32),
            kTp=L("kTp%d" % i_, 128, [2, 512], BF16),
            accp=L("accp%d" % i_, 128, [512], F32),
            ktok128=L("ktok128%d" % i_, 128, [4, 256], BF16),
            vaug128=L("vaug128%d" % i_, 128, [4, 4, 129], BF16),
            vws=L("vws%d" % i_, 128, [4, 4, 129], BF16),
            Gp=L("Gp%d" % i_, 128, [4, 8], F32),
            Ep=L("Ep%d" % i_, 128, [4, 4], F32),
            Lp=L("Lp%d" % i_, 128, [4, 4], F32),
            WSp=L("WSp%d" % i_, 128, [4, 4], F32),
            DECp=L("DECp%d" % i_, 64, [4], F32)))
    L = Lay()
    aTs = [L("aT%d" % i_, 128, [4, 512], BF16) for i_ in range(8)]
    hnTs = [L("hnT%d" % i_, 128, [512], BF16) for i_ in range(8)]
    x1Ts = [L("x1T%d" % i_, 128, [512], F32) for i_ in range(8)]

    pb = [P.psum("pb%d" % i, [128, 512], F32) for i in range(8)]
    rot = [0]
    rotn = [2]

    def nb():
        rot[0] = (rot[0] + 1) % rotn[0]
        return pb[rot[0]]

    def pbf(i):
        return V(pb[i], pb[i].ap.bitcast(BF16))

    P.dma(cst.v(), cst_d.v(), q="sp")
    P.dma(t2.v(), t2_d.v(), q="pool")
    P.copy(identb.v(), cf("ident"))
    P.copy(onesb.v(), cf("ones"))
    ones64f = cst[0:64, CSTOFF["ones"][0]:CSTOFF["ones"][0] + 64]
    triAf = cf("triA", 64)
    triBf = cf("triB", 64)
    g_mix8 = cf("g_mix8")
    g_ffn8 = cf("g_ffn8")
    g_mem8 = cf("g_mem8")

    seg_ctr = [0]

    xq = ["sp"]

    def rms_load(src, n):
        xsb = xs[seg_ctr[0] % 2]
        seg_ctr[0] += 1
        P.dma(xsb[:, :, 0:n], src, q=xq[0])
        return xsb

    def rmsnorm_seg(src, g8, dst, n, xsb=None):
        if xsb is None:
            xsb = rms_load(src, n)
        P.act(sq[:, :, 0:n], xsb[:, :, 0:n], AF.Square)
        for kc in range(8):
            P.mm(pb[7][:, 0:n], onesb.v(), sq[:, kc, 0:n], start=(kc == 0), stop=(kc == 7))
        P.act(rstd[:, 0:n], pb[7][:, 0:n], AF.Sqrt, bias=EPS, scale=1.0 / D)
        P.recip(rstd[:, 0:n], rstd[:, 0:n])
        for kc in range(8):
            P.stt(dst[:, kc, :], xsb[:, kc, 0:n], g8[:, kc:kc + 1], rstd[:, 0:n], ALU.mult, ALU.mult)

    def bc_last(v, shape):
        return V(v.t, v.ap.rearrange("p (a o) -> p a o", o=1).to_broadcast(list(shape)))

    memT_v = V(memT_d, memT_d.ap.rearrange("(kc p) t -> p kc t", p=128))
    rmsnorm_seg(memT_v, g_mem8, hh[:, :, 0:256], 256)
    wkv = ring[0][:, 0:4096].rearrange("p (kc n) -> p kc n", kc=8)
    P.dma(wkv, V(wmkv_d, wmkv_d.ap.rearrange("(kc p) n -> p kc n", p=128)), q="pool")
    for h in range(4):
        ps = nb()
        for kc in range(8):
            P.mm(ps[0:64, 0:256], wkv[:, kc, h * 64:(h + 1) * 64], hh[:, kc, :], start=(kc == 0), stop=(kc == 7))
        sqf, rs = sqfs[h % 2], rss[h % 2]
        P.act(sqf[:, 0:256], ps[0:64, 0:256], AF.Square)
        ps2 = nb()
        P.mm(ps2[0:64, 0:256], ones64f, sqf[:, 0:256])
        P.act(rs[:, 0:256], ps2[0:64, 0:256], AF.Sqrt, bias=EPS, scale=1.0 / 64)
        P.recip(rs[:, 0:256], rs[:, 0:256])
        P.stt(kmT[:, h, :], ps[0:64, 0:256], cf("g_memk", 64), rs[:, 0:256], ALU.mult, ALU.mult)
    P.memset(vm_aug.v(), 1.0)
    for mc in range(2):
        ps = nb()
        for kc in range(8):
            P.mm(ps[:, 0:256], hh[:, kc, mc * 128:(mc + 1) * 128], wkv[:, kc, 256:512], start=(kc == 0), stop=(kc == 7))
        P.copy(vm_aug[:, mc, :, 0:64], ps[:, 0:256].rearrange("p (h d) -> p h d", h=4))

    def conv_chunks():
        for i, (nm, ln) in enumerate(CHUNKS):
            o, l = CHOFF[nm]
            g = wbf_grp[min(NGRP - 1, i * NGRP // len(CHUNKS))]
            P.dma(V(g, wbf_ap[:, o:o + l]), wall_d[:, o:o + l], q="pool", sem_tile=g)

    def chunk_group(nm):
        i = [c[0] for c in CHUNKS].index(nm)
        return wbf_grp[min(NGRP - 1, i * NGRP // len(CHUNKS))]

    wseq = []
    for _t in range(n_main_tiles):
        wseq += [c[0] for c in CHUNKS]
    wstate = {"next_load": 0, "next_use": 0, "released": 0}

    def _pump():
        while wstate["next_load"] < len(wseq) and wstate["next_load"] - NSLOT < wstate["released"]:
            i = wstate["next_load"]
            nm = wseq[i]
            o, l = CHOFF[nm]
            g = chunk_group(nm)
            P.dma(ring[i % NSLOT][:, 0:l], V(g, wbf_ap[:, o:o + l]), q="sp")
            wstate["next_load"] = i + 1

    def next_w(nm, hold=False):
        i = wstate["next_use"]
        assert wseq[i] == nm, (wseq[i], nm)
        if not hold:
            wstate["released"] = i
        _pump()
        assert wstate["next_load"] > i
        wstate["next_use"] = i + 1
        return ring[i % NSLOT]

    def gate_math(ndir):
        G4 = Gt.v().rearrange("p c (d e) -> p c d e", d=2, e=8)
        E4 = Et.v().rearrange("p c (d e) -> p c d e", d=2, e=4)
        L4 = Lt.v().rearrange("p c (d e) -> p c d e", d=2, e=4)
        U4 = tmpu.v().rearrange("p c (d e) -> p c d e", d=2, e=4)
        nd = ndir
        P.act(E4[:, :, 0:nd, :], G4[:, :, 0:nd, 4:8], AF.Exp, scale=-1.0)
        P.act(L4[:, :, 0:nd, :], E4[:, :, 0:nd, :], AF.Ln, bias=1.0)
        psb = pb[5]
        for c in range(NCH):
            P.mm(psb[0:64, c * 24:c * 24 + 4], triAf, Lt[:, c, 0:4])
            if nd == 2:
                P.mm(psb[0:64, c * 24 + 4:c * 24 + 8], triBf, Lt[:, c, 4:8])
            P.mm(psb[0:64, c * 24 + 8:c * 24 + 8 + 4 * nd], ones64f, Lt[:, c, 0:4 * nd])
        psv = psb[0:64, 0:NCH * 24].rearrange("p (c e) -> p c e", c=NCH)
        bp4 = psv[:, :, 0:8].rearrange("p c (d e) -> p c d e", d=2)
        P.tt(U4[:, :, 0:nd, :], G4[:, :, 0:nd, 0:4], bp4[:, :, 0:nd, :], ALU.add)
        P.act(Ut[:, :, 0:4 * nd], tmpu[:, :, 0:4 * nd], AF.Exp)
        P.act(Wt[:, :, 0:4 * nd], psv[:, :, 0:4 * nd], AF.Exp, scale=-1.0, bias=LN8)
        if nd == 2:
            P.act(IWt.v(), psv[:, :, 0:8], AF.Exp, scale=1.0, bias=-LN8)
        P.act(DECt[:, :, 0:4 * nd], psv[:, :, 8:8 + 4 * nd], AF.Exp, scale=-1.0)

    def state_update(c, x, pdb):
        Cx = Cst[x]
        for bk in range(2):
            P.tt(Cx[:, bk * 258:(bk + 1) * 258], pdb[bk][0:64, 0:258], Cx[:, bk * 258:(bk + 1) * 258], ALU.add)
        C3 = Cx.v().rearrange("p (h e) -> p h e", h=4)
        P.tt(C3, C3, bc_last(DECt[:, c, x * 4:(x + 1) * 4], [64, 4, 129]), ALU.mult)

    pre_ctr = [0]

    def dmisc0(col, src):
        rows, n = src.ap.shape[0], src.ap.shape[1]
        P.copy(dstage[0:rows, 0:n], src)
        P.dma(dbg["d_misc"][0:rows, col:col + n], dstage[0:rows, 0:n], q="pool")

    def gen_pre(ch, src_v, col0s, convw, gsel, Cx, save, wq, wv, wg):
        B_ = PB[ch]
        hT, zk, kTp, accp, ktok128, vaug128, vws = (B_[k_] for k_ in ("hT", "zk", "kTp", "accp", "ktok128", "vaug128", "vws"))
        Gp, Ep, Lp, WSp, DECp = (B_[k_] for k_ in ("Gp", "Ep", "Lp", "WSp", "DECp"))
        mybanks = [pb[2 * ch], pb[2 * ch + 1]]
        cnt = [0]

        def mb():
            cnt[0] += 1
            return mybanks[cnt[0] % 2]

        gcol = ch * 64
        for ti, col0 in enumerate(col0s):
            if save:
                P.dma(cstart_d[ti].v(), Cx.v(), q="sp")
            for sg in range(2):
                rmsnorm_seg(src_v[:, :, col0 + sg * 258:col0 + (sg + 1) * 258], g_mix8, hT[:, :, sg * 258:(sg + 1) * 258], 258)
                yield
            for blk in range(2):
                for sg in range(2):
                    ps = mb()
                    for kc in range(8):
                        P.mm(ps[:, 0:258], wq[:, kc, 256 + blk * 128:256 + (blk + 1) * 128], hT[:, kc, sg * 258:(sg + 1) * 258],
                             start=(kc == 0), stop=(kc == 7))
                    P.copy(zk[:, blk, sg * 258:(sg + 1) * 258], ps[:, 0:258], eng="act")
                yield
                P.ts(accp.v(), zk[:, blk, 0:512], convw[:, blk * 5:blk * 5 + 1], ALU.mult)
                for j in range(1, 5):
                    P.stt(accp.v(), zk[:, blk, j:j + 512], convw[:, blk * 5 + j:blk * 5 + j + 1], accp.v(), ALU.mult, ALU.add)
                P.act(kTp[:, blk, :], accp.v(), AF.Silu, bias=cf("convb_p")[:, blk:blk + 1])
                yield
            for tb in range(4):
                ps = mb()
                for kc in range(8):
                    P.mm(ps.v(), hT[:, kc, 2 + tb * 128:2 + (tb + 1) * 128], wv[:, kc, :], start=(kc == 0), stop=(kc == 7))
                P.copy(vaug128[:, tb, :, 0:128], ps.v().rearrange("p (h e) -> p h e", h=4), eng="act")
                for kc in range(8):
                    P.mm(pb[6][:, gcol + tb * 16:gcol + (tb + 1) * 16], hT[:, kc, 2 + tb * 128:2 + (tb + 1) * 128], wg[:, kc, :],
                         start=(kc == 0), stop=(kc == 7))
                pt = pbf(4 + ch)
                for blk in range(2):
                    P.transpose(pt[:, (tb % 2) * 256 + blk * 128:(tb % 2) * 256 + (blk + 1) * 128], kTp[:, blk, tb * 128:(tb + 1) * 128], identb.v())
                P.copy(ktok128[:, tb, :], pt[:, (tb % 2) * 256:(tb % 2) * 256 + 256])
                yield
            gb = cf("gbias")
            for tb in range(4):
                P.tt(Gp[:, tb, :], pb[6][:, gcol + tb * 16 + gsel * 8:gcol + tb * 16 + gsel * 8 + 8], gb[:, gsel * 8:gsel * 8 + 8], ALU.add)
            P.act(Ep.v(), Gp[:, :, 4:8], AF.Exp, scale=-1.0)
            P.act(Lp.v(), Ep.v(), AF.Ln, bias=1.0)
            yield
            pss_ = pb[7]
            pc = 32 * ch + 320
            sut = cf("sut128")
            onesf = cf("ones")
            for tb in range(4):
                P.mm(pss_[:, pc + tb * 4:pc + (tb + 1) * 4], sut, Lp[:, tb, :], start=True, stop=(tb == 3))
                for t2_ in range(tb + 1, 4):
                    P.mm(pss_[:, pc + tb * 4:pc + (tb + 1) * 4], onesf, Lp[:, t2_, :], start=False, stop=(t2_ == 3))
            for tb in range(4):
                P.mm(pss_[0:64, pc + 16:pc + 20], onesf[:, 0:64], Lp[:, tb, :], start=(tb == 0), stop=(tb == 3))
            P.tt(WSp.v(), Gp[:, :, 0:4], pss_[:, pc:pc + 16].rearrange("p (t h) -> p t h", t=4), ALU.subtract)
            P.act(WSp.v(), WSp.v(), AF.Exp)
            P.act(DECp.v(), pss_[0:64, pc + 16:pc + 20], AF.Exp, scale=-1.0)
            yield
            for tb in range(4):
                P.tt(vws[:, tb], vaug128[:, tb], bc_last(WSp[:, tb, :], [128, 4, 129]), ALU.mult)
            yield
            pdb = mybanks
            for h in range(4):
                for tb in range(4):
                    P.mm(pdb[h // 2][0:64, (h % 2) * 129:(h % 2) * 129 + 129], ktok128[:, tb, h * 64:(h + 1) * 64], vws[:, tb, h, :],
                         start=(tb == 0), stop=(tb == 3))
            C3 = Cx.v().rearrange("p (h e) -> p h e", h=4)
            P.tt(C3, C3, bc_last(DECp.v(), [64, 4, 129]), ALU.mult)
            for bk in range(2):
                P.tt(Cx[:, bk * 258:(bk + 1) * 258], pdb[bk][0:64, 0:258], Cx[:, bk * 258:(bk + 1) * 258], ALU.add)
            yield
        if save:
            P.dma(cstart_d[len(col0s)].v(), Cx.v(), q="sp")

    if do_pre:
        o_, l_ = CHOFF["qk_m"]
        P.dma(ring[1][:, 0:l_], wall_d[:, o_:o_ + l_], q="pool")
        o_, l_ = CHOFF["v_m"]
        P.dma(ring[2][:, 0:l_], wall_d[:, o_:o_ + l_], q="pool")
        o_, l_ = CHOFF["g16"]
        P.dma(ring[3][:, 0:l_], wall_d[:, o_:o_ + l_], q="pool")
    conv_chunks()
    if do_pre:
        wq_p = ring[1][:, 0:4096].rearrange("p (kc n) -> p kc n", kc=8)
        wv_p = ring[2][:, 0:4096].rearrange("p (kc n) -> p kc n", kc=8)
        wg_p = ring[3][:, 0:128].rearrange("p (kc n) -> p kc n", kc=8)
        for B_ in PB:
            P.memset(B_["vaug128"][:, :, :, 128:129], 1.0)
        P.memset(Cst[0].v(), 0.0)
        P.memset(Cst[1].v(), 0.0)
        gens = [gen_pre(0, xp_v, [t * TILE for t in range(NT)], cf("convw_pp"), 1, Cst[1], False, wq_p, wv_p, wg_p),
                gen_pre(1, xo_v, [256 + t * TILE - 2 for t in range(NT - 1)], cf("convw_po"), 0, Cst[0], True, wq_p, wv_p, wg_p)]
        while gens:
            for g_ in list(gens):
                try:
                    next(g_)
                except StopIteration:
                    gens.remove(g_)
        if debug:
            P.dma(dbg["d_c"].v(), Cst[1].v(), q="pool")
    else:
        P.memset(Cst[1].v(), 0.0)
        P.memset(Cst[0].v(), 0.0)
        for t in range(NT):
            P.dma(cstart_d[t].v(), Cst[0].v(), q="pool")
    P.copy(Cb[1].v(), Cst[1].v())
    xq[0] = "pool"
    rotn[0] = 4

    def dump(nm, view_rows, tile_i, src):
        rows = src.ap.shape[0]
        P.copy(dstage[0:rows, :], src)
        P.dma(dbg[nm][view_rows, tile_i * TILE:(tile_i + 1) * TILE], dstage[0:rows, :], q="pool")

    def dmisc(col, src):
        rows, n = src.ap.shape[0], src.ap.shape[1]
        P.copy(dstage[0:rows, 0:n], src)
        P.dma(dbg["d_misc"][0:rows, col:col + n], dstage[0:rows, 0:n], q="pool")

    hT_for = [None]

    def center_load(tile_i):
        c0 = 256 + tile_i * TILE
        return [rms_load(xo_v[:, :, c0 - 2 + sg * 258:c0 - 2 + (sg + 1) * 258], 258) for sg in range(2)]

    def center_norm(tile_i, bufs=None):
        c0 = 256 + tile_i * TILE
        for sg in range(2):
            rmsnorm_seg(xo_v[:, :, c0 - 2 + sg * 258:c0 - 2 + (sg + 1) * 258], g_mix8, hT[:, :, sg * 258:(sg + 1) * 258], 258,
                        xsb=(bufs[sg] if bufs else None))
        hT_for[0] = tile_i

    def main_tile(tile_i):
        c0 = 256 + tile_i * TILE
        hc = hT[:, :, 2:514]
        if hT_for[0] != tile_i:
            center_norm(tile_i)

        Wq = next_w("memq")[:, 0:2048].rearrange("p (kc n) -> p kc n", kc=8)
        Wnq = next_w("na_q", hold=True)[:, 0:2048].rearrange("p (kc n) -> p kc n", kc=8)
        Wnkv = next_w("na_kv", hold=True)[:, 0:4096].rearrange("p (kc n) -> p kc n", kc=8)

        def gen_mem():
            mcnt = [0]

            def mbk():
                mcnt[0] += 1
                return pb[mcnt[0] % 2]

            for h in range(4):
                sqf, rs = sqfs[h % 2], rss[h % 2]
                ps = mbk()
                for kc in range(8):
                    P.mm(ps[0:64, :], Wq[:, kc, h * 64:(h + 1) * 64], hc[:, kc, :], start=(kc == 0), stop=(kc == 7))
                P.act(sqf.v(), ps[0:64, :], AF.Square)
                ps2 = mbk()
                P.mm(ps2[0:64, :], ones64f, sqf.v())
                P.act(rs.v(), ps2[0:64, :], AF.Sqrt, bias=EPS, scale=1.0 / 64)
                P.recip(rs.v(), rs.v())
                P.stt(memq[:, h, :], ps[0:64, :], cf("g_memq", 64), rs.v(), ALU.mult, ALU.mult)
                yield
                for mc in range(2):
                    ps3 = mbk()
                    P.mm(ps3.v(), kmT[:, h, mc * 128:(mc + 1) * 128], memq[:, h, :])
                    P.act(PTm[:, h, mc, :], ps3.v(), AF.Exp, scale=0.125)
                yield
            for tb in range(4):
                hmem_tok, rdm = hmem_toks[tb % 2], rdms[tb % 2]
                pv = pb[6]
                for h in range(4):
                    for mc in range(2):
                        P.mm(pv[:, h * 65:(h + 1) * 65], PTm[:, h, mc, tb * 128:(tb + 1) * 128], vm_aug[:, mc, h, :],
                             start=(mc == 0), stop=(mc == 1))
                pv3 = pv[:, 0:260].rearrange("p (h e) -> p h e", h=4)
                P.recip(rdm.v(), pv3[:, :, 64])
                P.tt(hmem_tok.v(), pv3[:, :, 0:64], bc_last(rdm.v(), [128, 4, 64]), ALU.mult)
                pt = pbf(7)
                for h in range(4):
                    P.transpose(pt[0:64, h * 128:(h + 1) * 128], hmem_tok[:, h, :], identb.v())
                P.copy(hmemT[:, :, tb * 128:(tb + 1) * 128], pt[0:64, 0:512].rearrange("p (h t) -> p h t", h=4), eng="act")
                yield

        gq3 = cf("g_naqk").rearrange("p (h d) -> p h d", h=16)

        def gen_na(par):
            sqn, tmpn, qkn, ssn = sqns[par], tmpns[par], qkns[par], ssns[par]
            for jb in (2 + par, 4 + par, 0 + par, 6 + par):
                own = 2 <= jb <= 5
                if par == 0 and jb == 0:
                    rmsnorm_seg(xo_v[:, :, c0 - 256:c0], g_mix8, hh.v(), 256)
                if par == 0 and jb == 6:
                    rmsnorm_seg(xo_v[:, :, c0 + 512:c0 + 768], g_mix8, hh.v(), 256)
                if own:
                    hsrc = hT[:, :, 2 + (jb - 2) * 128:2 + (jb - 1) * 128]
                else:
                    hsrc = hh[:, :, (jb % 2) * 128:(jb % 2 + 1) * 128]
                pkv = pb[2 + par]
                for kc in range(8):
                    P.mm(pkv.v(), hsrc[:, kc, :], Wnkv[:, kc, :], start=(kc == 0), stop=(kc == 7))
                P.act(sqn[:, 0:8, :], pkv[:, 0:256].rearrange("p (h d) -> p h d", h=8), AF.Square)
                nh = 8
                if own:
                    pq = pb[4][:, par * 256:(par + 1) * 256]
                    for kc in range(8):
                        P.mm(pq, hsrc[:, kc, :], Wnq[:, kc, :], start=(kc == 0), stop=(kc == 7))
                    P.act(sqn[:, 8:16, :], pq.rearrange("p (h d) -> p h d", h=8), AF.Square)
                    nh = 16
                P.reduce(ssn[:, 0:nh], sqn[:, 0:nh, :], ALU.add)
                P.act(ssn[:, 0:nh], ssn[:, 0:nh], AF.Sqrt, bias=EPS, scale=1.0 / 32)
                P.recip(ssn[:, 0:nh], ssn[:, 0:nh])
                yield
                P.tt(tmpn[:, 0:8, :], pkv[:, 0:256].rearrange("p (h d) -> p h d", h=8), bc_last(ssn[:, 0:8], [128, 8, 32]), ALU.mult)
                P.tt(qkn[:, 0:8, :], tmpn[:, 0:8, :], gq3[:, 8:16, :], ALU.mult)
                if own:
                    P.tt(tmpn[:, 8:16, :], pq.rearrange("p (h d) -> p h d", h=8), bc_last(ssn[:, 8:16], [128, 8, 32]), ALU.mult)
                    P.tt(qkn[:, 8:16, :], tmpn[:, 8:16, :], gq3[:, 0:8, :], ALU.mult)
                P.copy(vna[:, jb, :, 0:32], pkv[:, 256:512].rearrange("p (h d) -> p h d", h=8), eng="act")
                yield
                pt = pbf(5 if par == 0 else 7)
                for h in range(8):
                    P.transpose(pt[0:32, h * 128:(h + 1) * 128], qkn[:, h, :], identb.v())
                P.copy(knaT[:, :, jb * 128:(jb + 1) * 128], pt[0:32, 0:1024].rearrange("p (h t) -> p h t", h=8))
                if own:
                    for h in range(8):
                        P.transpose(pt[0:32, h * 128:(h + 1) * 128], qkn[:, 8 + h, :], identb.v())
                    P.copy(qnaT[:, :, (jb - 2) * 128:(jb - 1) * 128], pt[0:32, 0:1024].rearrange("p (h t) -> p h t", h=8), eng="act")
                yield

        P.memset(vna[:, :, :, 32:33], 1.0)
        gens = [gen_mem(), gen_na(0), gen_na(1)]
        while gens:
            for g_ in list(gens):
                try:
                    next(g_)
                except StopIteration:
                    gens.remove(g_)
        if debug:
            for h in range(4):
                dump("d_hmem", slice(h * 64, (h + 1) * 64), tile_i, hmemT[:, h, :])

        for B in range(4):
            edge = (tile_i == 0 and B < 2)
            kbs = [2, 3, 4, 5] if edge else [B, B + 1, B + 2, B + 3, B + 4]
            for i, kb in enumerate(kbs):
                scb = [pb[4], pb[6]] if i % 2 == 0 else [pb[2], pb[3]]
                sbn = sbns[i % 2]
                for h in range(8):
                    P.mm(scb[h // 4][:, (h % 4) * 128:(h % 4 + 1) * 128], knaT[:, h, kb * 128:(kb + 1) * 128],
                         qnaT[:, h, B * 128:(B + 1) * 128])
                for qo in range(2):
                    d = 2 * kb - 4 - 2 * B - qo
                    slot = d + 5 if d <= 2 or not edge else 10 + (d - 3)
                    assert 0 <= slot < 14 and (edge or d <= 4), (slot, d)
                    for hb in range(2):
                        sc3 = scb[hb].v().rearrange("p (h q) -> p h q", h=4)
                        P.stt(sbn[:, hb * 4:(hb + 1) * 4, qo * 64:(qo + 1) * 64], sc3[:, :, qo * 64:(qo + 1) * 64],
                              32.0 ** -0.5, t2v[:, slot, hb * 4:(hb + 1) * 4, :], ALU.mult, ALU.add)
                P.act(PTn[:, i], sbn.v(), AF.Exp)
                if debug and tile_i == NT - 1 and B == 0 and i == 2:
                    dmisc(1536, sbn.v().rearrange("p h q -> p (h q)")[:, 0:512])
                    dmisc(2576, scb[0].v())
            pv = pb[7]
            for h in range(8):
                for i, kb in enumerate(kbs):
                    P.mm(pv[:, h * 33:(h + 1) * 33], PTn[:, i, h, :], vna[:, kb, h, :], start=(i == 0), stop=(i == len(kbs) - 1))
            pv3 = pv[:, 0:264].rearrange("p (h e) -> p h e", h=8)
            if debug and tile_i == NT - 1 and B == 0:
                dmisc(0, qnaT[:, 0, :])
                dmisc(512, knaT[:, 0, 0:512])
                dmisc(1024, knaT[:, 0, 512:1024])
                dmisc(2048, pv[:, 0:264])
                dmisc(2312, vna[:, 2].rearrange("p h e -> p (h e)"))
            P.recip(rdn.v(), pv3[:, :, 32])
            P.tt(hna_tok.v(), pv3[:, :, 0:32], bc_last(rdn.v(), [128, 8, 32]), ALU.mult)
            pt = pbf(5)
            hna2 = hna_tok.v().rearrange("p h d -> p (h d)")
            for k2 in range(2):
                P.transpose(pt[:, k2 * 128:(k2 + 1) * 128], hna2[:, k2 * 128:(k2 + 1) * 128], identb.v())
            P.copy(hnaT[:, :, B * 128:(B + 1) * 128], pt[:, 0:256].rearrange("p (k t) -> p k t", k=2))
        if debug:
            for k2 in range(2):
                dump("d_hna", slice(k2 * 128, (k2 + 1) * 128), tile_i, hnaT[:, k2, :])

        Wqk = next_w("qk_m")[:, 0:4096].rearrange("p (kc n) -> p kc n", kc=8)
        cwm = cf("convw_m", 64)
        for g in range(8):
            zq_, accq_ = zqs[g % 2], accqs[g % 2]
            for sg in range(2):
                ps = nb()
                for kc in range(8):
                    P.mm(ps[0:64, 0:258], Wqk[:, kc, g * 64:(g + 1) * 64], hT[:, kc, sg * 258:(sg + 1) * 258],
                         start=(kc == 0), stop=(kc == 7))
                P.copy(zq_[:, sg * 258:(sg + 1) * 258], ps[0:64, 0:258], eng="act")
            P.ts(accq_.v(), zq_[:, 0:512], cwm[:, g * 5:g * 5 + 1], ALU.mult)
            for j in range(1, 5):
                P.stt(accq_.v(), zq_[:, j:j + 512], cwm[:, g * 5 + j:g * 5 + j + 1], accq_.v(), ALU.mult, ALU.add)
            P.act(qkT[:, g, :], accq_.v(), AF.Silu, bias=cf("convb_m", 64)[:, g:g + 1])
        Wv = next_w("v_m")[:, 0:4096].rearrange("p (kc n) -> p kc n", kc=8)
        P.memset(vaug[:, :, :, 128:129], 1.0)
        for c in range(NCH):
            ps = nb()
            for kc in range(8):
                P.mm(ps[0:64, :], hT[:, kc, 2 + c * 64:2 + (c + 1) * 64], Wv[:, kc, :], start=(kc == 0), stop=(kc == 7))
            P.copy(vaug[:, c, :, 0:128], ps[0:64, :].rearrange("p (h e) -> p h e", h=4), eng="act")
        Wo = next_w("o_m")[:, 0:4096].rearrange("p (kc n) -> p kc n", kc=8)
        for c in range(NCH):
            ps = nb()
            for kc in range(8):
                P.mm(ps[0:64, :], hT[:, kc, 2 + c * 64:2 + (c + 1) * 64], Wo[:, kc, :], start=(kc == 0), stop=(kc == 7))
            P.act(osig[:, c, :], ps[0:64, :], AF.Sigmoid)
        Wg = next_w("g16")[:, 0:128].rearrange("p (kc n) -> p kc n", kc=8)
        for c in range(NCH):
            for kc in range(8):
                P.mm(pb[6][0:64, c * 16:(c + 1) * 16], hT[:, kc, 2 + c * 64:2 + (c + 1) * 64], Wg[:, kc, :], start=(kc == 0), stop=(kc == 7))
        gb = cf("gbias", 64)
        for c in range(NCH):
            P.tt(Gt[:, c, :], pb[6][0:64, c * 16:(c + 1) * 16], gb, ALU.add)
        gate_math(2)
        for c in range(NCH):
            pt = pbf(4)
            for h in range(4):
                P.transpose(pt[0:64, h * 64:(h + 1) * 64], qkT[:, 4 + h, c * 64:(c + 1) * 64], identb[0:64, 0:64])
            P.copy(ktok[:, c, :], pt[0:64, 0:256])
        P.dma(Cst[0].v(), cstart_d[tile_i].v(), q="pool")
        P.copy(Cb[0].v(), Cst[0].v(), eng="act")

        vu2 = zqs[1].v().bitcast(BF16)

        def vubuf(x, par):
            if par == 0:
                return vus[x].v()
            return vu2[:, x * 516:(x + 1) * 516].rearrange("p (h e) -> p h e", h=4)

        def make_vu(c, x, par):
            P.tt(vubuf(x, par), vaug[:, c], bc_last(Ut[:, c, x * 4:(x + 1) * 4], [64, 4, 129]), ALU.mult, eng="pool")

        def st_mm(c, x):
            cs = slice(c * 64, (c + 1) * 64)
            st = pb[0] if x == 0 else pb[3]
            for h in range(4):
                P.mm(st[0:64, h * 64:(h + 1) * 64], qkT[:, 4 + h, cs], qkT[:, h, cs])

        def pm_op(c, x):
            st = pb[0] if x == 0 else pb[3]
            P.tt(PMs[x].v(), st[0:64, 0:256], cf("mask4A" if x == 0 else "mask4B", 64), ALU.mult)

        def front_a(c, x):
            st_mm(c, x)
            pm_op(c, x)

        def front_b(c, x, par):
            PM, vu = PMs[x], vubuf(x, par)
            cs = slice(c * 64, (c + 1) * 64)
            pob = [pb[1], pb[2]] if x == 0 else [pb[4], pb[6]]
            for h in range(4):
                po_h = pob[h // 2][0:64, (h % 2) * 129:(h % 2) * 129 + 129]
                P.mm(po_h, PM[:, h * 64:(h + 1) * 64], vu[:, h, :], start=True, stop=False)
                P.mm(po_h, qkT[:, h, cs], Cb[x][:, h * 129:(h + 1) * 129], start=False, stop=True)

        def norm_abs(c, x):
            t1 = t1s[x]
            pob = [pb[1], pb[2]] if x == 0 else [pb[4], pb[6]]
            for bk in range(2):
                po3 = pob[bk][0:64, 0:258].rearrange("p (h e) -> p h e", h=2)
                P.act(t1[:, bk * 2:(bk + 1) * 2].rearrange("p (h o) -> p h o", o=1), po3[:, :, 128:129], AF.Abs)

        def norm_dve(c, x):
            t1, rr = t1s[x], rrs[x]
            P.tt(t1.v(), t1.v(), IWt[:, c, x * 4:(x + 1) * 4], ALU.max)
            P.recip(rr.v(), t1.v())

        def norm_copy(c, x, first):
            rr, tmph = rrs[x], tmphs[x]
            pob = [pb[1], pb[2]] if x == 0 else [pb[4], pb[6]]
            h4 = hacc[:, c, :].rearrange("p (h e) -> p h e", h=4)
            for bk in range(2):
                for h2 in range(2):
                    h = bk * 2 + h2
                    src = pob[bk][0:64, h2 * 129:h2 * 129 + 128]
                    if first:
                        P.act(h4[:, h, :], src, AF.Copy, scale=rr[:, h:h + 1])
                    else:
                        P.act(tmph[:, h2, :], src, AF.Copy, scale=rr[:, h:h + 1])
                if not first:
                    P.tt(h4[:, bk * 2:(bk + 1) * 2, :], h4[:, bk * 2:(bk + 1) * 2, :], tmph.v(), ALU.add, eng="pool")

        def state_mm(c, x, par):
            vu = vubuf(x, par)
            pdv = pb[5] if x == 0 else pb[7]
            pdn = (pb[0] if x == 0 else pb[3])
            for h in range(4):
                P.mm(pdv[0:64, h * 128:(h + 1) * 128], ktok[:, c, h * 64:(h + 1) * 64], vu[:, h, 0:128])
            for h in range(4):
                P.mm(pdn[0:64, 256 + h:257 + h], ktok[:, c, h * 64:(h + 1) * 64], vu[:, h, 128:129])

        def state_dve(c, x):
            pdv = pb[5] if x == 0 else pb[7]
            pdn = (pb[0] if x == 0 else pb[3])
            C3 = Cst[x].v().rearrange("p (h e) -> p h e", h=4)
            P.tt(C3[:, :, 0:128], pdv[0:64, :].rearrange("p (h e) -> p h e", h=4), C3[:, :, 0:128], ALU.add)
            P.tt(C3[:, :, 128:129], pdn[0:64, 256:260].rearrange("p (h o) -> p h o", o=1), C3[:, :, 128:129], ALU.add)
            decb = bc_last(DECt[:, c, x * 4:(x + 1) * 4], [64, 4, 129])
            P.tt(Cb[x].v().rearrange("p (h e) -> p h e", h=4), C3, decb, ALU.mult)
            P.tt(C3, C3, decb, ALU.mult, eng="pool")

        def finalize(c, k):
            sqh_, hn_ = zqs[k % 2][:, 0:512], accqs[k % 2].v()
            hmt = hm_tok.v() if k % 2 == 0 else vus[0].v().rearrange("p h e -> p (h e)")[:, 0:512]
            ssh_ = t1s[k % 2]
            P.tt(sqh_, hacc[:, c, :], hacc[:, c, :], ALU.mult, eng="pool")
            P.reduce(ssh_.v(), sqh_.rearrange("p (h e) -> p h e", h=4), ALU.add)
            P.act(ssh_.v(), ssh_.v(), AF.Sqrt, bias=EPS, scale=1.0 / 128)
            P.recip(ssh_.v(), ssh_.v())
            P.tt(hn_.rearrange("p (h e) -> p h e", h=4), hacc[:, c, :].rearrange("p (h e) -> p h e", h=4),
                 bc_last(ssh_.v(), [64, 4, 128]), ALU.mult)
            P.tt(hn_, hn_, cf("g_mlstm", 64), ALU.mult)
            P.tt(hmt, hn_, osig[:, c, :], ALU.mult, eng="pool")
            pt = pbf(0 if k % 2 == 0 else 3)
            for k4 in range(4):
                P.transpose(pt[:, 768 + k4 * 64:768 + (k4 + 1) * 64], hmt[:, k4 * 128:(k4 + 1) * 128], identb[0:64, 0:64])
            P.copy(hmT[:, :, c * 64:(c + 1) * 64], pt[:, 768:1024].rearrange("p (k t) -> p k t", k=4), eng="act")

        make_vu(0, 0, 0)
        make_vu(NCH - 1, 1, 0)
        front_a(0, 0)
        front_a(NCH - 1, 1)
        for i_ in range(NCH):
            ca, cb = i_, NCH - 1 - i_
            par = i_ % 2
            first = i_ < NCH // 2
            if i_ + 1 < NCH:
                make_vu(ca + 1, 0, 1 - par)
                make_vu(cb - 1, 1, 1 - par)
            front_b(ca, 0, par)
            front_b(cb, 1, par)
            state_mm(ca, 0, par)
            state_mm(cb, 1, par)
            if i_ + 1 < NCH:
                st_mm(ca + 1, 0)
                st_mm(cb - 1, 1)
            norm_abs(ca, 0)
            norm_abs(cb, 1)
            norm_dve(ca, 0)
            norm_dve(cb, 1)
            if i_ + 1 < NCH:
                pm_op(ca + 1, 0)
                pm_op(cb - 1, 1)
            norm_copy(ca, 0, first)
            norm_copy(cb, 1, first)
            state_dve(ca, 0)
            state_dve(cb, 1)
        for k_, c_ in enumerate(range(NCH)):
            finalize(c_, k_)
        if debug:
            for k4 in range(4):
                dump("d_hm", slice(k4 * 128, (k4 + 1) * 128), tile_i, hmT[:, k4, :])

        for j in range(8):
            P.dma(x1Ts[j].v(), xo_v[:, j, c0:c0 + TILE], q="pool")

        for j in range(8):
            W = next_w("mrg%d" % j)
            Wgt = W[:, 0:3072].rearrange("p (kc n) -> p kc n", kc=8)
            Wpm = W[:, 3072:3584].rearrange("p (kc n) -> p kc n", kc=4)
            Wpn = W[:, 3584:3840].rearrange("p (kc n) -> p kc n", kc=2)
            Wpe = W[0:64, 3840:4352].rearrange("p (kc n) -> p kc n", kc=4)
            yacc = yaccs[j % 2]
            for b in range(3):
                gs = gss[(j * 3 + b) % 2]
                tmpy = tmpys[b % 2]
                pg = nb()
                for kc in range(8):
                    P.mm(pg.v(), Wgt[:, kc, b * 128:(b + 1) * 128], hc[:, kc, :], start=(kc == 0), stop=(kc == 7))
                P.act(gs.v(), pg.v(), AF.Sigmoid)
                pp = nb()
                if b == 0:
                    for k4 in range(4):
                        P.mm(pp.v(), Wpm[:, k4, :], hmT[:, k4, :], start=(k4 == 0), stop=(k4 == 3))
                    P.tt(yacc.v(), pp.v(), gs.v(), ALU.mult)
                elif b == 1:
                    for k2 in range(2):
                        P.mm(pp.v(), Wpn[:, k2, :], hnaT[:, k2, :], start=(k2 == 0), stop=(k2 == 1))
                    P.tt(tmpy.v(), pp.v(), gs.v(), ALU.mult)
                    P.tt(yacc.v(), yacc.v(), tmpy.v(), ALU.add)
                else:
                    for h in range(4):
                        P.mm(pp.v(), Wpe[:, h, :], hmemT[:, h, :], start=(h == 0), stop=(h == 3))
                    P.tt(tmpy.v(), pp.v(), gs.v(), ALU.mult)
                    P.tt(yTs[j].v(), yacc.v(), tmpy.v(), ALU.add)
            if debug:
                dump("d_y", slice(j * 128, (j + 1) * 128), tile_i, yTs[j].v())

        for j in range(8):
            if j % 4 == 0:
                Wout = next_w("out%d" % (j // 4))[:, 0:4096].rearrange("p (kc n) -> p kc n", kc=8)
            ps = nb()
            for kc in range(8):
                P.mm(ps.v(), Wout[:, kc, (j % 4) * 128:(j % 4 + 1) * 128], yTs[kc].v(), start=(kc == 0), stop=(kc == 7))
            P.tt(x1Ts[j].v(), x1Ts[j].v(), ps.v(), ALU.add)
            P.act(sqxs[j % 2].v(), x1Ts[j].v(), AF.Square)
            if j >= 1:
                P.mm(pb[7].v(), onesb.v(), sqxs[(j - 1) % 2].v(), start=(j == 1), stop=False)
            if debug:
                dump("d_x1", slice(j * 128, (j + 1) * 128), tile_i, x1Ts[j].v())
        P.mm(pb[7].v(), onesb.v(), sqxs[1].v(), start=False, stop=True)
        P.act(rstd.v(), pb[7].v(), AF.Sqrt, bias=EPS, scale=1.0 / D)
        P.recip(rstd.v(), rstd.v())
        for j in range(8):
            P.stt(hnTs[j].v(), x1Ts[j].v(), g_ffn8[:, j:j + 1], rstd.v(), ALU.mult, ALU.mult)

        nxt = (tile_i - 1 >= NT - n_main_tiles)
        if nxt:
            nbufs = center_load(tile_i - 1)

        for i in range(8):
            Wup = next_w("up%d" % i)[:, 0:4096].rearrange("p (kc n) -> p kc n", kc=8)
            for f4 in range(4):
                f = i * 4 + f4
                ps = nb()
                for kc in range(8):
                    P.mm(ps.v(), Wup[:, kc, f4 * 128:(f4 + 1) * 128], hnTs[kc].v(), start=(kc == 0), stop=(kc == 7))
                P.act(rls[f % 2].v(), ps.v(), AF.Relu)
                P.tt(aTs[i][:, f4, :], rls[f % 2].v(), rls[f % 2].v(), ALU.mult, eng="pool")
        if nxt:
            center_norm(tile_i - 1, nbufs)
        last = []
        for j in range(8):
            Wdn = next_w("dn%d" % j)[:, 0:4096].rearrange("p (f n) -> p f n", f=32)
            ps = nb()
            for f in range(32):
                P.mm(ps.v(), Wdn[:, f, :], aTs[f // 4][:, f % 4, :], start=(f == 0), stop=(f == 31))
            o_ = ost[j % 2]
            P.tt(o_.v(), ps.v(), x1Ts[j].v(), ALU.add)
            last.append(P.dma(outT_d[j * 128:(j + 1) * 128, tile_i * TILE:(tile_i + 1) * TILE], o_.v(), q="pool"))
        return last

    finals = []
    for tile_i in range(NT - 1, NT - 1 - n_main_tiles, -1):
        finals = main_tile(tile_i)
    fin = list(finals[-2:])
    for e in ("pool",):
        dmas = [op for op in P.ops[e] if op.is_dma]
        if dmas:
            fin.append(dmas[-1])
    for op in fin:
        op.needed = True
    P.emit(final_wait_ops=fin)
    P.close()
    return nc, P


_CACHE = {}


def kernel(**inputs):
    maps = _host_prep(inputs)
    if "nc" not in _CACHE:
        _CACHE["nc"] = build_program()[0]
    nc = _CACHE["nc"]
    res = run_bass_kernel_spmd(nc, maps, core_ids=list(range(8)))
    return _assemble(res.results)
```

```python
import math
from contextlib import ExitStack
import numpy as np
import concourse.bass as bass
import concourse.mybir as mybir
from concourse.bass_utils import run_bass_kernel_spmd

F32 = mybir.dt.float32
BF16 = mybir.dt.bfloat16
AF = mybir.ActivationFunctionType
ALU = mybir.AluOpType
AX = mybir.AxisListType

ENGS = ["pe", "act", "dve", "pool", "sp"]
SAME_DIST = 3
SEM_LIMIT = 30000


class V:
    __slots__ = ("t", "ap")

    def __init__(self, t, ap):
        self.t = t
        self.ap = ap

    def __getitem__(self, k):
        return V(self.t, self.ap[k])

    def rearrange(self, pattern_, **kw):
        return V(self.t, self.ap.rearrange(pattern_, **kw))

    def bc(self, shape):
        return V(self.t, self.ap.to_broadcast(list(shape)))

    def bitcast(self, dt):
        return V(self.t, self.ap.bitcast(dt))


class TT:
    def __init__(self, name, ap, lo=0, hi=0, region=None):
        self.name = name
        self.ap = ap
        self.w = None
        self.r = {}
        self.dsem = None
        self.dcount = 0
        self.overlaps = []
        self.lo, self.hi = lo, hi
        if region is not None:
            for o in region:
                if o.lo < hi and lo < o.hi:
                    o.overlaps.append(self)
                    self.overlaps.append(o)
            region.append(self)

    def __getitem__(self, k):
        return V(self, self.ap[k])

    def v(self):
        return V(self, self.ap)


class Op:
    __slots__ = ("eng", "idx", "fn", "waits", "needed", "is_dma", "semkey", "semval", "cnt")

    def __init__(self, eng, idx, fn, is_dma):
        self.eng = eng
        self.idx = idx
        self.fn = fn
        self.waits = []
        self.needed = False
        self.is_dma = is_dma
        self.semkey = None
        self.semval = None
        self.cnt = None


class Prog:
    def __init__(self, nc):
        self.nc = nc
        self.ops = {e: [] for e in ENGS}
        self.waited = {e: {} for e in ENGS}
        self.next_dsem = 0
        self.stack = ExitStack()

    def sbuf(self, name, shape, dt):
        h = self.stack.enter_context(self.nc.sbuf_tensor("sb_" + name, list(shape), dt))
        return TT(name, h[:])

    def psum(self, name, shape, dt):
        h = self.stack.enter_context(self.nc.psum_tensor("ps_" + name, list(shape), dt))
        return TT(name, h[:])

    def dram(self, name, ap):
        return TT("@" + name, ap)

    def _add(self, eng, fn, reads, writes, dma_tile=None):
        ops = self.ops[eng]
        op = Op(eng, len(ops), fn, dma_tile is not None)
        rt = [x.t for x in reads]
        wt = [x.t for x in writes]
        cand = []
        for t in rt:
            if t.w is not None:
                cand.append(t.w)
            for o in t.overlaps:
                if o.w is not None:
                    cand.append(o.w)
        for t in wt:
            if t.w is not None:
                cand.append(t.w)
            cand.extend(t.r.values())
            for o in t.overlaps:
                if o.w is not None:
                    cand.append(o.w)
                cand.extend(o.r.values())
        wd = self.waited[eng]
        deps = {}
        for d in cand:
            if d.is_dma:
                key, val = d.semkey, d.semval
            else:
                key, val = d.eng, d.idx
                if d.eng == eng and (eng == "pe" or op.idx - d.idx > SAME_DIST):
                    continue
            if wd.get(key, -1) >= val:
                continue
            if key not in deps or deps[key][1] < val:
                deps[key] = (d, val)
        for key, (d, val) in deps.items():
            wd[key] = val
            d.needed = True
            op.waits.append(d)
        if dma_tile is not None:
            if dma_tile.dsem is None:
                dma_tile.dsem = {}
                dma_tile.dcount = {}
            if eng not in dma_tile.dsem:
                dma_tile.dsem[eng] = self.next_dsem
                dma_tile.dcount[eng] = 0
                self.next_dsem += 1
            dma_tile.dcount[eng] += 1
            op.semkey = ("d", dma_tile.dsem[eng])
            op.semval = 16 * dma_tile.dcount[eng]
            rkey = op.semkey
        else:
            rkey = eng
        for t in rt:
            t.r[rkey] = op
        for t in wt:
            t.w = op
            t.r = {}
        ops.append(op)
        return op

    def mm(self, out, lhsT, rhs, start=True, stop=True):
        self._add("pe", lambda e: e.matmul(out.ap, lhsT.ap, rhs.ap, start=start, stop=stop),
                  [lhsT, rhs] + ([] if start else [out]), [out])

    def transpose(self, out, in_, ident):
        self._add("pe", lambda e: e.transpose(out.ap, in_.ap, ident.ap), [in_, ident], [out])

    def act(self, out, in_, func, bias=None, scale=None, accum_out=None):
        reads = [in_]
        kw = {}
        if bias is not None:
            if isinstance(bias, V):
                reads.append(bias)
                kw["bias"] = bias.ap
            else:
                kw["bias"] = bias
        if scale is not None:
            if isinstance(scale, V):
                reads.append(scale)
                kw["scale"] = scale.ap
            else:
                kw["scale"] = scale
        writes = [out]
        if accum_out is not None:
            writes.append(accum_out)
            kw["accum_out"] = accum_out.ap
        self._add("act", lambda e: e.activation(out.ap, in_.ap, func, **kw), reads, writes)

    def tt(self, out, in0, in1, op, eng="dve"):
        self._add(eng, lambda e: e.tensor_tensor(out.ap, in0.ap, in1.ap, op), [in0, in1], [out])

    def ts(self, out, in0, s1, op0, s2=None, op1=None, eng="dve"):
        reads = [in0]
        a1 = s1
        if isinstance(s1, V):
            reads.append(s1)
            a1 = s1.ap
        a2 = s2
        if isinstance(s2, V):
            reads.append(s2)
            a2 = s2.ap
        if op1 is None:
            self._add(eng, lambda e: e.tensor_scalar(out.ap, in0.ap, a1, None, op0), reads, [out])
        else:
            self._add(eng, lambda e: e.tensor_scalar(out.ap, in0.ap, a1, a2, op0, op1), reads, [out])

    def stt(self, out, in0, scalar, in1, op0, op1, eng="dve"):
        reads = [in0, in1]
        a = scalar
        if isinstance(scalar, V):
            reads.append(scalar)
            a = scalar.ap
        self._add(eng, lambda e: e.scalar_tensor_tensor(out.ap, in0.ap, a, in1.ap, op0, op1), reads, [out])

    def copy(self, out, in_, eng="dve"):
        if eng == "act":
            self._add(eng, lambda e: e.copy(out.ap, in_.ap), [in_], [out])
        else:
            self._add(eng, lambda e: e.tensor_copy(out.ap, in_.ap), [in_], [out])

    def memset(self, out, val, eng="dve"):
        self._add(eng, lambda e: e.memset(out.ap, val), [], [out])

    def reduce(self, out, in_, op, axis=AX.X, eng="dve"):
        self._add(eng, lambda e: e.tensor_reduce(out.ap, in_.ap, axis, op), [in_], [out])

    def recip(self, out, in_):
        self._add("dve", lambda e: e.reciprocal(out.ap, in_.ap), [in_], [out])

    def dma(self, out, in_, q="sp", sem_tile=None):
        if sem_tile is None:
            sem_tile = out.t if out.t.name[0] != "@" else in_.t
        op = self._add(q, None, [in_], [out], dma_tile=sem_tile)
        op.fn = (out.ap, in_.ap)
        return op

    def emit(self, final_wait_ops=()):
        nc = self.nc
        st = self.stack
        esems = {}
        for e in ENGS:
            c = 0
            for op in self.ops[e]:
                if not op.is_dma and op.needed:
                    c += 1
                    op.cnt = c
            esems[e] = [st.enter_context(nc.semaphore("s_%s_%d" % (e, i))) for i in range(c // SEM_LIMIT + 1)]
        dsems = [st.enter_context(nc.semaphore("d_%d" % i)) for i in range(self.next_dsem)]

        def semof(d):
            if d.is_dma:
                return dsems[d.semkey[1]], d.semval
            k = (d.cnt - 1) // SEM_LIMIT
            return esems[d.eng][k], d.cnt - k * SEM_LIMIT

        block = st.enter_context(nc.Block())
        engattr = {"pe": "tensor", "act": "scalar", "dve": "vector", "pool": "gpsimd", "sp": "sync"}
        prog = self

        def make(ename):
            def body(eng):
                for op in prog.ops[ename]:
                    for d in op.waits:
                        s, v = semof(d)
                        eng.wait_ge(s, v)
                    if op.is_dma:
                        o, i = op.fn
                        eng.dma_start(out=o, in_=i).then_inc(dsems[op.semkey[1]], 16)
                    else:
                        ins = op.fn(eng)
                        if op.needed:
                            s, v = semof(op)
                            ins.then_inc(s, 1)
                if ename == "sp":
                    for d in final_wait_ops:
                        s, v = semof(d)
                        eng.wait_ge(s, v)
            return body

        for e in ENGS:
            getattr(block, engattr[e])(make(e))

    def close(self):
        self.stack.close()


D = 1024
S = 8192
NB = 4
HALF = 4096
TILE = 512
NT = HALF // TILE
NCH = TILE // 64
DFF = 4096
EPS = 1e-6
NEG = -80.0
LN8 = math.log(0.125)

O_QK, O_V, O_O, O_I, O_F, O_NA, O_MQ, O_G = 0, 512, 1024, 1536, 1544, 1552, 2320, 2576
XO_W = 256 + HALF + 256
XP_W = HALF + 4

SLOT = 4352
NSLOT = 4


def _chunks():
    ch = [("memq", 8 * 256), ("na_q", 8 * 256), ("na_kv", 8 * 512), ("qk_m", 8 * 512), ("v_m", 8 * 512),
          ("o_m", 8 * 512), ("g16", 8 * 16)]
    ch += [("mrg%d" % j, SLOT) for j in range(8)]
    ch += [("out%d" % i, 8 * 512) for i in range(2)]
    ch += [("up%d" % i, 8 * 512) for i in range(8)]
    ch += [("dn%d" % j, 32 * 128) for j in range(8)]
    off = {}
    o = 0
    for n, l in ch:
        off[n] = (o, l)
        o += l
    return ch, off, o


CHUNKS, CHOFF, WTOT = _chunks()

CST_FIELDS = [("g_mix8", 8), ("g_ffn8", 8), ("g_mem8", 8), ("convw_m", 40), ("convb_m", 8),
              ("convw_po", 10), ("convw_pp", 10), ("convb_p", 2), ("gbias", 16), ("g_mlstm", 512),
              ("g_naqk", 512), ("g_memq", 1), ("g_memk", 1), ("ident", 128), ("ones", 128),
              ("triA", 64), ("triB", 64), ("mask4A", 256), ("mask4B", 256), ("sut128", 128)]
CSTOFF = {}
_o = 0
for _n, _l in CST_FIELDS:
    CSTOFF[_n] = (_o, _l)
    _o += _l
NCST = _o
NT2 = 14 * 8 * 64


def _kmaj(W):
    K, N = W.shape
    kc = K // 128
    return np.ascontiguousarray(W.reshape(kc, 128, N).transpose(1, 0, 2)).reshape(128, kc * N)


def _host_prep(inp):
    f32 = np.float32
    w_in = np.asarray(inp["w_in"][0], f32)
    maps = []
    wall0 = np.zeros((128, WTOT), f32)

    def put(name, arr):
        o, l = CHOFF[name]
        assert arr.shape == (128, l), (name, arr.shape, l)
        wall0[:, o:o + l] = arr

    put("memq", _kmaj(w_in[:, O_MQ:O_MQ + 256]))
    put("na_q", _kmaj(w_in[:, O_NA:O_NA + 256]))
    put("na_kv", _kmaj(w_in[:, O_NA + 256:O_NA + 768]))
    put("qk_m", _kmaj(w_in[:, O_QK:O_QK + 512]))
    put("v_m", _kmaj(w_in[:, O_V:O_V + 512]))
    put("o_m", _kmaj(w_in[:, O_O:O_O + 512]))
    wpm = np.asarray(inp["w_proj_mlstm"][0], f32)
    wpn = np.asarray(inp["w_proj_na"][0], f32)
    wpe = np.asarray(inp["w_proj_mem"][0], f32)
    for j in range(8):
        cols = np.concatenate([O_G + b * 1024 + j * 128 + np.arange(128) for b in range(3)])
        a = _kmaj(w_in[:, cols])
        b_ = _kmaj(wpm[:, j * 128:(j + 1) * 128])
        c_ = _kmaj(wpn[:, j * 128:(j + 1) * 128])
        d_ = np.zeros((128, 4 * 128), f32)
        d_[0:64] = wpe[:, j * 128:(j + 1) * 128].reshape(4, 64, 128).transpose(1, 0, 2).reshape(64, 512)
        put("mrg%d" % j, np.concatenate([a, b_, c_, d_], axis=1))
    w_out = np.asarray(inp["w_out"][0], f32)
    for i in range(2):
        put("out%d" % i, _kmaj(w_out[:, i * 512:(i + 1) * 512]))
    w_up = np.asarray(inp["w_up"][0], f32)
    for i in range(8):
        put("up%d" % i, _kmaj(w_up[:, i * 512:(i + 1) * 512]))
    w_dn = np.asarray(inp["w_down"][0], f32)
    for j in range(8):
        put("dn%d" % j, _kmaj(w_dn[:, j * 128:(j + 1) * 128]))

    conv_w = np.asarray(inp["conv_w"][0], f32)
    conv_b = np.asarray(inp["conv_b"][0], f32)
    b_i = np.asarray(inp["b_igate"][0], f32)
    b_f = np.asarray(inp["b_fgate"][0], f32)
    rpb = np.asarray(inp["rpb"][0], f32)
    ar = np.arange(64)
    tri = (ar[:, None] <= ar[None, :]).astype(f32)

    walls, csts, t2s = [], [], []
    for half in range(2):
        dA, dB = half, 1 - half
        wall = wall0.copy()
        gcols = np.concatenate([O_I + dA * 4 + np.arange(4), O_F + dA * 4 + np.arange(4),
                                O_I + dB * 4 + np.arange(4), O_F + dB * 4 + np.arange(4)])
        o, l = CHOFF["g16"]
        wall[:, o:o + l] = _kmaj(w_in[:, gcols])
        walls.append(wall)

        cst = np.zeros((128, NCST), f32)

        def cput(name, arr, rows=128):
            o, l = CSTOFF[name]
            cst[0:rows, o:o + l] = arr

        cput("g_mix8", np.asarray(inp["g_mix"][0], f32).reshape(8, 128).T)
        cput("g_ffn8", np.asarray(inp["g_ffn"][0], f32).reshape(8, 128).T)
        cput("g_mem8", np.asarray(inp["g_mem"][0], f32).reshape(8, 128).T)
        cw_own = conv_w if half == 0 else conv_w[::-1]
        cw_par = cw_own[::-1]
        cput("convw_m", cw_own.T.reshape(8, 64, 5).transpose(1, 0, 2).reshape(64, 40), rows=64)
        cput("convb_m", conv_b.reshape(8, 64).T, rows=64)
        cput("convw_po", cw_own.T[256:512].reshape(2, 128, 5).transpose(1, 0, 2).reshape(128, 10))
        cput("convw_pp", cw_par.T[256:512].reshape(2, 128, 5).transpose(1, 0, 2).reshape(128, 10))
        cput("convb_p", conv_b[256:512].reshape(2, 128).T)
        gb = np.concatenate([b_i[dA], b_f[dA], b_i[dB], b_f[dB]])
        cput("gbias", np.tile(gb[None, :], (128, 1)))
        cput("g_mlstm", np.tile(np.asarray(inp["g_mlstm"][0], f32)[None, :], (128, 1)))
        gq = np.asarray(inp["g_na_q"][0], f32)
        gk = np.asarray(inp["g_na_k"][0], f32)
        cput("g_naqk", np.tile(np.concatenate([np.tile(gq, 8), np.tile(gk, 8)])[None, :], (128, 1)))
        cput("g_memq", np.asarray(inp["g_mem_q"][0], f32)[:, None], rows=64)
        cput("g_memk", np.asarray(inp["g_mem_k"][0], f32)[:, None], rows=64)
        cput("ident", np.eye(128, dtype=f32))
        cput("ones", np.ones((128, 128), f32))
        cput("triA", tri, rows=64)
        cput("triB", tri.T, rows=64)
        cput("mask4A", np.tile(tri, (1, 4)), rows=64)
        cput("mask4B", np.tile(tri.T, (1, 4)), rows=64)
        a128 = np.arange(128)
        cput("sut128", (a128[:, None] > a128[None, :]).astype(f32))
        csts.append(cst)

        t2 = np.full((128, 14, 8, 64), NEG, f32)
        kj = np.arange(64)[:, None]
        cq = np.arange(64)[None, :]
        if half == 0:
            cg, kg = cq, kj
        else:
            cg, kg = 63 - cq, 63 - kj
        cs = np.clip(cg - 8, 0, 48)
        colok = (kg >= cs) & (kg <= cs + 15)
        dcg = kg - cg
        for slot in range(14):
            if slot < 10:
                d = 4 - slot
                edge = False
            else:
                d = slot - 10 + 3
                edge = True
            for ko in range(2):
                dr = d + ko
                drg = dr if half == 0 else -dr
                if edge:
                    ok = abs(drg) <= 7
                else:
                    ok = (-4 <= drg <= 3)
                if not ok:
                    continue
                idx = np.clip(dcg + 15, 0, 30)
                for h in range(8):
                    vals = rpb[h, drg + 7][idx]
                    t2[ko * 64:(ko + 1) * 64, slot, h, :] = np.where(colok, vals, NEG)
        t2s.append(t2.reshape(128, NT2))

    x = np.asarray(inp["x"], f32)
    mem = np.asarray(inp["mem"], f32)
    for b in range(NB):
        memT = np.ascontiguousarray(mem[b].T)
        for half in range(2):
            seq = x[b] if half == 0 else x[b][::-1]
            xo = np.zeros((D, XO_W), f32)
            xo[:, 256:] = seq[0:HALF + 256].T
            xp = np.zeros((D, XP_W), f32)
            xp[:, 2:] = seq[::-1][0:HALF + 2].T
            maps.append({"xo": xo, "xp": xp, "memT": memT, "wall": walls[half], "cst": csts[half],
                         "t2": t2s[half], "w_mem_kv": np.asarray(inp["w_mem_kv"][0], f32)})
    return maps


def _assemble(results):
    out = np.zeros((NB, S, D), np.float32)
    for b in range(NB):
        for half in range(2):
            o = results[2 * b + half]["outT"]
            loc = o.T
            if half == 0:
                out[b, 0:HALF] = loc
            else:
                out[b, HALF:] = loc[::-1]
    return out


DEBUG = False
RBYTES = 63488


def build_program(n_main_tiles=NT, do_pre=True, debug=False):
    nc = bass.Bass("TRN2", target_bir_lowering=False)
    P = Prog(nc)

    def din(name, shape):
        return P.dram(name, nc.dram_tensor(name, shape, F32, kind="ExternalInput").ap())

    xo_d = din("xo", [D, XO_W])
    xp_d = din("xp", [D, XP_W])
    memT_d = din("memT", [D, 256])
    wall_d = din("wall", [128, WTOT])
    cst_d = din("cst", [128, NCST])
    t2_d = din("t2", [128, NT2])
    wmkv_d = din("w_mem_kv", [D, 512])
    outT_d = P.dram("outT", nc.dram_tensor("outT", [D, HALF], F32, kind="ExternalOutput").ap())
    wbf_ap = nc.dram_tensor("wbf", [128, WTOT], BF16).ap()
    NGRP = 4
    wbf_grp = [P.dram("wbf%d" % i, wbf_ap) for i in range(NGRP)]
    cstart_ap = nc.dram_tensor("cstart", [NT, 64, 516], F32).ap()
    cstart_d = [P.dram("cstart%d" % i, cstart_ap[i]) for i in range(NT)]
    dbg = {}
    if debug:
        for nm, rows in (("d_hmem", 256), ("d_hna", 256), ("d_hm", 512), ("d_y", 1024), ("d_x1", 1024)):
            dbg[nm] = P.dram(nm, nc.dram_tensor(nm, [rows, HALF], F32, kind="ExternalOutput").ap())
        dbg["d_c"] = P.dram("d_c", nc.dram_tensor("d_c", [64, 516], F32, kind="ExternalOutput").ap())
        dbg["d_misc"] = P.dram("d_misc", nc.dram_tensor("d_misc", [128, 4096], F32, kind="ExternalOutput").ap())

    xo_v = V(xo_d, xo_d.ap.rearrange("(kc p) t -> p kc t", p=128))
    xp_v = V(xp_d, xp_d.ap.rearrange("(kc p) t -> p kc t", p=128))

    cst = P.sbuf("cst", [128, NCST], F32)

    def cf(name, rows=128):
        o, l = CSTOFF[name]
        return cst[0:rows, o:o + l]

    identb = P.sbuf("identb", [128, 128], BF16)
    onesb = P.sbuf("onesb", [128, 128], BF16)
    t2 = P.sbuf("t2", [128, NT2], BF16)
    t2v = t2.v().rearrange("p (s h c) -> p s h c", s=14, h=8, c=64)
    ring = [P.sbuf("ring%d" % i, [128, SLOT], BF16) for i in range(NSLOT)]
    xs = [P.sbuf("xs%d" % i, [128, 8, 258], F32) for i in range(2)]
    sq = P.sbuf("sq", [128, 8, 258], BF16)
    rstd = P.sbuf("rstd", [128, 512], F32)
    hT = P.sbuf("hT", [128, 8, 516], BF16)
    hh = P.sbuf("hh", [128, 8, 256], BF16)
    kmT = P.sbuf("kmT", [64, 4, 256], BF16)
    vm_aug = P.sbuf("vm_aug", [128, 2, 4, 65], BF16)
    Cst = [P.sbuf("C%d" % i, [64, 516], F32) for i in range(2)]
    Cb = [P.sbuf("Cb%d" % i, [64, 516], BF16) for i in range(2)]
    hmT = P.sbuf("hmT", [128, 4, 512], BF16)
    hnaT = P.sbuf("hnaT", [128, 2, 512], BF16)
    hmemT = P.sbuf("hmemT", [64, 4, 512], BF16)
    yTs = [P.sbuf("yT%d" % i_, [128, 512], BF16) for i_ in range(8)]
    yaccs = [P.sbuf("yacc%d" % i_, [128, 512], F32) for i_ in range(2)]
    tmpys = [P.sbuf("tmpy%d" % i_, [128, 512], F32) for i_ in range(2)]
    gss = [P.sbuf("gs%d" % i_, [128, 512], F32) for i_ in range(2)]
    sqxs = [P.sbuf("sqx%d" % i_, [128, 512], BF16) for i_ in range(2)]
    rls = [P.sbuf("rl%d" % i_, [128, 512], F32) for i_ in range(2)]
    ost = [P.sbuf("ost%d" % i, [128, 512], F32) for i in range(2)]
    dstage = P.sbuf("dstage", [128, 512], F32) if debug else None
    R = P.sbuf("R", [128, RBYTES // 2], BF16)
    region = []

    def carve(name, off, parts, shape, dt):
        n = 1
        for s_ in shape:
            n *= s_
        nb = n * (4 if dt == F32 else 2)
        assert off % 4 == 0 and off + nb <= RBYTES, (name, off, nb)
        ap = R.ap[0:parts, off // 2:(off + nb) // 2]
        if dt == F32:
            ap = ap.bitcast(F32)
        if len(shape) == 2:
            ap = ap.rearrange("p (a b) -> p a b", a=shape[0], b=shape[1])
        elif len(shape) == 3:
            ap = ap.rearrange("p (a b c) -> p a b c", a=shape[0], b=shape[1], c=shape[2])
        return TT(name, ap, off, off + nb, region)

    class Lay:
        def __init__(self):
            self.o = 0

        def __call__(self, name, parts, shape, dt):
            t = carve(name, self.o, parts, shape, dt)
            self.o = t.hi + (-t.hi) % 4
            return t

    L = Lay()
    qnaT = L("qnaT", 32, [8, 512], BF16)
    knaT = L("knaT", 32, [8, 1024], BF16)
    vna = L("vna", 128, [8, 8, 33], BF16)
    sqns = [L("sqn%d" % i_, 128, [16, 32], F32) for i_ in range(2)]
    tmpns = [L("tmpn%d" % i_, 128, [16, 32], F32) for i_ in range(2)]
    qkns = [L("qkn%d" % i_, 128, [16, 32], BF16) for i_ in range(2)]
    ssns = [L("ssn%d" % i_, 128, [16], F32) for i_ in range(2)]
    hna_tok = L("hna_tok", 128, [8, 32], BF16)
    rdn = L("rdn", 128, [8], F32)
    mem_off = L.o
    PTn = L("PTn", 128, [5, 8, 128], BF16)
    sbns = [L("sbn%d" % i_, 128, [8, 128], F32) for i_ in range(2)]
    L = Lay()
    L.o = mem_off
    memq = L("memq", 64, [4, 512], BF16)
    PTm = L("PTm", 128, [4, 2, 512], BF16)
    hmem_toks = [L("hmem_tok%d" % i_, 128, [4, 64], BF16) for i_ in range(2)]
    sqfs = [L("sqf%d" % i_, 64, [512], F32) for i_ in range(2)]
    rss = [L("rs%d" % i_, 64, [512], F32) for i_ in range(2)]
    rdms = [L("rdm%d" % i_, 128, [4], F32) for i_ in range(2)]
    L = Lay()
    qkT = L("qkT", 64, [8, 512], BF16)
    ktok = L("ktok", 64, [8, 256], BF16)
    vaug = L("vaug", 64, [8, 4, 129], BF16)
    osig = L("osig", 64, [8, 512], BF16)
    hacc = L("hacc", 64, [8, 512], F32)
    zqs = [L("zq%d" % i_, 64, [516], F32) for i_ in range(2)]
    accqs = [L("accq%d" % i_, 64, [512], F32) for i_ in range(2)]
    zq, accq = zqs[0], accqs[0]
    Gt = L("Gt", 64, [8, 16], F32)
    Et = L("Et", 64, [8, 8], F32)
    Lt = L("Lt", 64, [8, 8], F32)
    tmpu = L("tmpu", 64, [8, 8], F32)
    Ut = L("Ut", 64, [8, 8], F32)
    Wt = L("Wt", 64, [8, 8], F32)
    DECt = L("DECt", 64, [8, 8], F32)
    IWt = L("IWt", 64, [8, 8], F32)
    PMs = [L("PM%d" % i_, 64, [256], BF16) for i_ in range(2)]
    vus = [L("vu%d" % i_, 64, [4, 129], BF16) for i_ in range(2)]
    t1s = [L("t1%d" % i_, 64, [4], F32) for i_ in range(2)]
    rrs = [L("rr%d" % i_, 64, [4], F32) for i_ in range(2)]
    tmphs = [L("tmph%d" % i_, 64, [2, 128], F32) for i_ in range(2)]
    ssh = L("ssh", 64, [4], F32)
    hm_tok = L("hm_tok", 64, [512], BF16)
    L = Lay()
    PB = []
    for i_ in range(2):
        PB.append(dict(
            hT=L("p_hT%d" % i_, 128, [8, 516], BF16),
            zk=L("zk%d" % i_, 128, [2, 516], F32),
            kTp=L("kTp%d" % i_, 128, [2, 512], BF16),
            accp=L("accp%d" % i_, 128, [512], F32),
            ktok128=L("ktok128%d" % i_, 128, [4, 256], BF16),
            vaug128=L("vaug128%d" % i_, 128, [4, 4, 129], BF16),
            vws=L("vws%d" % i_, 128, [4, 4, 129], BF16),
            Gp=L("Gp%d" % i_, 128, [4, 8], F32),
            Ep=L("Ep%d" % i_, 128, [4, 4], F32),
            Lp=L("Lp%d" % i_, 128, [4, 4], F32),
            WSp=L("WSp%d" % i_, 128, [4, 4], F32),
            DECp=L("DECp%d" % i_, 64, [4], F32)))
    L = Lay()
    aTs = [L("aT%d" % i_, 128, [4, 512], BF16) for i_ in range(8)]
    hnTs = [L("hnT%d" % i_, 128, [512], BF16) for i_ in range(8)]
    x1Ts = [L("x1T%d" % i_, 128, [512], F32) for i_ in range(8)]

    pb = [P.psum("pb%d" % i, [128, 512], F32) for i in range(8)]
    rot = [0]
    rotn = [2]

    def nb():
        rot[0] = (rot[0] + 1) % rotn[0]
        return pb[rot[0]]

    def pbf(i):
        return V(pb[i], pb[i].ap.bitcast(BF16))

    P.dma(cst.v(), cst_d.v(), q="sp")
    P.dma(t2.v(), t2_d.v(), q="pool")
    P.ts(t2.v(), t2.v(), 32.0 ** 0.5, ALU.mult)
    P.copy(identb.v(), cf("ident"))
    P.copy(onesb.v(), cf("ones"))
    ones64f = cst[0:64, CSTOFF["ones"][0]:CSTOFF["ones"][0] + 64]
    triAf = cf("triA", 64)
    triBf = cf("triB", 64)
    g_mix8 = cf("g_mix8")
    g_ffn8 = cf("g_ffn8")
    g_mem8 = cf("g_mem8")

    seg_ctr = [0]

    xq = ["sp"]

    def rms_load(src, n):
        xsb = xs[seg_ctr[0] % 2]
        seg_ctr[0] += 1
        P.dma(xsb[:, :, 0:n], src, q=xq[0])
        return xsb

    def rmsnorm_seg(src, g8, dst, n, xsb=None):
        if xsb is None:
            xsb = rms_load(src, n)
        P.act(sq[:, :, 0:n], xsb[:, :, 0:n], AF.Square)
        for kc in range(8):
            P.mm(pb[7][:, 0:n], onesb.v(), sq[:, kc, 0:n], start=(kc == 0), stop=(kc == 7))
        P.act(rstd[:, 0:n], pb[7][:, 0:n], AF.Sqrt, bias=EPS, scale=1.0 / D)
        P.recip(rstd[:, 0:n], rstd[:, 0:n])
        for kc in range(8):
            P.stt(dst[:, kc, :], xsb[:, kc, 0:n], g8[:, kc:kc + 1], rstd[:, 0:n], ALU.mult, ALU.mult)

    def bc_last(v, shape):
        return V(v.t, v.ap.rearrange("p (a o) -> p a o", o=1).to_broadcast(list(shape)))

    memT_v = V(memT_d, memT_d.ap.rearrange("(kc p) t -> p kc t", p=128))
    rmsnorm_seg(memT_v, g_mem8, hh[:, :, 0:256], 256)
    wkv = ring[0][:, 0:4096].rearrange("p (kc n) -> p kc n", kc=8)
    P.dma(wkv, V(wmkv_d, wmkv_d.ap.rearrange("(kc p) n -> p kc n", p=128)), q="pool")
    for h in range(4):
        ps = nb()
        for kc in range(8):
            P.mm(ps[0:64, 0:256], wkv[:, kc, h * 64:(h + 1) * 64], hh[:, kc, :], start=(kc == 0), stop=(kc == 7))
        sqf, rs = sqfs[h % 2], rss[h % 2]
        P.act(sqf[:, 0:256], ps[0:64, 0:256], AF.Square)
        ps2 = nb()
        P.mm(ps2[0:64, 0:256], ones64f, sqf[:, 0:256])
        P.act(rs[:, 0:256], ps2[0:64, 0:256], AF.Sqrt, bias=EPS, scale=1.0 / 64)
        P.recip(rs[:, 0:256], rs[:, 0:256])
        P.stt(kmT[:, h, :], ps[0:64, 0:256], cf("g_memk", 64), rs[:, 0:256], ALU.mult, ALU.mult)
    P.memset(vm_aug.v(), 1.0)
    for mc in range(2):
        ps = nb()
        for kc in range(8):
            P.mm(ps[:, 0:256], hh[:, kc, mc * 128:(mc + 1) * 128], wkv[:, kc, 256:512], start=(kc == 0), stop=(kc == 7))
        P.copy(vm_aug[:, mc, :, 0:64], ps[:, 0:256].rearrange("p (h d) -> p h d", h=4))

    def conv_chunks():
        for i, (nm, ln) in enumerate(CHUNKS):
            o, l = CHOFF[nm]
            g = wbf_grp[min(NGRP - 1, i * NGRP // len(CHUNKS))]
            P.dma(V(g, wbf_ap[:, o:o + l]), wall_d[:, o:o + l], q="pool", sem_tile=g)

    def chunk_group(nm):
        i = [c[0] for c in CHUNKS].index(nm)
        return wbf_grp[min(NGRP - 1, i * NGRP // len(CHUNKS))]

    wseq = []
    for _t in range(n_main_tiles):
        wseq += [c[0] for c in CHUNKS]
    wstate = {"next_load": 0, "next_use": 0, "released": 0}

    def _pump():
        while wstate["next_load"] < len(wseq) and wstate["next_load"] - NSLOT < wstate["released"]:
            i = wstate["next_load"]
            nm = wseq[i]
            o, l = CHOFF[nm]
            g = chunk_group(nm)
            P.dma(ring[i % NSLOT][:, 0:l], V(g, wbf_ap[:, o:o + l]), q="sp")
            wstate["next_load"] = i + 1

    def next_w(nm, hold=False):
        i = wstate["next_use"]
        assert wseq[i] == nm, (wseq[i], nm)
        if not hold:
            wstate["released"] = i
        _pump()
        assert wstate["next_load"] > i
        wstate["next_use"] = i + 1
        return ring[i % NSLOT]

    def gate_math(ndir):
        G4 = Gt.v().rearrange("p c (d e) -> p c d e", d=2, e=8)
        E4 = Et.v().rearrange("p c (d e) -> p c d e", d=2, e=4)
        L4 = Lt.v().rearrange("p c (d e) -> p c d e", d=2, e=4)
        U4 = tmpu.v().rearrange("p c (d e) -> p c d e", d=2, e=4)
        nd = ndir
        P.act(E4[:, :, 0:nd, :], G4[:, :, 0:nd, 4:8], AF.Exp, scale=-1.0)
        P.act(L4[:, :, 0:nd, :], E4[:, :, 0:nd, :], AF.Ln, bias=1.0)
        psb = pb[5]
        for c in range(NCH):
            P.mm(psb[0:64, c * 24:c * 24 + 4], triAf, Lt[:, c, 0:4])
            if nd == 2:
                P.mm(psb[0:64, c * 24 + 4:c * 24 + 8], triBf, Lt[:, c, 4:8])
            P.mm(psb[0:64, c * 24 + 8:c * 24 + 8 + 4 * nd], ones64f, Lt[:, c, 0:4 * nd])
        psv = psb[0:64, 0:NCH * 24].rearrange("p (c e) -> p c e", c=NCH)
        bp4 = psv[:, :, 0:8].rearrange("p c (d e) -> p c d e", d=2)
        P.tt(U4[:, :, 0:nd, :], G4[:, :, 0:nd, 0:4], bp4[:, :, 0:nd, :], ALU.add)
        P.act(Ut[:, :, 0:4 * nd], tmpu[:, :, 0:4 * nd], AF.Exp)
        P.act(Wt[:, :, 0:4 * nd], psv[:, :, 0:4 * nd], AF.Exp, scale=-1.0, bias=LN8)
        if nd == 2:
            P.act(IWt.v(), psv[:, :, 0:8], AF.Exp, scale=1.0, bias=-LN8)
        P.act(DECt[:, :, 0:4 * nd], psv[:, :, 8:8 + 4 * nd], AF.Exp, scale=-1.0)

    def state_update(c, x, pdb):
        Cx = Cst[x]
        for bk in range(2):
            P.tt(Cx[:, bk * 258:(bk + 1) * 258], pdb[bk][0:64, 0:258], Cx[:, bk * 258:(bk + 1) * 258], ALU.add)
        C3 = Cx.v().rearrange("p (h e) -> p h e", h=4)
        P.tt(C3, C3, bc_last(DECt[:, c, x * 4:(x + 1) * 4], [64, 4, 129]), ALU.mult)

    pre_ctr = [0]

    def dmisc0(col, src):
        rows, n = src.ap.shape[0], src.ap.shape[1]
        P.copy(dstage[0:rows, 0:n], src)
        P.dma(dbg["d_misc"][0:rows, col:col + n], dstage[0:rows, 0:n], q="pool")

    def gen_pre(ch, src_v, col0s, convw, gsel, Cx, save, wq, wv, wg):
        B_ = PB[ch]
        hT, zk, kTp, accp, ktok128, vaug128, vws = (B_[k_] for k_ in ("hT", "zk", "kTp", "accp", "ktok128", "vaug128", "vws"))
        Gp, Ep, Lp, WSp, DECp = (B_[k_] for k_ in ("Gp", "Ep", "Lp", "WSp", "DECp"))
        mybanks = [pb[2 * ch], pb[2 * ch + 1]]
        cnt = [0]

        def mb():
            cnt[0] += 1
            return mybanks[cnt[0] % 2]

        gcol = ch * 64
        for ti, col0 in enumerate(col0s):
            if save:
                P.dma(cstart_d[ti].v(), Cx.v(), q="sp")
            for sg in range(2):
                rmsnorm_seg(src_v[:, :, col0 + sg * 258:col0 + (sg + 1) * 258], g_mix8, hT[:, :, sg * 258:(sg + 1) * 258], 258)
                yield
            for blk in range(2):
                for sg in range(2):
                    ps = mb()
                    for kc in range(8):
                        P.mm(ps[:, 0:258], wq[:, kc, 256 + blk * 128:256 + (blk + 1) * 128], hT[:, kc, sg * 258:(sg + 1) * 258],
                             start=(kc == 0), stop=(kc == 7))
                    P.copy(zk[:, blk, sg * 258:(sg + 1) * 258], ps[:, 0:258], eng="act")
                yield
                P.ts(accp.v(), zk[:, blk, 0:512], convw[:, blk * 5:blk * 5 + 1], ALU.mult)
                for j in range(1, 5):
                    P.stt(accp.v(), zk[:, blk, j:j + 512], convw[:, blk * 5 + j:blk * 5 + j + 1], accp.v(), ALU.mult, ALU.add)
                P.act(kTp[:, blk, :], accp.v(), AF.Silu, bias=cf("convb_p")[:, blk:blk + 1])
                yield
            for tb in range(4):
                ps = mb()
                for kc in range(8):
                    P.mm(ps.v(), hT[:, kc, 2 + tb * 128:2 + (tb + 1) * 128], wv[:, kc, :], start=(kc == 0), stop=(kc == 7))
                P.copy(vaug128[:, tb, :, 0:128], ps.v().rearrange("p (h e) -> p h e", h=4), eng="act")
                for kc in range(8):
                    P.mm(pb[6][:, gcol + tb * 16:gcol + (tb + 1) * 16], hT[:, kc, 2 + tb * 128:2 + (tb + 1) * 128], wg[:, kc, :],
                         start=(kc == 0), stop=(kc == 7))
                pt = pbf(4 + ch)
                for blk in range(2):
                    P.transpose(pt[:, (tb % 2) * 256 + blk * 128:(tb % 2) * 256 + (blk + 1) * 128], kTp[:, blk, tb * 128:(tb + 1) * 128], identb.v())
                P.copy(ktok128[:, tb, :], pt[:, (tb % 2) * 256:(tb % 2) * 256 + 256])
                yield
            gb = cf("gbias")
            for tb in range(4):
                P.tt(Gp[:, tb, :], pb[6][:, gcol + tb * 16 + gsel * 8:gcol + tb * 16 + gsel * 8 + 8], gb[:, gsel * 8:gsel * 8 + 8], ALU.add)
            P.act(Ep.v(), Gp[:, :, 4:8], AF.Exp, scale=-1.0)
            P.act(Lp.v(), Ep.v(), AF.Ln, bias=1.0)
            yield
            pss_ = pb[7]
            pc = 32 * ch + 320
            sut = cf("sut128")
            onesf = cf("ones")
            for tb in range(4):
                P.mm(pss_[:, pc + tb * 4:pc + (tb + 1) * 4], sut, Lp[:, tb, :], start=True, stop=(tb == 3))
                for t2_ in range(tb + 1, 4):
                    P.mm(pss_[:, pc + tb * 4:pc + (tb + 1) * 4], onesf, Lp[:, t2_, :], start=False, stop=(t2_ == 3))
            for tb in range(4):
                P.mm(pss_[0:64, pc + 16:pc + 20], onesf[:, 0:64], Lp[:, tb, :], start=(tb == 0), stop=(tb == 3))
            P.tt(WSp.v(), Gp[:, :, 0:4], pss_[:, pc:pc + 16].rearrange("p (t h) -> p t h", t=4), ALU.subtract)
            P.act(WSp.v(), WSp.v(), AF.Exp)
            P.act(DECp.v(), pss_[0:64, pc + 16:pc + 20], AF.Exp, scale=-1.0)
            yield
            for tb in range(4):
                P.tt(vws[:, tb], vaug128[:, tb], bc_last(WSp[:, tb, :], [128, 4, 129]), ALU.mult)
            yield
            pdb = mybanks
            for h in range(4):
                for tb in range(4):
                    P.mm(pdb[h // 2][0:64, (h % 2) * 129:(h % 2) * 129 + 129], ktok128[:, tb, h * 64:(h + 1) * 64], vws[:, tb, h, :],
                         start=(tb == 0), stop=(tb == 3))
            C3 = Cx.v().rearrange("p (h e) -> p h e", h=4)
            P.tt(C3, C3, bc_last(DECp.v(), [64, 4, 129]), ALU.mult)
            for bk in range(2):
                P.tt(Cx[:, bk * 258:(bk + 1) * 258], pdb[bk][0:64, 0:258], Cx[:, bk * 258:(bk + 1) * 258], ALU.add)
            yield
        if save:
            P.dma(cstart_d[len(col0s)].v(), Cx.v(), q="sp")

    if do_pre:
        o_, l_ = CHOFF["qk_m"]
        P.dma(ring[1][:, 0:l_], wall_d[:, o_:o_ + l_], q="pool")
        o_, l_ = CHOFF["v_m"]
        P.dma(ring[2][:, 0:l_], wall_d[:, o_:o_ + l_], q="pool")
        o_, l_ = CHOFF["g16"]
        P.dma(ring[3][:, 0:l_], wall_d[:, o_:o_ + l_], q="pool")
    conv_chunks()
    if do_pre:
        wq_p = ring[1][:, 0:4096].rearrange("p (kc n) -> p kc n", kc=8)
        wv_p = ring[2][:, 0:4096].rearrange("p (kc n) -> p kc n", kc=8)
        wg_p = ring[3][:, 0:128].rearrange("p (kc n) -> p kc n", kc=8)
        for B_ in PB:
            P.memset(B_["vaug128"][:, :, :, 128:129], 1.0)
        P.memset(Cst[0].v(), 0.0)
        P.memset(Cst[1].v(), 0.0)
        gens = [gen_pre(0, xp_v, [t * TILE for t in range(NT)], cf("convw_pp"), 1, Cst[1], False, wq_p, wv_p, wg_p),
                gen_pre(1, xo_v, [256 + t * TILE - 2 for t in range(NT - 1)], cf("convw_po"), 0, Cst[0], True, wq_p, wv_p, wg_p)]
        while gens:
            for g_ in list(gens):
                try:
                    next(g_)
                except StopIteration:
                    gens.remove(g_)
        if debug:
            P.dma(dbg["d_c"].v(), Cst[1].v(), q="pool")
    else:
        P.memset(Cst[1].v(), 0.0)
        P.memset(Cst[0].v(), 0.0)
        for t in range(NT):
            P.dma(cstart_d[t].v(), Cst[0].v(), q="pool")
    P.copy(Cb[1].v(), Cst[1].v())
    xq[0] = "pool"
    rotn[0] = 4

    def dump(nm, view_rows, tile_i, src):
        rows = src.ap.shape[0]
        P.copy(dstage[0:rows, :], src)
        P.dma(dbg[nm][view_rows, tile_i * TILE:(tile_i + 1) * TILE], dstage[0:rows, :], q="pool")

    def dmisc(col, src):
        rows, n = src.ap.shape[0], src.ap.shape[1]
        P.copy(dstage[0:rows, 0:n], src)
        P.dma(dbg["d_misc"][0:rows, col:col + n], dstage[0:rows, 0:n], q="pool")

    hT_for = [None]

    def center_load(tile_i):
        c0 = 256 + tile_i * TILE
        return [rms_load(xo_v[:, :, c0 - 2 + sg * 258:c0 - 2 + (sg + 1) * 258], 258) for sg in range(2)]

    def center_norm(tile_i, bufs=None):
        c0 = 256 + tile_i * TILE
        for sg in range(2):
            rmsnorm_seg(xo_v[:, :, c0 - 2 + sg * 258:c0 - 2 + (sg + 1) * 258], g_mix8, hT[:, :, sg * 258:(sg + 1) * 258], 258,
                        xsb=(bufs[sg] if bufs else None))
        hT_for[0] = tile_i

    def main_tile(tile_i):
        c0 = 256 + tile_i * TILE
        hc = hT[:, :, 2:514]
        if hT_for[0] != tile_i:
            center_norm(tile_i)

        Wq = next_w("memq")[:, 0:2048].rearrange("p (kc n) -> p kc n", kc=8)
        Wnq = next_w("na_q", hold=True)[:, 0:2048].rearrange("p (kc n) -> p kc n", kc=8)
        Wnkv = next_w("na_kv", hold=True)[:, 0:4096].rearrange("p (kc n) -> p kc n", kc=8)

        def gen_mem():
            mcnt = [0]

            def mbk():
                mcnt[0] += 1
                return pb[mcnt[0] % 2]

            for h in range(4):
                sqf, rs = sqfs[h % 2], rss[h % 2]
                ps = mbk()
                for kc in range(8):
                    P.mm(ps[0:64, :], Wq[:, kc, h * 64:(h + 1) * 64], hc[:, kc, :], start=(kc == 0), stop=(kc == 7))
                P.act(sqf.v(), ps[0:64, :], AF.Square)
                ps2 = mbk()
                P.mm(ps2[0:64, :], ones64f, sqf.v())
                P.act(rs.v(), ps2[0:64, :], AF.Sqrt, bias=EPS, scale=1.0 / 64)
                P.recip(rs.v(), rs.v())
                P.stt(memq[:, h, :], ps[0:64, :], cf("g_memq", 64), rs.v(), ALU.mult, ALU.mult)
                yield
                for mc in range(2):
                    ps3 = mbk()
                    P.mm(ps3.v(), kmT[:, h, mc * 128:(mc + 1) * 128], memq[:, h, :])
                    P.act(PTm[:, h, mc, :], ps3.v(), AF.Exp, scale=0.125)
                yield
            for tb in range(4):
                hmem_tok, rdm = hmem_toks[tb % 2], rdms[tb % 2]
                pv = pb[6]
                for h in range(4):
                    for mc in range(2):
                        P.mm(pv[:, h * 65:(h + 1) * 65], PTm[:, h, mc, tb * 128:(tb + 1) * 128], vm_aug[:, mc, h, :],
                             start=(mc == 0), stop=(mc == 1))
                pv3 = pv[:, 0:260].rearrange("p (h e) -> p h e", h=4)
                P.recip(rdm.v(), pv3[:, :, 64])
                P.tt(hmem_tok.v(), pv3[:, :, 0:64], bc_last(rdm.v(), [128, 4, 64]), ALU.mult)
                pt = pbf(7)
                for h in range(4):
                    P.transpose(pt[0:64, h * 128:(h + 1) * 128], hmem_tok[:, h, :], identb.v())
                P.copy(hmemT[:, :, tb * 128:(tb + 1) * 128], pt[0:64, 0:512].rearrange("p (h t) -> p h t", h=4), eng="act")
                yield

        gq3 = cf("g_naqk").rearrange("p (h d) -> p h d", h=16)

        def gen_na(par):
            sqn, tmpn, qkn, ssn = sqns[par], tmpns[par], qkns[par], ssns[par]
            for jb in (2 + par, 4 + par, 0 + par, 6 + par):
                own = 2 <= jb <= 5
                if par == 0 and jb == 0:
                    rmsnorm_seg(xo_v[:, :, c0 - 256:c0], g_mix8, hh.v(), 256)
                if par == 0 and jb == 6:
                    rmsnorm_seg(xo_v[:, :, c0 + 512:c0 + 768], g_mix8, hh.v(), 256)
                if own:
                    hsrc = hT[:, :, 2 + (jb - 2) * 128:2 + (jb - 1) * 128]
                else:
                    hsrc = hh[:, :, (jb % 2) * 128:(jb % 2 + 1) * 128]
                pkv = pb[2 + par]
                for kc in range(8):
                    P.mm(pkv.v(), hsrc[:, kc, :], Wnkv[:, kc, :], start=(kc == 0), stop=(kc == 7))
                P.act(sqn[:, 0:8, :], pkv[:, 0:256].rearrange("p (h d) -> p h d", h=8), AF.Square)
                nh = 8
                if own:
                    pq = pb[4][:, par * 256:(par + 1) * 256]
                    for kc in range(8):
                        P.mm(pq, hsrc[:, kc, :], Wnq[:, kc, :], start=(kc == 0), stop=(kc == 7))
                    P.act(sqn[:, 8:16, :], pq.rearrange("p (h d) -> p h d", h=8), AF.Square)
                    nh = 16
                P.reduce(ssn[:, 0:nh], sqn[:, 0:nh, :], ALU.add)
                P.act(ssn[:, 0:nh], ssn[:, 0:nh], AF.Sqrt, bias=EPS, scale=1.0 / 32)
                P.recip(ssn[:, 0:nh], ssn[:, 0:nh])
                yield
                P.tt(tmpn[:, 0:8, :], pkv[:, 0:256].rearrange("p (h d) -> p h d", h=8), bc_last(ssn[:, 0:8], [128, 8, 32]), ALU.mult)
                P.tt(qkn[:, 0:8, :], tmpn[:, 0:8, :], gq3[:, 8:16, :], ALU.mult)
                if own:
                    P.tt(tmpn[:, 8:16, :], pq.rearrange("p (h d) -> p h d", h=8), bc_last(ssn[:, 8:16], [128, 8, 32]), ALU.mult)
                    P.tt(qkn[:, 8:16, :], tmpn[:, 8:16, :], gq3[:, 0:8, :], ALU.mult)
                P.copy(vna[:, jb, :, 0:32], pkv[:, 256:512].rearrange("p (h d) -> p h d", h=8), eng="act")
                yield
                pt = pbf(5 if par == 0 else 7)
                for h in range(8):
                    P.transpose(pt[0:32, h * 128:(h + 1) * 128], qkn[:, h, :], identb.v())
                P.copy(knaT[:, :, jb * 128:(jb + 1) * 128], pt[0:32, 0:1024].rearrange("p (h t) -> p h t", h=8))
                if own:
                    for h in range(8):
                        P.transpose(pt[0:32, h * 128:(h + 1) * 128], qkn[:, 8 + h, :], identb.v())
                    P.copy(qnaT[:, :, (jb - 2) * 128:(jb - 1) * 128], pt[0:32, 0:1024].rearrange("p (h t) -> p h t", h=8), eng="act")
                yield

        P.memset(vna[:, :, :, 32:33], 1.0)
        gens = [gen_mem(), gen_na(0), gen_na(1)]
        while gens:
            for g_ in list(gens):
                try:
                    next(g_)
                except StopIteration:
                    gens.remove(g_)
        if debug:
            for h in range(4):
                dump("d_hmem", slice(h * 64, (h + 1) * 64), tile_i, hmemT[:, h, :])

        for B in range(4):
            edge = (tile_i == 0 and B < 2)
            kbs = [2, 3, 4, 5] if edge else [B, B + 1, B + 2, B + 3, B + 4]
            for i, kb in enumerate(kbs):
                scb = [pb[4], pb[6]] if i % 2 == 0 else [pb[2], pb[3]]
                sbn = sbns[i % 2]
                d0 = 2 * kb - 4 - 2 * B
                if not edge:
                    s0 = 4 - d0
                    assert 0 <= s0 and s0 + 1 < 10, s0
                    for hb in range(2):
                        rb = t2v[:, s0:s0 + 2, hb * 4:(hb + 1) * 4, :].rearrange("p q h c -> p h q c")
                        P.mm(scb[hb].v(), identb.v(), rb, start=True, stop=False)
                        for h4 in range(4):
                            h = hb * 4 + h4
                            P.mm(scb[hb][:, h4 * 128:(h4 + 1) * 128], knaT[:, h, kb * 128:(kb + 1) * 128],
                                 qnaT[:, h, B * 128:(B + 1) * 128], start=False, stop=(h4 == 3))
                        P.act(PTn[:, i, hb * 4:(hb + 1) * 4, :], scb[hb].v().rearrange("p (h q) -> p h q", h=4), AF.Exp,
                              scale=32.0 ** -0.5)
                    continue
                for h in range(8):
                    P.mm(scb[h // 4][:, (h % 4) * 128:(h % 4 + 1) * 128], knaT[:, h, kb * 128:(kb + 1) * 128],
                         qnaT[:, h, B * 128:(B + 1) * 128])
                for qo in range(2):
                    d = d0 - qo
                    slot = 4 - d if d <= 2 else 10 + (d - 3)
                    assert 0 <= slot < 14, (slot, d)
                    for hb in range(2):
                        sc3 = scb[hb].v().rearrange("p (h q) -> p h q", h=4)
                        P.tt(sbn[:, hb * 4:(hb + 1) * 4, qo * 64:(qo + 1) * 64], sc3[:, :, qo * 64:(qo + 1) * 64],
                             t2v[:, slot, hb * 4:(hb + 1) * 4, :], ALU.add)
                P.act(PTn[:, i], sbn.v(), AF.Exp, scale=32.0 ** -0.5)
            pv = pb[7]
            for h in range(8):
                for i, kb in enumerate(kbs):
                    P.mm(pv[:, h * 33:(h + 1) * 33], PTn[:, i, h, :], vna[:, kb, h, :], start=(i == 0), stop=(i == len(kbs) - 1))
            pv3 = pv[:, 0:264].rearrange("p (h e) -> p h e", h=8)
            if debug and tile_i == NT - 1 and B == 0:
                dmisc(0, qnaT[:, 0, :])
                dmisc(512, knaT[:, 0, 0:512])
                dmisc(1024, knaT[:, 0, 512:1024])
                dmisc(2048, pv[:, 0:264])
                dmisc(2312, vna[:, 2].rearrange("p h e -> p (h e)"))
            P.recip(rdn.v(), pv3[:, :, 32])
            P.tt(hna_tok.v(), pv3[:, :, 0:32], bc_last(rdn.v(), [128, 8, 32]), ALU.mult)
            pt = pbf(5)
            hna2 = hna_tok.v().rearrange("p h d -> p (h d)")
            for k2 in range(2):
                P.transpose(pt[:, k2 * 128:(k2 + 1) * 128], hna2[:, k2 * 128:(k2 + 1) * 128], identb.v())
            P.copy(hnaT[:, :, B * 128:(B + 1) * 128], pt[:, 0:256].rearrange("p (k t) -> p k t", k=2))
        if debug:
            for k2 in range(2):
                dump("d_hna", slice(k2 * 128, (k2 + 1) * 128), tile_i, hnaT[:, k2, :])

        Wqk = next_w("qk_m")[:, 0:4096].rearrange("p (kc n) -> p kc n", kc=8)
        cwm = cf("convw_m", 64)
        for g in range(8):
            zq_, accq_ = zqs[g % 2], accqs[g % 2]
            for sg in range(2):
                ps = nb()
                for kc in range(8):
                    P.mm(ps[0:64, 0:258], Wqk[:, kc, g * 64:(g + 1) * 64], hT[:, kc, sg * 258:(sg + 1) * 258],
                         start=(kc == 0), stop=(kc == 7))
                P.copy(zq_[:, sg * 258:(sg + 1) * 258], ps[0:64, 0:258], eng="act")
            P.ts(accq_.v(), zq_[:, 0:512], cwm[:, g * 5:g * 5 + 1], ALU.mult)
            for j in range(1, 5):
                P.stt(accq_.v(), zq_[:, j:j + 512], cwm[:, g * 5 + j:g * 5 + j + 1], accq_.v(), ALU.mult, ALU.add)
            P.act(qkT[:, g, :], accq_.v(), AF.Silu, bias=cf("convb_m", 64)[:, g:g + 1])
        Wv = next_w("v_m")[:, 0:4096].rearrange("p (kc n) -> p kc n", kc=8)
        P.memset(vaug[:, :, :, 128:129], 1.0)
        for c in range(NCH):
            ps = nb()
            for kc in range(8):
                P.mm(ps[0:64, :], hT[:, kc, 2 + c * 64:2 + (c + 1) * 64], Wv[:, kc, :], start=(kc == 0), stop=(kc == 7))
            P.copy(vaug[:, c, :, 0:128], ps[0:64, :].rearrange("p (h e) -> p h e", h=4), eng="act")
        Wo = next_w("o_m")[:, 0:4096].rearrange("p (kc n) -> p kc n", kc=8)
        for c in range(NCH):
            ps = nb()
            for kc in range(8):
                P.mm(ps[0:64, :], hT[:, kc, 2 + c * 64:2 + (c + 1) * 64], Wo[:, kc, :], start=(kc == 0), stop=(kc == 7))
            P.act(osig[:, c, :], ps[0:64, :], AF.Sigmoid)
        Wg = next_w("g16")[:, 0:128].rearrange("p (kc n) -> p kc n", kc=8)
        for c in range(NCH):
            for kc in range(8):
                P.mm(pb[6][0:64, c * 16:(c + 1) * 16], hT[:, kc, 2 + c * 64:2 + (c + 1) * 64], Wg[:, kc, :], start=(kc == 0), stop=(kc == 7))
        gb = cf("gbias", 64)
        for c in range(NCH):
            P.tt(Gt[:, c, :], pb[6][0:64, c * 16:(c + 1) * 16], gb, ALU.add)
        gate_math(2)
        for c in range(NCH):
            pt = pbf(4)
            for h in range(4):
                P.transpose(pt[0:64, h * 64:(h + 1) * 64], qkT[:, 4 + h, c * 64:(c + 1) * 64], identb[0:64, 0:64])
            P.copy(ktok[:, c, :], pt[0:64, 0:256])
        P.dma(Cst[0].v(), cstart_d[tile_i].v(), q="pool")
        P.copy(Cb[0].v(), Cst[0].v(), eng="act")

        vu2 = zqs[1].v().bitcast(BF16)

        def vubuf(x, par):
            if par == 0:
                return vus[x].v()
            return vu2[:, x * 516:(x + 1) * 516].rearrange("p (h e) -> p h e", h=4)

        def make_vu(c, x, par):
            P.tt(vubuf(x, par), vaug[:, c], bc_last(Ut[:, c, x * 4:(x + 1) * 4], [64, 4, 129]), ALU.mult, eng="pool")

        def st_mm(c, x):
            cs = slice(c * 64, (c + 1) * 64)
            st = pb[0] if x == 0 else pb[3]
            for h in range(4):
                P.mm(st[0:64, h * 64:(h + 1) * 64], qkT[:, 4 + h, cs], qkT[:, h, cs])

        def pm_op(c, x):
            st = pb[0] if x == 0 else pb[3]
            P.tt(PMs[x].v(), st[0:64, 0:256], cf("mask4A" if x == 0 else "mask4B", 64), ALU.mult)

        def front_a(c, x):
            st_mm(c, x)
            pm_op(c, x)

        def front_b(c, x, par):
            PM, vu = PMs[x], vubuf(x, par)
            cs = slice(c * 64, (c + 1) * 64)
            pob = [pb[1], pb[2]] if x == 0 else [pb[4], pb[6]]
            for h in range(4):
                po_h = pob[h // 2][0:64, (h % 2) * 129:(h % 2) * 129 + 129]
                P.mm(po_h, PM[:, h * 64:(h + 1) * 64], vu[:, h, :], start=True, stop=False)
                P.mm(po_h, qkT[:, h, cs], Cb[x][:, h * 129:(h + 1) * 129], start=False, stop=True)

        def norm_abs(c, x):
            t1 = t1s[x]
            pob = [pb[1], pb[2]] if x == 0 else [pb[4], pb[6]]
            for bk in range(2):
                po3 = pob[bk][0:64, 0:258].rearrange("p (h e) -> p h e", h=2)
                P.act(t1[:, bk * 2:(bk + 1) * 2].rearrange("p (h o) -> p h o", o=1), po3[:, :, 128:129], AF.Abs)

        def norm_dve(c, x):
            t1, rr = t1s[x], rrs[x]
            P.tt(t1.v(), t1.v(), IWt[:, c, x * 4:(x + 1) * 4], ALU.max)
            P.recip(rr.v(), t1.v())

        def norm_copy(c, x, first):
            rr, tmph = rrs[x], tmphs[x]
            pob = [pb[1], pb[2]] if x == 0 else [pb[4], pb[6]]
            h4 = hacc[:, c, :].rearrange("p (h e) -> p h e", h=4)
            for bk in range(2):
                for h2 in range(2):
                    h = bk * 2 + h2
                    src = pob[bk][0:64, h2 * 129:h2 * 129 + 128]
                    if first:
                        P.act(h4[:, h, :], src, AF.Copy, scale=rr[:, h:h + 1])
                    else:
                        P.act(tmph[:, h2, :], src, AF.Copy, scale=rr[:, h:h + 1])
                if not first:
                    P.tt(h4[:, bk * 2:(bk + 1) * 2, :], h4[:, bk * 2:(bk + 1) * 2, :], tmph.v(), ALU.add, eng="pool")

        def state_mm(c, x, par):
            vu = vubuf(x, par)
            pdv = pb[5] if x == 0 else pb[7]
            pdn = (pb[0] if x == 0 else pb[3])
            for h in range(4):
                P.mm(pdv[0:64, h * 128:(h + 1) * 128], ktok[:, c, h * 64:(h + 1) * 64], vu[:, h, 0:128])
            for h in range(4):
                P.mm(pdn[0:64, 256 + h:257 + h], ktok[:, c, h * 64:(h + 1) * 64], vu[:, h, 128:129])

        def state_dve(c, x):
            pdv = pb[5] if x == 0 else pb[7]
            pdn = (pb[0] if x == 0 else pb[3])
            C3 = Cst[x].v().rearrange("p (h e) -> p h e", h=4)
            P.tt(C3[:, :, 0:128], pdv[0:64, :].rearrange("p (h e) -> p h e", h=4), C3[:, :, 0:128], ALU.add)
            P.tt(C3[:, :, 128:129], pdn[0:64, 256:260].rearrange("p (h o) -> p h o", o=1), C3[:, :, 128:129], ALU.add)
            decb = bc_last(DECt[:, c, x * 4:(x + 1) * 4], [64, 4, 129])
            P.tt(Cb[x].v().rearrange("p (h e) -> p h e", h=4), C3, decb, ALU.mult)
            P.tt(C3, C3, decb, ALU.mult, eng="pool")

        def finalize(c, k):
            sqh_, hn_ = zqs[k % 2][:, 0:512], accqs[k % 2].v()
            hmt = hm_tok.v() if k % 2 == 0 else vus[0].v().rearrange("p h e -> p (h e)")[:, 0:512]
            ssh_ = t1s[k % 2]
            P.tt(sqh_, hacc[:, c, :], hacc[:, c, :], ALU.mult, eng="pool")
            P.reduce(ssh_.v(), sqh_.rearrange("p (h e) -> p h e", h=4), ALU.add)
            P.act(ssh_.v(), ssh_.v(), AF.Sqrt, bias=EPS, scale=1.0 / 128)
            P.recip(ssh_.v(), ssh_.v())
            P.tt(hn_.rearrange("p (h e) -> p h e", h=4), hacc[:, c, :].rearrange("p (h e) -> p h e", h=4),
                 bc_last(ssh_.v(), [64, 4, 128]), ALU.mult)
            P.tt(hn_, hn_, cf("g_mlstm", 64), ALU.mult)
            P.tt(hmt, hn_, osig[:, c, :], ALU.mult, eng="pool")
            pt = pbf(0 if k % 2 == 0 else 3)
            for k4 in range(4):
                P.transpose(pt[:, 768 + k4 * 64:768 + (k4 + 1) * 64], hmt[:, k4 * 128:(k4 + 1) * 128], identb[0:64, 0:64])
            P.copy(hmT[:, :, c * 64:(c + 1) * 64], pt[:, 768:1024].rearrange("p (k t) -> p k t", k=4), eng="act")

        make_vu(0, 0, 0)
        make_vu(NCH - 1, 1, 0)
        front_a(0, 0)
        front_a(NCH - 1, 1)
        for i_ in range(NCH):
            ca, cb = i_, NCH - 1 - i_
            par = i_ % 2
            first = i_ < NCH // 2
            if i_ + 1 < NCH:
                make_vu(ca + 1, 0, 1 - par)
                make_vu(cb - 1, 1, 1 - par)
            front_b(ca, 0, par)
            front_b(cb, 1, par)
            state_mm(ca, 0, par)
            state_mm(cb, 1, par)
            if i_ + 1 < NCH:
                st_mm(ca + 1, 0)
                st_mm(cb - 1, 1)
            norm_abs(ca, 0)
            norm_abs(cb, 1)
            norm_dve(ca, 0)
            norm_dve(cb, 1)
            if i_ + 1 < NCH:
                pm_op(ca + 1, 0)
                pm_op(cb - 1, 1)
            norm_copy(ca, 0, first)
            norm_copy(cb, 1, first)
            state_dve(ca, 0)
            state_dve(cb, 1)
        for k_, c_ in enumerate(range(NCH)):
            finalize(c_, k_)
        if debug:
            for k4 in range(4):
                dump("d_hm", slice(k4 * 128, (k4 + 1) * 128), tile_i, hmT[:, k4, :])

        for j in range(8):
            P.dma(x1Ts[j].v(), xo_v[:, j, c0:c0 + TILE], q="pool")

        for j in range(8):
            W = next_w("mrg%d" % j)
            Wgt = W[:, 0:3072].rearrange("p (kc n) -> p kc n", kc=8)
            Wpm = W[:, 3072:3584].rearrange("p (kc n) -> p kc n", kc=4)
            Wpn = W[:, 3584:3840].rearrange("p (kc n) -> p kc n", kc=2)
            Wpe = W[0:64, 3840:4352].rearrange("p (kc n) -> p kc n", kc=4)
            yacc = yaccs[j % 2]
            for b in range(3):
                gs = gss[(j * 3 + b) % 2]
                tmpy = tmpys[b % 2]
                pg = nb()
                for kc in range(8):
                    P.mm(pg.v(), Wgt[:, kc, b * 128:(b + 1) * 128], hc[:, kc, :], start=(kc == 0), stop=(kc == 7))
                P.act(gs.v(), pg.v(), AF.Sigmoid)
                pp = nb()
                if b == 0:
                    for k4 in range(4):
                        P.mm(pp.v(), Wpm[:, k4, :], hmT[:, k4, :], start=(k4 == 0), stop=(k4 == 3))
                    P.tt(yacc.v(), pp.v(), gs.v(), ALU.mult)
                elif b == 1:
                    for k2 in range(2):
                        P.mm(pp.v(), Wpn[:, k2, :], hnaT[:, k2, :], start=(k2 == 0), stop=(k2 == 1))
                    P.tt(tmpy.v(), pp.v(), gs.v(), ALU.mult)
                    P.tt(yacc.v(), yacc.v(), tmpy.v(), ALU.add)
                else:
                    for h in range(4):
                        P.mm(pp.v(), Wpe[:, h, :], hmemT[:, h, :], start=(h == 0), stop=(h == 3))
                    P.tt(tmpy.v(), pp.v(), gs.v(), ALU.mult)
                    P.tt(yTs[j].v(), yacc.v(), tmpy.v(), ALU.add)
            if debug:
                dump("d_y", slice(j * 128, (j + 1) * 128), tile_i, yTs[j].v())

        for j in range(8):
            if j % 4 == 0:
                Wout = next_w("out%d" % (j // 4))[:, 0:4096].rearrange("p (kc n) -> p kc n", kc=8)
            ps = nb()
            for kc in range(8):
                P.mm(ps.v(), Wout[:, kc, (j % 4) * 128:(j % 4 + 1) * 128], yTs[kc].v(), start=(kc == 0), stop=(kc == 7))
            P.tt(x1Ts[j].v(), x1Ts[j].v(), ps.v(), ALU.add)
            P.act(sqxs[j % 2].v(), x1Ts[j].v(), AF.Square)
            if j >= 1:
                P.mm(pb[7].v(), onesb.v(), sqxs[(j - 1) % 2].v(), start=(j == 1), stop=False)
            if debug:
                dump("d_x1", slice(j * 128, (j + 1) * 128), tile_i, x1Ts[j].v())
        P.mm(pb[7].v(), onesb.v(), sqxs[1].v(), start=False, stop=True)
        P.act(rstd.v(), pb[7].v(), AF.Sqrt, bias=EPS, scale=1.0 / D)
        P.recip(rstd.v(), rstd.v())
        for j in range(8):
            P.stt(hnTs[j].v(), x1Ts[j].v(), g_ffn8[:, j:j + 1], rstd.v(), ALU.mult, ALU.mult)

        nxt = (tile_i - 1 >= NT - n_main_tiles)
        if nxt:
            nbufs = center_load(tile_i - 1)

        for i in range(8):
            Wup = next_w("up%d" % i)[:, 0:4096].rearrange("p (kc n) -> p kc n", kc=8)
            for f4 in range(4):
                f = i * 4 + f4
                ps = nb()
                for kc in range(8):
                    P.mm(ps.v(), Wup[:, kc, f4 * 128:(f4 + 1) * 128], hnTs[kc].v(), start=(kc == 0), stop=(kc == 7))
                P.act(rls[f % 2].v(), ps.v(), AF.Relu)
                P.tt(aTs[i][:, f4, :], rls[f % 2].v(), rls[f % 2].v(), ALU.mult, eng="pool")
        if nxt:
            center_norm(tile_i - 1, nbufs)
        last = []
        for j in range(8):
            Wdn = next_w("dn%d" % j)[:, 0:4096].rearrange("p (f n) -> p f n", f=32)
            ps = nb()
            for f in range(32):
                P.mm(ps.v(), Wdn[:, f, :], aTs[f // 4][:, f % 4, :], start=(f == 0), stop=(f == 31))
            o_ = ost[j % 2]
            P.tt(o_.v(), ps.v(), x1Ts[j].v(), ALU.add)
            last.append(P.dma(outT_d[j * 128:(j + 1) * 128, tile_i * TILE:(tile_i + 1) * TILE], o_.v(), q="pool"))
        return last

    finals = []
    for tile_i in range(NT - 1, NT - 1 - n_main_tiles, -1):
        finals = main_tile(tile_i)
    fin = list(finals[-2:])
    for e in ("pool",):
        dmas = [op for op in P.ops[e] if op.is_dma]
        if dmas:
            fin.append(dmas[-1])
    for op in fin:
        op.needed = True
    P.emit(final_wait_ops=fin)
    P.close()
    return nc, P


_CACHE = {}


def kernel(**inputs):
    maps = _host_prep(inputs)
    if "nc" not in _CACHE:
        _CACHE["nc"] = build_program()[0]
    nc = _CACHE["nc"]
    res = run_bass_kernel_spmd(nc, maps, core_ids=list(range(8)))
    return _assemble(res.results)
```

```python
import math
from contextlib import ExitStack
import numpy as np
import concourse.bass as bass
import concourse.mybir as mybir
from concourse.bass_utils import run_bass_kernel_spmd

F32 = mybir.dt.float32
BF16 = mybir.dt.bfloat16
AF = mybir.ActivationFunctionType
ALU = mybir.AluOpType
AX = mybir.AxisListType

ENGS = ["pe", "act", "dve", "pool", "sp"]
SAME_DIST = 3
SEM_LIMIT = 30000


class V:
    __slots__ = ("t", "ap")

    def __init__(self, t, ap):
        self.t = t
        self.ap = ap

    def __getitem__(self, k):
        return V(self.t, self.ap[k])

    def rearrange(self, pattern_, **kw):
        return V(self.t, self.ap.rearrange(pattern_, **kw))

    def bc(self, shape):
        return V(self.t, self.ap.to_broadcast(list(shape)))

    def bitcast(self, dt):
        return V(self.t, self.ap.bitcast(dt))


class TT:
    def __init__(self, name, ap, lo=0, hi=0, region=None):
        self.name = name
        self.ap = ap
        self.w = None
        self.r = {}
        self.dsem = None
        self.dcount = 0
        self.overlaps = []
        self.lo, self.hi = lo, hi
        if region is not None:
            for o in region:
                if o.lo < hi and lo < o.hi:
                    o.overlaps.append(self)
                    self.overlaps.append(o)
            region.append(self)

    def __getitem__(self, k):
        return V(self, self.ap[k])

    def v(self):
        return V(self, self.ap)


class Op:
    __slots__ = ("eng", "idx", "fn", "waits", "needed", "is_dma", "semkey", "semval", "cnt")

    def __init__(self, eng, idx, fn, is_dma):
        self.eng = eng
        self.idx = idx
        self.fn = fn
        self.waits = []
        self.needed = False
        self.is_dma = is_dma
        self.semkey = None
        self.semval = None
        self.cnt = None


class Prog:
    def __init__(self, nc):
        self.nc = nc
        self.ops = {e: [] for e in ENGS}
        self.waited = {e: {} for e in ENGS}
        self.next_dsem = 0
        self.stack = ExitStack()

    def sbuf(self, name, shape, dt):
        h = self.stack.enter_context(self.nc.sbuf_tensor("sb_" + name, list(shape), dt))
        return TT(name, h[:])

    def psum(self, name, shape, dt):
        h = self.stack.enter_context(self.nc.psum_tensor("ps_" + name, list(shape), dt))
        return TT(name, h[:])

    def dram(self, name, ap):
        return TT("@" + name, ap)

    def _add(self, eng, fn, reads, writes, dma_tile=None):
        ops = self.ops[eng]
        op = Op(eng, len(ops), fn, dma_tile is not None)
        rt = [x.t for x in reads]
        wt = [x.t for x in writes]
        cand = []
        for t in rt:
            if t.w is not None:
                cand.append(t.w)
            for o in t.overlaps:
                if o.w is not None:
                    cand.append(o.w)
        for t in wt:
            if t.w is not None:
                cand.append(t.w)
            cand.extend(t.r.values())
            for o in t.overlaps:
                if o.w is not None:
                    cand.append(o.w)
                cand.extend(o.r.values())
        wd = self.waited[eng]
        deps = {}
        for d in cand:
            if d.is_dma:
                key, val = d.semkey, d.semval
            else:
                key, val = d.eng, d.idx
                if d.eng == eng and not op.is_dma and eng != "pool" and (eng == "pe" or op.idx - d.idx > SAME_DIST):
                    continue
            if wd.get(key, -1) >= val:
                continue
            if key not in deps or deps[key][1] < val:
                deps[key] = (d, val)
        for key, (d, val) in deps.items():
            wd[key] = val
            d.needed = True
            op.waits.append(d)
        if dma_tile is not None:
            if dma_tile.dsem is None:
                dma_tile.dsem = {}
                dma_tile.dcount = {}
            if eng not in dma_tile.dsem:
                dma_tile.dsem[eng] = self.next_dsem
                dma_tile.dcount[eng] = 0
                self.next_dsem += 1
            dma_tile.dcount[eng] += 1
            op.semkey = ("d", dma_tile.dsem[eng])
            op.semval = 16 * dma_tile.dcount[eng]
            rkey = op.semkey
        else:
            rkey = eng
        for t in rt:
            t.r[rkey] = op
        for t in wt:
            t.w = op
            t.r = {}
        ops.append(op)
        return op

    def mm(self, out, lhsT, rhs, start=True, stop=True):
        self._add("pe", lambda e: e.matmul(out.ap, lhsT.ap, rhs.ap, start=start, stop=stop),
                  [lhsT, rhs] + ([] if start else [out]), [out])

    def transpose(self, out, in_, ident):
        self._add("pe", lambda e: e.transpose(out.ap, in_.ap, ident.ap), [in_, ident], [out])

    def act(self, out, in_, func, bias=None, scale=None, accum_out=None):
        reads = [in_]
        kw = {}
        if bias is not None:
            if isinstance(bias, V):
                reads.append(bias)
                kw["bias"] = bias.ap
            else:
                kw["bias"] = bias
        if scale is not None:
            if isinstance(scale, V):
                reads.append(scale)
                kw["scale"] = scale.ap
            else:
                kw["scale"] = scale
        writes = [out]
        if accum_out is not None:
            writes.append(accum_out)
            kw["accum_out"] = accum_out.ap
        self._add("act", lambda e: e.activation(out.ap, in_.ap, func, **kw), reads, writes)

    def tt(self, out, in0, in1, op, eng="dve"):
        self._add(eng, lambda e: e.tensor_tensor(out.ap, in0.ap, in1.ap, op), [in0, in1], [out])

    def ts(self, out, in0, s1, op0, s2=None, op1=None, eng="dve"):
        reads = [in0]
        a1 = s1
        if isinstance(s1, V):
            reads.append(s1)
            a1 = s1.ap
        a2 = s2
        if isinstance(s2, V):
            reads.append(s2)
            a2 = s2.ap
        if op1 is None:
            self._add(eng, lambda e: e.tensor_scalar(out.ap, in0.ap, a1, None, op0), reads, [out])
        else:
            self._add(eng, lambda e: e.tensor_scalar(out.ap, in0.ap, a1, a2, op0, op1), reads, [out])

    def stt(self, out, in0, scalar, in1, op0, op1, eng="dve"):
        reads = [in0, in1]
        a = scalar
        if isinstance(scalar, V):
            reads.append(scalar)
            a = scalar.ap
        self._add(eng, lambda e: e.scalar_tensor_tensor(out.ap, in0.ap, a, in1.ap, op0, op1), reads, [out])

    def copy(self, out, in_, eng="dve"):
        if eng == "act":
            self._add(eng, lambda e: e.copy(out.ap, in_.ap), [in_], [out])
        else:
            self._add(eng, lambda e: e.tensor_copy(out.ap, in_.ap), [in_], [out])

    def memset(self, out, val, eng="dve"):
        self._add(eng, lambda e: e.memset(out.ap, val), [], [out])

    def reduce(self, out, in_, op, axis=AX.X, eng="dve"):
        self._add(eng, lambda e: e.tensor_reduce(out.ap, in_.ap, axis, op), [in_], [out])

    def recip(self, out, in_):
        self._add("dve", lambda e: e.reciprocal(out.ap, in_.ap), [in_], [out])

    def dma(self, out, in_, q="sp", sem_tile=None):
        if sem_tile is None:
            sem_tile = out.t if out.t.name[0] != "@" else in_.t
        op = self._add(q, None, [in_], [out], dma_tile=sem_tile)
        op.fn = (out.ap, in_.ap)
        return op

    def emit(self, final_wait_ops=()):
        nc = self.nc
        st = self.stack
        esems = {}
        for e in ENGS:
            c = 0
            for op in self.ops[e]:
                if not op.is_dma and op.needed:
                    c += 1
                    op.cnt = c
            esems[e] = [st.enter_context(nc.semaphore("s_%s_%d" % (e, i))) for i in range(c // SEM_LIMIT + 1)]
        dsems = [st.enter_context(nc.semaphore("d_%d" % i)) for i in range(self.next_dsem)]

        def semof(d):
            if d.is_dma:
                return dsems[d.semkey[1]], d.semval
            k = (d.cnt - 1) // SEM_LIMIT
            return esems[d.eng][k], d.cnt - k * SEM_LIMIT

        block = st.enter_context(nc.Block())
        engattr = {"pe": "tensor", "act": "scalar", "dve": "vector", "pool": "gpsimd", "sp": "sync"}
        prog = self

        def make(ename):
            def body(eng):
                for op in prog.ops[ename]:
                    for d in op.waits:
                        s, v = semof(d)
                        eng.wait_ge(s, v)
                    if op.is_dma:
                        o, i = op.fn
                        eng.dma_start(out=o, in_=i).then_inc(dsems[op.semkey[1]], 16)
                    else:
                        ins = op.fn(eng)
                        if op.needed:
                            s, v = semof(op)
                            ins.then_inc(s, 1)
                if ename == "sp":
                    for d in final_wait_ops:
                        s, v = semof(d)
                        eng.wait_ge(s, v)
            return body

        for e in ENGS:
            getattr(block, engattr[e])(make(e))

    def close(self):
        self.stack.close()


D = 1024
S = 8192
NB = 4
HALF = 4096
TILE = 512
NT = HALF // TILE
NCH = TILE // 64
DFF = 4096
EPS = 1e-6
NEG = -80.0
LN8 = math.log(0.125)

O_QK, O_V, O_O, O_I, O_F, O_NA, O_MQ, O_G = 0, 512, 1024, 1536, 1544, 1552, 2320, 2576
XO_W = 256 + HALF + 256
XP_W = HALF + 4

SLOT = 4352
NSLOT = 4


def _chunks():
    ch = [("memq", 8 * 256), ("na_q", 8 * 256), ("na_kv", 8 * 512), ("qk_m", 8 * 512), ("v_m", 8 * 512),
          ("o_m", 8 * 512), ("g16", 8 * 16)]
    ch += [("mrg%d" % j, SLOT) for j in range(8)]
    ch += [("out%d" % i, 8 * 512) for i in range(2)]
    ch += [("up%d" % i, 8 * 512) for i in range(8)]
    ch += [("dn%d" % j, 32 * 128) for j in range(8)]
    off = {}
    o = 0
    for n, l in ch:
        off[n] = (o, l)
        o += l
    return ch, off, o


CHUNKS, CHOFF, WTOT = _chunks()

CST_FIELDS = [("g_mix8", 8), ("g_ffn8", 8), ("g_mem8", 8), ("convw_m", 40), ("convb_m", 8),
              ("convw_po", 10), ("convw_pp", 10), ("convb_p", 2), ("gbias", 16), ("g_mlstm", 512),
              ("g_naqk", 512), ("g_memq", 1), ("g_memk", 1), ("ident", 128), ("ones", 128),
              ("triA", 64), ("triB", 64), ("mask4A", 256), ("mask4B", 256), ("sut128", 128)]
CSTOFF = {}
_o = 0
for _n, _l in CST_FIELDS:
    CSTOFF[_n] = (_o, _l)
    _o += _l
NCST = _o
NT2 = 14 * 8 * 64


def _kmaj(W):
    K, N = W.shape
    kc = K // 128
    return np.ascontiguousarray(W.reshape(kc, 128, N).transpose(1, 0, 2)).reshape(128, kc * N)


def _host_prep(inp):
    f32 = np.float32
    w_in = np.asarray(inp["w_in"][0], f32)
    maps = []
    wall0 = np.zeros((128, WTOT), f32)

    def put(name, arr):
        o, l = CHOFF[name]
        assert arr.shape == (128, l), (name, arr.shape, l)
        wall0[:, o:o + l] = arr

    put("memq", _kmaj(w_in[:, O_MQ:O_MQ + 256]))
    put("na_q", _kmaj(w_in[:, O_NA:O_NA + 256]))
    put("na_kv", _kmaj(w_in[:, O_NA + 256:O_NA + 768]))
    put("qk_m", _kmaj(w_in[:, O_QK:O_QK + 512]))
    put("v_m", _kmaj(w_in[:, O_V:O_V + 512]))
    put("o_m", _kmaj(w_in[:, O_O:O_O + 512]))
    wpm = np.asarray(inp["w_proj_mlstm"][0], f32)
    wpn = np.asarray(inp["w_proj_na"][0], f32)
    wpe = np.asarray(inp["w_proj_mem"][0], f32)
    for j in range(8):
        cols = np.concatenate([O_G + b * 1024 + j * 128 + np.arange(128) for b in range(3)])
        a = _kmaj(w_in[:, cols])
        b_ = _kmaj(wpm[:, j * 128:(j + 1) * 128])
        c_ = _kmaj(wpn[:, j * 128:(j + 1) * 128])
        d_ = np.zeros((128, 4 * 128), f32)
        d_[0:64] = wpe[:, j * 128:(j + 1) * 128].reshape(4, 64, 128).transpose(1, 0, 2).reshape(64, 512)
        put("mrg%d" % j, np.concatenate([a, b_, c_, d_], axis=1))
    w_out = np.asarray(inp["w_out"][0], f32)
    for i in range(2):
        put("out%d" % i, _kmaj(w_out[:, i * 512:(i + 1) * 512]))
    w_up = np.asarray(inp["w_up"][0], f32)
    for i in range(8):
        put("up%d" % i, _kmaj(w_up[:, i * 512:(i + 1) * 512]))
    w_dn = np.asarray(inp["w_down"][0], f32)
    for j in range(8):
        put("dn%d" % j, _kmaj(w_dn[:, j * 128:(j + 1) * 128]))

    conv_w = np.asarray(inp["conv_w"][0], f32)
    conv_b = np.asarray(inp["conv_b"][0], f32)
    b_i = np.asarray(inp["b_igate"][0], f32)
    b_f = np.asarray(inp["b_fgate"][0], f32)
    rpb = np.asarray(inp["rpb"][0], f32)
    ar = np.arange(64)
    tri = (ar[:, None] <= ar[None, :]).astype(f32)

    walls, csts, t2s = [], [], []
    for half in range(2):
        dA, dB = half, 1 - half
        wall = wall0.copy()
        gcols = np.concatenate([O_I + dA * 4 + np.arange(4), O_F + dA * 4 + np.arange(4),
                                O_I + dB * 4 + np.arange(4), O_F + dB * 4 + np.arange(4)])
        o, l = CHOFF["g16"]
        wall[:, o:o + l] = _kmaj(w_in[:, gcols])
        walls.append(wall)

        cst = np.zeros((128, NCST), f32)

        def cput(name, arr, rows=128):
            o, l = CSTOFF[name]
            cst[0:rows, o:o + l] = arr

        cput("g_mix8", np.asarray(inp["g_mix"][0], f32).reshape(8, 128).T)
        cput("g_ffn8", np.asarray(inp["g_ffn"][0], f32).reshape(8, 128).T)
        cput("g_mem8", np.asarray(inp["g_mem"][0], f32).reshape(8, 128).T)
        cw_own = conv_w if half == 0 else conv_w[::-1]
        cw_par = cw_own[::-1]
        cput("convw_m", cw_own.T.reshape(8, 64, 5).transpose(1, 0, 2).reshape(64, 40), rows=64)
        cput("convb_m", conv_b.reshape(8, 64).T, rows=64)
        cput("convw_po", cw_own.T[256:512].reshape(2, 128, 5).transpose(1, 0, 2).reshape(128, 10))
        cput("convw_pp", cw_par.T[256:512].reshape(2, 128, 5).transpose(1, 0, 2).reshape(128, 10))
        cput("convb_p", conv_b[256:512].reshape(2, 128).T)
        gb = np.concatenate([b_i[dA], b_f[dA], b_i[dB], b_f[dB]])
        cput("gbias", np.tile(gb[None, :], (128, 1)))
        cput("g_mlstm", np.tile(np.asarray(inp["g_mlstm"][0], f32)[None, :], (128, 1)))
        gq = np.asarray(inp["g_na_q"][0], f32)
        gk = np.asarray(inp["g_na_k"][0], f32)
        cput("g_naqk", np.tile(np.concatenate([np.tile(gq, 8), np.tile(gk, 8)])[None, :], (128, 1)))
        cput("g_memq", np.asarray(inp["g_mem_q"][0], f32)[:, None], rows=64)
        cput("g_memk", np.asarray(inp["g_mem_k"][0], f32)[:, None], rows=64)
        cput("ident", np.eye(128, dtype=f32))
        cput("ones", np.ones((128, 128), f32))
        cput("triA", tri, rows=64)
        cput("triB", tri.T, rows=64)
        cput("mask4A", np.tile(tri, (1, 4)), rows=64)
        cput("mask4B", np.tile(tri.T, (1, 4)), rows=64)
        a128 = np.arange(128)
        cput("sut128", (a128[:, None] > a128[None, :]).astype(f32))
        csts.append(cst)

        t2 = np.full((128, 14, 8, 64), NEG, f32)
        kj = np.arange(64)[:, None]
        cq = np.arange(64)[None, :]
        if half == 0:
            cg, kg = cq, kj
        else:
            cg, kg = 63 - cq, 63 - kj
        cs = np.clip(cg - 8, 0, 48)
        colok = (kg >= cs) & (kg <= cs + 15)
        dcg = kg - cg
        for slot in range(14):
            if slot < 10:
                d = 4 - slot
                edge = False
            else:
                d = slot - 10 + 3
                edge = True
            for ko in range(2):
                dr = d + ko
                drg = dr if half == 0 else -dr
                if edge:
                    ok = abs(drg) <= 7
                else:
                    ok = (-4 <= drg <= 3)
                if not ok:
                    continue
                idx = np.clip(dcg + 15, 0, 30)
                for h in range(8):
                    vals = rpb[h, drg + 7][idx]
                    t2[ko * 64:(ko + 1) * 64, slot, h, :] = np.where(colok, vals, NEG)
        t2s.append(t2.reshape(128, NT2))

    x = np.asarray(inp["x"], f32)
    mem = np.asarray(inp["mem"], f32)
    for b in range(NB):
        memT = np.ascontiguousarray(mem[b].T)
        for half in range(2):
            seq = x[b] if half == 0 else x[b][::-1]
            xo = np.zeros((D, XO_W), f32)
            xo[:, 256:] = seq[0:HALF + 256].T
            xp = np.zeros((D, XP_W), f32)
            xp[:, 2:] = seq[::-1][0:HALF + 2].T
            maps.append({"xo": xo, "xp": xp, "memT": memT, "wall": walls[half], "cst": csts[half],
                         "t2": t2s[half], "w_mem_kv": np.asarray(inp["w_mem_kv"][0], f32)})
    return maps


def _assemble(results):
    out = np.zeros((NB, S, D), np.float32)
    for b in range(NB):
        for half in range(2):
            o = results[2 * b + half]["outT"]
            loc = o.T
            if half == 0:
                out[b, 0:HALF] = loc
            else:
                out[b, HALF:] = loc[::-1]
    return out


DEBUG = False
RBYTES = 63488


def build_program(n_main_tiles=NT, do_pre=True, debug=False):
    nc = bass.Bass("TRN2", target_bir_lowering=False)
    P = Prog(nc)

    def din(name, shape):
        return P.dram(name, nc.dram_tensor(name, shape, F32, kind="ExternalInput").ap())

    xo_d = din("xo", [D, XO_W])
    xp_d = din("xp", [D, XP_W])
    memT_d = din("memT", [D, 256])
    wall_d = din("wall", [128, WTOT])
    cst_d = din("cst", [128, NCST])
    t2_d = din("t2", [128, NT2])
    wmkv_d = din("w_mem_kv", [D, 512])
    outT_d = P.dram("outT", nc.dram_tensor("outT", [D, HALF], F32, kind="ExternalOutput").ap())
    wbf_ap = nc.dram_tensor("wbf", [128, WTOT], BF16).ap()
    NGRP = 4
    wbf_grp = [P.dram("wbf%d" % i, wbf_ap) for i in range(NGRP)]
    cstart_ap = nc.dram_tensor("cstart", [NT, 64, 516], F32).ap()
    cstart_d = [P.dram("cstart%d" % i, cstart_ap[i]) for i in range(NT)]
    dbg = {}
    if debug:
        for nm, rows in (("d_hmem", 256), ("d_hna", 256), ("d_hm", 512), ("d_y", 1024), ("d_x1", 1024)):
            dbg[nm] = P.dram(nm, nc.dram_tensor(nm, [rows, HALF], F32, kind="ExternalOutput").ap())
        dbg["d_c"] = P.dram("d_c", nc.dram_tensor("d_c", [64, 516], F32, kind="ExternalOutput").ap())
        dbg["d_misc"] = P.dram("d_misc", nc.dram_tensor("d_misc", [128, 4096], F32, kind="ExternalOutput").ap())

    xo_v = V(xo_d, xo_d.ap.rearrange("(kc p) t -> p kc t", p=128))
    xp_v = V(xp_d, xp_d.ap.rearrange("(kc p) t -> p kc t", p=128))

    cst = P.sbuf("cst", [128, NCST], F32)

    def cf(name, rows=128):
        o, l = CSTOFF[name]
        return cst[0:rows, o:o + l]

    identb = P.sbuf("identb", [128, 128], BF16)
    onesb = P.sbuf("onesb", [128, 128], BF16)
    t2 = P.sbuf("t2", [128, NT2], BF16)
    t2v = t2.v().rearrange("p (s h c) -> p s h c", s=14, h=8, c=64)
    ring = [P.sbuf("ring%d" % i, [128, SLOT], BF16) for i in range(NSLOT)]
    xs = [P.sbuf("xs%d" % i, [128, 8, 258], F32) for i in range(2)]
    sq = P.sbuf("sq", [128, 8, 258], BF16)
    rstd = P.sbuf("rstd", [128, 512], F32)
    hT = P.sbuf("hT", [128, 8, 516], BF16)
    hh = P.sbuf("hh", [128, 8, 256], BF16)
    kmT = P.sbuf("kmT", [64, 4, 256], BF16)
    vm_aug = P.sbuf("vm_aug", [128, 2, 4, 65], BF16)
    Cst = [P.sbuf("C%d" % i, [64, 516], F32) for i in range(2)]
    Cb = [P.sbuf("Cb%d" % i, [64, 516], BF16) for i in range(2)]
    hmT = P.sbuf("hmT", [128, 4, 512], BF16)
    hnaT = P.sbuf("hnaT", [128, 2, 512], BF16)
    hmemT = P.sbuf("hmemT", [64, 4, 512], BF16)
    yTs = [P.sbuf("yT%d" % i_, [128, 512], BF16) for i_ in range(8)]
    yaccs = [P.sbuf("yacc%d" % i_, [128, 512], F32) for i_ in range(2)]
    tmpys = [P.sbuf("tmpy%d" % i_, [128, 512], F32) for i_ in range(2)]
    gss = [P.sbuf("gs%d" % i_, [128, 512], F32) for i_ in range(2)]
    sqxs = [P.sbuf("sqx%d" % i_, [128, 512], BF16) for i_ in range(2)]
    rls = [P.sbuf("rl%d" % i_, [128, 512], F32) for i_ in range(2)]
    ost = [P.sbuf("ost%d" % i, [128, 512], F32) for i in range(2)]
    dstage = P.sbuf("dstage", [128, 512], F32) if debug else None
    R = P.sbuf("R", [128, RBYTES // 2], BF16)
    region = []

    def carve(name, off, parts, shape, dt):
        n = 1
        for s_ in shape:
            n *= s_
        nb = n * (4 if dt == F32 else 2)
        assert off % 4 == 0 and off + nb <= RBYTES, (name, off, nb)
        ap = R.ap[0:parts, off // 2:(off + nb) // 2]
        if dt == F32:
            ap = ap.bitcast(F32)
        if len(shape) == 2:
            ap = ap.rearrange("p (a b) -> p a b", a=shape[0], b=shape[1])
        elif len(shape) == 3:
            ap = ap.rearrange("p (a b c) -> p a b c", a=shape[0], b=shape[1], c=shape[2])
        return TT(name, ap, off, off + nb, region)

    class Lay:
        def __init__(self):
            self.o = 0

        def __call__(self, name, parts, shape, dt):
            t = carve(name, self.o, parts, shape, dt)
            self.o = t.hi + (-t.hi) % 4
            return t

    L = Lay()
    qnaT = L("qnaT", 32, [8, 512], BF16)
    knaT = L("knaT", 32, [8, 1024], BF16)
    vna = L("vna", 128, [8, 8, 33], BF16)
    sqns = [L("sqn%d" % i_, 128, [16, 32], F32) for i_ in range(2)]
    tmpns = [L("tmpn%d" % i_, 128, [16, 32], F32) for i_ in range(2)]
    qkns = [L("qkn%d" % i_, 128, [16, 32], BF16) for i_ in range(2)]
    ssns = [L("ssn%d" % i_, 128, [16], F32) for i_ in range(2)]
    hna_tok = L("hna_tok", 128, [8, 32], BF16)
    rdn = L("rdn", 128, [8], F32)
    mem_off = L.o
    PTn = L("PTn", 128, [5, 8, 128], BF16)
    sbns = [L("sbn%d" % i_, 128, [8, 128], F32) for i_ in range(2)]
    L = Lay()
    L.o = mem_off
    memq = L("memq", 64, [4, 512], BF16)
    PTm = L("PTm", 128, [4, 2, 512], BF16)
    hmem_toks = [L("hmem_tok%d" % i_, 128, [4, 64], BF16) for i_ in range(2)]
    sqfs = [L("sqf%d" % i_, 64, [512], F32) for i_ in range(2)]
    rss = [L("rs%d" % i_, 64, [512], F32) for i_ in range(2)]
    rdms = [L("rdm%d" % i_, 128, [4], F32) for i_ in range(2)]
    L = Lay()
    qkT = L("qkT", 64, [8, 512], BF16)
    ktok = L("ktok", 64, [8, 256], BF16)
    vaug = L("vaug", 64, [8, 4, 129], BF16)
    osig = L("osig", 64, [8, 512], BF16)
    hacc = L("hacc", 64, [8, 512], F32)
    zqs = [L("zq%d" % i_, 64, [516], F32) for i_ in range(2)]
    accqs = [L("accq%d" % i_, 64, [512], F32) for i_ in range(2)]
    zq, accq = zqs[0], accqs[0]
    Gt = L("Gt", 64, [8, 16], F32)
    Et = L("Et", 64, [8, 8], F32)
    Lt = L("Lt", 64, [8, 8], F32)
    tmpu = L("tmpu", 64, [8, 8], F32)
    Ut = L("Ut", 64, [8, 8], F32)
    Wt = L("Wt", 64, [8, 8], F32)
    DECt = L("DECt", 64, [8, 8], F32)
    IWt = L("IWt", 64, [8, 8], F32)
    PMs = [L("PM%d" % i_, 64, [256], BF16) for i_ in range(2)]
    vus = [L("vu%d" % i_, 64, [4, 129], BF16) for i_ in range(2)]
    t1s = [L("t1%d" % i_, 64, [4], F32) for i_ in range(2)]
    rrs = [L("rr%d" % i_, 64, [4], F32) for i_ in range(2)]
    tmphs = [L("tmph%d" % i_, 64, [2, 128], F32) for i_ in range(2)]
    ssh = L("ssh", 64, [4], F32)
    hm_tok = L("hm_tok", 64, [512], BF16)
    L = Lay()
    PB = []
    for i_ in range(2):
        PB.append(dict(
            hT=L("p_hT%d" % i_, 128, [8, 516], BF16),
            zk=L("zk%d" % i_, 128, [2, 516], F32),
            kTp=L("kTp%d" % i_, 128, [2, 512], BF16),
            accp=L("accp%d" % i_, 128, [512], F32),
            ktok128=L("ktok128%d" % i_, 128, [4, 256], BF16),
            vaug128=L("vaug128%d" % i_, 128, [4, 4, 129], BF16),
            vws=L("vws%d" % i_, 128, [4, 4, 129], BF16),
            Gp=L("Gp%d" % i_, 128, [4, 8], F32),
            Ep=L("Ep%d" % i_, 128, [4, 4], F32),
            Lp=L("Lp%d" % i_, 128, [4, 4], F32),
            WSp=L("WSp%d" % i_, 128, [4, 4], F32),
            DECp=L("DECp%d" % i_, 64, [4], F32)))
    L = Lay()
    aTs = [L("aT%d" % i_, 128, [4, 512], BF16) for i_ in range(8)]
    hnTs = [L("hnT%d" % i_, 128, [512], BF16) for i_ in range(8)]
    x1Ts = [L("x1T%d" % i_, 128, [512], F32) for i_ in range(8)]

    pb = [P.psum("pb%d" % i, [128, 512], F32) for i in range(8)]
    rot = [0]
    rotn = [2]

    def nb():
        rot[0] = (rot[0] + 1) % rotn[0]
        return pb[rot[0]]

    def pbf(i):
        return V(pb[i], pb[i].ap.bitcast(BF16))

    P.dma(cst.v(), cst_d.v(), q="sp")
    P.dma(t2.v(), t2_d.v(), q="pool")
    P.ts(t2.v(), t2.v(), 32.0 ** 0.5, ALU.mult)
    P.copy(identb.v(), cf("ident"))
    P.copy(onesb.v(), cf("ones"))
    ones64f = cst[0:64, CSTOFF["ones"][0]:CSTOFF["ones"][0] + 64]
    triAf = cf("triA", 64)
    triBf = cf("triB", 64)
    g_mix8 = cf("g_mix8")
    g_ffn8 = cf("g_ffn8")
    g_mem8 = cf("g_mem8")

    seg_ctr = [0]

    xq = ["sp"]

    def rms_load(src, n):
        xsb = xs[seg_ctr[0] % 2]
        seg_ctr[0] += 1
        P.dma(xsb[:, :, 0:n], src, q=xq[0])
        return xsb

    def rmsnorm_seg(src, g8, dst, n, xsb=None):
        if xsb is None:
            xsb = rms_load(src, n)
        P.act(sq[:, :, 0:n], xsb[:, :, 0:n], AF.Square)
        for kc in range(8):
            P.mm(pb[7][:, 0:n], onesb.v(), sq[:, kc, 0:n], start=(kc == 0), stop=(kc == 7))
        P.act(rstd[:, 0:n], pb[7][:, 0:n], AF.Sqrt, bias=EPS, scale=1.0 / D)
        P.recip(rstd[:, 0:n], rstd[:, 0:n])
        for kc in range(8):
            P.stt(dst[:, kc, :], xsb[:, kc, 0:n], g8[:, kc:kc + 1], rstd[:, 0:n], ALU.mult, ALU.mult)

    def bc_last(v, shape):
        return V(v.t, v.ap.rearrange("p (a o) -> p a o", o=1).to_broadcast(list(shape)))

    memT_v = V(memT_d, memT_d.ap.rearrange("(kc p) t -> p kc t", p=128))
    rmsnorm_seg(memT_v, g_mem8, hh[:, :, 0:256], 256)
    wkv = ring[0][:, 0:4096].rearrange("p (kc n) -> p kc n", kc=8)
    P.dma(wkv, V(wmkv_d, wmkv_d.ap.rearrange("(kc p) n -> p kc n", p=128)), q="pool")
    for h in range(4):
        ps = nb()
        for kc in range(8):
            P.mm(ps[0:64, 0:256], wkv[:, kc, h * 64:(h + 1) * 64], hh[:, kc, :], start=(kc == 0), stop=(kc == 7))
        sqf, rs = sqfs[h % 2], rss[h % 2]
        P.act(sqf[:, 0:256], ps[0:64, 0:256], AF.Square)
        ps2 = nb()
        P.mm(ps2[0:64, 0:256], ones64f, sqf[:, 0:256])
        P.act(rs[:, 0:256], ps2[0:64, 0:256], AF.Sqrt, bias=EPS, scale=1.0 / 64)
        P.recip(rs[:, 0:256], rs[:, 0:256])
        P.stt(kmT[:, h, :], ps[0:64, 0:256], cf("g_memk", 64), rs[:, 0:256], ALU.mult, ALU.mult)
    P.memset(vm_aug.v(), 1.0)
    for mc in range(2):
        ps = nb()
        for kc in range(8):
            P.mm(ps[:, 0:256], hh[:, kc, mc * 128:(mc + 1) * 128], wkv[:, kc, 256:512], start=(kc == 0), stop=(kc == 7))
        P.copy(vm_aug[:, mc, :, 0:64], ps[:, 0:256].rearrange("p (h d) -> p h d", h=4))

    def conv_chunks():
        for i, (nm, ln) in enumerate(CHUNKS):
            o, l = CHOFF[nm]
            g = wbf_grp[min(NGRP - 1, i * NGRP // len(CHUNKS))]
            P.dma(V(g, wbf_ap[:, o:o + l]), wall_d[:, o:o + l], q="pool", sem_tile=g)

    def chunk_group(nm):
        i = [c[0] for c in CHUNKS].index(nm)
        return wbf_grp[min(NGRP - 1, i * NGRP // len(CHUNKS))]

    wseq = []
    for _t in range(n_main_tiles):
        wseq += [c[0] for c in CHUNKS]
    wstate = {"next_load": 0, "next_use": 0, "released": 0}

    def _pump():
        while wstate["next_load"] < len(wseq) and wstate["next_load"] - NSLOT < wstate["released"]:
            i = wstate["next_load"]
            nm = wseq[i]
            o, l = CHOFF[nm]
            g = chunk_group(nm)
            P.dma(ring[i % NSLOT][:, 0:l], V(g, wbf_ap[:, o:o + l]), q="sp")
            wstate["next_load"] = i + 1

    def next_w(nm, hold=False):
        i = wstate["next_use"]
        assert wseq[i] == nm, (wseq[i], nm)
        if not hold:
            wstate["released"] = i
        _pump()
        assert wstate["next_load"] > i
        wstate["next_use"] = i + 1
        return ring[i % NSLOT]

    def gate_math(ndir):
        G4 = Gt.v().rearrange("p c (d e) -> p c d e", d=2, e=8)
        E4 = Et.v().rearrange("p c (d e) -> p c d e", d=2, e=4)
        L4 = Lt.v().rearrange("p c (d e) -> p c d e", d=2, e=4)
        U4 = tmpu.v().rearrange("p c (d e) -> p c d e", d=2, e=4)
        nd = ndir
        P.act(E4[:, :, 0:nd, :], G4[:, :, 0:nd, 4:8], AF.Exp, scale=-1.0)
        P.act(L4[:, :, 0:nd, :], E4[:, :, 0:nd, :], AF.Ln, bias=1.0)
        psb = pb[5]
        for c in range(NCH):
            P.mm(psb[0:64, c * 24:c * 24 + 4], triAf, Lt[:, c, 0:4])
            if nd == 2:
                P.mm(psb[0:64, c * 24 + 4:c * 24 + 8], triBf, Lt[:, c, 4:8])
            P.mm(psb[0:64, c * 24 + 8:c * 24 + 8 + 4 * nd], ones64f, Lt[:, c, 0:4 * nd])
        psv = psb[0:64, 0:NCH * 24].rearrange("p (c e) -> p c e", c=NCH)
        bp4 = psv[:, :, 0:8].rearrange("p c (d e) -> p c d e", d=2)
        P.tt(U4[:, :, 0:nd, :], G4[:, :, 0:nd, 0:4], bp4[:, :, 0:nd, :], ALU.add)
        P.act(Ut[:, :, 0:4 * nd], tmpu[:, :, 0:4 * nd], AF.Exp)
        P.act(Wt[:, :, 0:4 * nd], psv[:, :, 0:4 * nd], AF.Exp, scale=-1.0, bias=LN8)
        if nd == 2:
            P.act(IWt.v(), psv[:, :, 0:8], AF.Exp, scale=1.0, bias=-LN8)
        P.act(DECt[:, :, 0:4 * nd], psv[:, :, 8:8 + 4 * nd], AF.Exp, scale=-1.0)

    def state_update(c, x, pdb):
        Cx = Cst[x]
        for bk in range(2):
            P.tt(Cx[:, bk * 258:(bk + 1) * 258], pdb[bk][0:64, 0:258], Cx[:, bk * 258:(bk + 1) * 258], ALU.add)
        C3 = Cx.v().rearrange("p (h e) -> p h e", h=4)
        P.tt(C3, C3, bc_last(DECt[:, c, x * 4:(x + 1) * 4], [64, 4, 129]), ALU.mult)

    pre_ctr = [0]

    def dmisc0(col, src):
        rows, n = src.ap.shape[0], src.ap.shape[1]
        P.copy(dstage[0:rows, 0:n], src)
        P.dma(dbg["d_misc"][0:rows, col:col + n], dstage[0:rows, 0:n], q="pool")

    def gen_pre(ch, src_v, col0s, convw, gsel, Cx, save, wq, wv, wg):
        B_ = PB[ch]
        hT, zk, kTp, accp, ktok128, vaug128, vws = (B_[k_] for k_ in ("hT", "zk", "kTp", "accp", "ktok128", "vaug128", "vws"))
        Gp, Ep, Lp, WSp, DECp = (B_[k_] for k_ in ("Gp", "Ep", "Lp", "WSp", "DECp"))
        mybanks = [pb[2 * ch], pb[2 * ch + 1]]
        cnt = [0]

        def mb():
            cnt[0] += 1
            return mybanks[cnt[0] % 2]

        gcol = ch * 64
        for ti, col0 in enumerate(col0s):
            if save:
                P.dma(cstart_d[ti].v(), Cx.v(), q="sp")
            for sg in range(2):
                rmsnorm_seg(src_v[:, :, col0 + sg * 258:col0 + (sg + 1) * 258], g_mix8, hT[:, :, sg * 258:(sg + 1) * 258], 258)
                yield
            for blk in range(2):
                for sg in range(2):
                    ps = mb()
                    for kc in range(8):
                        P.mm(ps[:, 0:258], wq[:, kc, 256 + blk * 128:256 + (blk + 1) * 128], hT[:, kc, sg * 258:(sg + 1) * 258],
                             start=(kc == 0), stop=(kc == 7))
                    P.copy(zk[:, blk, sg * 258:(sg + 1) * 258], ps[:, 0:258], eng="act")
                yield
                P.ts(accp.v(), zk[:, blk, 0:512], convw[:, blk * 5:blk * 5 + 1], ALU.mult)
                for j in range(1, 5):
                    P.stt(accp.v(), zk[:, blk, j:j + 512], convw[:, blk * 5 + j:blk * 5 + j + 1], accp.v(), ALU.mult, ALU.add)
                P.act(kTp[:, blk, :], accp.v(), AF.Silu, bias=cf("convb_p")[:, blk:blk + 1])
                yield
            for tb in range(4):
                ps = mb()
                for kc in range(8):
                    P.mm(ps.v(), hT[:, kc, 2 + tb * 128:2 + (tb + 1) * 128], wv[:, kc, :], start=(kc == 0), stop=(kc == 7))
                P.copy(vaug128[:, tb, :, 0:128], ps.v().rearrange("p (h e) -> p h e", h=4), eng="act")
                for kc in range(8):
                    P.mm(pb[6][:, gcol + tb * 16:gcol + (tb + 1) * 16], hT[:, kc, 2 + tb * 128:2 + (tb + 1) * 128], wg[:, kc, :],
                         start=(kc == 0), stop=(kc == 7))
                pt = pbf(4 + ch)
                for blk in range(2):
                    P.transpose(pt[:, (tb % 2) * 256 + blk * 128:(tb % 2) * 256 + (blk + 1) * 128], kTp[:, blk, tb * 128:(tb + 1) * 128], identb.v())
                P.copy(ktok128[:, tb, :], pt[:, (tb % 2) * 256:(tb % 2) * 256 + 256])
                yield
            gb = cf("gbias")
            for tb in range(4):
                P.tt(Gp[:, tb, :], pb[6][:, gcol + tb * 16 + gsel * 8:gcol + tb * 16 + gsel * 8 + 8], gb[:, gsel * 8:gsel * 8 + 8], ALU.add)
            P.act(Ep.v(), Gp[:, :, 4:8], AF.Exp, scale=-1.0)
            P.act(Lp.v(), Ep.v(), AF.Ln, bias=1.0)
            yield
            pss_ = pb[7]
            pc = 32 * ch + 320
            sut = cf("sut128")
            onesf = cf("ones")
            for tb in range(4):
                P.mm(pss_[:, pc + tb * 4:pc + (tb + 1) * 4], sut, Lp[:, tb, :], start=True, stop=(tb == 3))
                for t2_ in range(tb + 1, 4):
                    P.mm(pss_[:, pc + tb * 4:pc + (tb + 1) * 4], onesf, Lp[:, t2_, :], start=False, stop=(t2_ == 3))
            for tb in range(4):
                P.mm(pss_[0:64, pc + 16:pc + 20], onesf[:, 0:64], Lp[:, tb, :], start=(tb == 0), stop=(tb == 3))
            P.tt(WSp.v(), Gp[:, :, 0:4], pss_[:, pc:pc + 16].rearrange("p (t h) -> p t h", t=4), ALU.subtract)
            P.act(WSp.v(), WSp.v(), AF.Exp)
            P.act(DECp.v(), pss_[0:64, pc + 16:pc + 20], AF.Exp, scale=-1.0)
            yield
            for tb in range(4):
                P.tt(vws[:, tb], vaug128[:, tb], bc_last(WSp[:, tb, :], [128, 4, 129]), ALU.mult)
            yield
            pdb = mybanks
            for h in range(4):
                for tb in range(4):
                    P.mm(pdb[h // 2][0:64, (h % 2) * 129:(h % 2) * 129 + 129], ktok128[:, tb, h * 64:(h + 1) * 64], vws[:, tb, h, :],
                         start=(tb == 0), stop=(tb == 3))
            C3 = Cx.v().rearrange("p (h e) -> p h e", h=4)
            P.tt(C3, C3, bc_last(DECp.v(), [64, 4, 129]), ALU.mult)
            for bk in range(2):
                P.tt(Cx[:, bk * 258:(bk + 1) * 258], pdb[bk][0:64, 0:258], Cx[:, bk * 258:(bk + 1) * 258], ALU.add)
            yield
        if save:
            P.dma(cstart_d[len(col0s)].v(), Cx.v(), q="sp")

    if do_pre:
        o_, l_ = CHOFF["qk_m"]
        P.dma(ring[1][:, 0:l_], wall_d[:, o_:o_ + l_], q="pool")
        o_, l_ = CHOFF["v_m"]
        P.dma(ring[2][:, 0:l_], wall_d[:, o_:o_ + l_], q="pool")
        o_, l_ = CHOFF["g16"]
        P.dma(ring[3][:, 0:l_], wall_d[:, o_:o_ + l_], q="pool")
    conv_chunks()
    if do_pre:
        wq_p = ring[1][:, 0:4096].rearrange("p (kc n) -> p kc n", kc=8)
        wv_p = ring[2][:, 0:4096].rearrange("p (kc n) -> p kc n", kc=8)
        wg_p = ring[3][:, 0:128].rearrange("p (kc n) -> p kc n", kc=8)
        for B_ in PB:
            P.memset(B_["vaug128"][:, :, :, 128:129], 1.0)
        P.memset(Cst[0].v(), 0.0)
        P.memset(Cst[1].v(), 0.0)
        gens = [gen_pre(0, xp_v, [t * TILE for t in range(NT)], cf("convw_pp"), 1, Cst[1], False, wq_p, wv_p, wg_p),
                gen_pre(1, xo_v, [256 + t * TILE - 2 for t in range(NT - 1)], cf("convw_po"), 0, Cst[0], True, wq_p, wv_p, wg_p)]
        while gens:
            for g_ in list(gens):
                try:
                    next(g_)
                except StopIteration:
                    gens.remove(g_)
        if debug:
            P.dma(dbg["d_c"].v(), Cst[1].v(), q="pool")
    else:
        P.memset(Cst[1].v(), 0.0)
        P.memset(Cst[0].v(), 0.0)
        for t in range(NT):
            P.dma(cstart_d[t].v(), Cst[0].v(), q="pool")
    P.copy(Cb[1].v(), Cst[1].v())
    xq[0] = "pool"
    rotn[0] = 4

    def dump(nm, view_rows, tile_i, src):
        rows = src.ap.shape[0]
        P.copy(dstage[0:rows, :], src)
        P.dma(dbg[nm][view_rows, tile_i * TILE:(tile_i + 1) * TILE], dstage[0:rows, :], q="pool")

    def dmisc(col, src):
        rows, n = src.ap.shape[0], src.ap.shape[1]
        P.copy(dstage[0:rows, 0:n], src)
        P.dma(dbg["d_misc"][0:rows, col:col + n], dstage[0:rows, 0:n], q="pool")

    hT_for = [None]

    def center_load(tile_i):
        c0 = 256 + tile_i * TILE
        return [rms_load(xo_v[:, :, c0 - 2 + sg * 258:c0 - 2 + (sg + 1) * 258], 258) for sg in range(2)]

    def center_norm(tile_i, bufs=None):
        c0 = 256 + tile_i * TILE
        for sg in range(2):
            rmsnorm_seg(xo_v[:, :, c0 - 2 + sg * 258:c0 - 2 + (sg + 1) * 258], g_mix8, hT[:, :, sg * 258:(sg + 1) * 258], 258,
                        xsb=(bufs[sg] if bufs else None))
        hT_for[0] = tile_i

    def main_tile(tile_i):
        c0 = 256 + tile_i * TILE
        hc = hT[:, :, 2:514]
        if hT_for[0] != tile_i:
            center_norm(tile_i)

        Wq = next_w("memq")[:, 0:2048].rearrange("p (kc n) -> p kc n", kc=8)
        Wnq = next_w("na_q", hold=True)[:, 0:2048].rearrange("p (kc n) -> p kc n", kc=8)
        Wnkv = next_w("na_kv", hold=True)[:, 0:4096].rearrange("p (kc n) -> p kc n", kc=8)

        def gen_mem():
            mcnt = [0]

            def mbk():
                mcnt[0] += 1
                return pb[mcnt[0] % 2]

            for h in range(4):
                sqf, rs = sqfs[h % 2], rss[h % 2]
                ps = mbk()
                for kc in range(8):
                    P.mm(ps[0:64, :], Wq[:, kc, h * 64:(h + 1) * 64], hc[:, kc, :], start=(kc == 0), stop=(kc == 7))
                P.act(sqf.v(), ps[0:64, :], AF.Square)
                ps2 = mbk()
                P.mm(ps2[0:64, :], ones64f, sqf.v())
                P.act(rs.v(), ps2[0:64, :], AF.Sqrt, bias=EPS, scale=1.0 / 64)
                P.recip(rs.v(), rs.v())
                P.stt(memq[:, h, :], ps[0:64, :], cf("g_memq", 64), rs.v(), ALU.mult, ALU.mult)
                yield
                for mc in range(2):
                    ps3 = mbk()
                    P.mm(ps3.v(), kmT[:, h, mc * 128:(mc + 1) * 128], memq[:, h, :])
                    P.act(PTm[:, h, mc, :], ps3.v(), AF.Exp, scale=0.125)
                yield
            for tb in range(4):
                hmem_tok, rdm = hmem_toks[tb % 2], rdms[tb % 2]
                pv = pb[6]
                for h in range(4):
                    for mc in range(2):
                        P.mm(pv[:, h * 65:(h + 1) * 65], PTm[:, h, mc, tb * 128:(tb + 1) * 128], vm_aug[:, mc, h, :],
                             start=(mc == 0), stop=(mc == 1))
                pv3 = pv[:, 0:260].rearrange("p (h e) -> p h e", h=4)
                P.recip(rdm.v(), pv3[:, :, 64])
                P.tt(hmem_tok.v(), pv3[:, :, 0:64], bc_last(rdm.v(), [128, 4, 64]), ALU.mult)
                pt = pbf(7)
                for h in range(4):
                    P.transpose(pt[0:64, h * 128:(h + 1) * 128], hmem_tok[:, h, :], identb.v())
                P.copy(hmemT[:, :, tb * 128:(tb + 1) * 128], pt[0:64, 0:512].rearrange("p (h t) -> p h t", h=4), eng="act")
                yield

        gq3 = cf("g_naqk").rearrange("p (h d) -> p h d", h=16)

        def gen_na(par):
            sqn, tmpn, qkn, ssn = sqns[par], tmpns[par], qkns[par], ssns[par]
            for jb in (2 + par, 4 + par, 0 + par, 6 + par):
                own = 2 <= jb <= 5
                if par == 0 and jb == 0:
                    rmsnorm_seg(xo_v[:, :, c0 - 256:c0], g_mix8, hh.v(), 256)
                if par == 0 and jb == 6:
                    rmsnorm_seg(xo_v[:, :, c0 + 512:c0 + 768], g_mix8, hh.v(), 256)
                if own:
                    hsrc = hT[:, :, 2 + (jb - 2) * 128:2 + (jb - 1) * 128]
                else:
                    hsrc = hh[:, :, (jb % 2) * 128:(jb % 2 + 1) * 128]
                pkv = pb[2 + par]
                for kc in range(8):
                    P.mm(pkv.v(), hsrc[:, kc, :], Wnkv[:, kc, :], start=(kc == 0), stop=(kc == 7))
                P.act(sqn[:, 0:8, :], pkv[:, 0:256].rearrange("p (h d) -> p h d", h=8), AF.Square)
                nh = 8
                if own:
                    pq = pb[4][:, par * 256:(par + 1) * 256]
                    for kc in range(8):
                        P.mm(pq, hsrc[:, kc, :], Wnq[:, kc, :], start=(kc == 0), stop=(kc == 7))
                    P.act(sqn[:, 8:16, :], pq.rearrange("p (h d) -> p h d", h=8), AF.Square)
                    nh = 16
                P.reduce(ssn[:, 0:nh], sqn[:, 0:nh, :], ALU.add)
                P.act(ssn[:, 0:nh], ssn[:, 0:nh], AF.Sqrt, bias=EPS, scale=1.0 / 32)
                P.recip(ssn[:, 0:nh], ssn[:, 0:nh])
                yield
                P.tt(tmpn[:, 0:8, :], pkv[:, 0:256].rearrange("p (h d) -> p h d", h=8), bc_last(ssn[:, 0:8], [128, 8, 32]), ALU.mult)
                P.tt(qkn[:, 0:8, :], tmpn[:, 0:8, :], gq3[:, 8:16, :], ALU.mult)
                if own:
                    P.tt(tmpn[:, 8:16, :], pq.rearrange("p (h d) -> p h d", h=8), bc_last(ssn[:, 8:16], [128, 8, 32]), ALU.mult)
                    P.tt(qkn[:, 8:16, :], tmpn[:, 8:16, :], gq3[:, 0:8, :], ALU.mult)
                P.copy(vna[:, jb, :, 0:32], pkv[:, 256:512].rearrange("p (h d) -> p h d", h=8), eng="act")
                yield
                pt = pbf(5 if par == 0 else 7)
                for h in range(8):
                    P.transpose(pt[0:32, h * 128:(h + 1) * 128], qkn[:, h, :], identb.v())
                P.copy(knaT[:, :, jb * 128:(jb + 1) * 128], pt[0:32, 0:1024].rearrange("p (h t) -> p h t", h=8))
                if own:
                    for h in range(8):
                        P.transpose(pt[0:32, h * 128:(h + 1) * 128], qkn[:, 8 + h, :], identb.v())
                    P.copy(qnaT[:, :, (jb - 2) * 128:(jb - 1) * 128], pt[0:32, 0:1024].rearrange("p (h t) -> p h t", h=8), eng="act")
                yield

        P.memset(vna[:, :, :, 32:33], 1.0)
        gens = [gen_mem(), gen_na(0), gen_na(1)]
        while gens:
            for g_ in list(gens):
                try:
                    next(g_)
                except StopIteration:
                    gens.remove(g_)
        if debug:
            for h in range(4):
                dump("d_hmem", slice(h * 64, (h + 1) * 64), tile_i, hmemT[:, h, :])

        for B in range(4):
            edge = (tile_i == 0 and B < 2)
            kbs = [2, 3, 4, 5] if edge else [B, B + 1, B + 2, B + 3, B + 4]
            for i, kb in enumerate(kbs):
                scb = [pb[4], pb[6]] if i % 2 == 0 else [pb[2], pb[3]]
                sbn = sbns[i % 2]
                d0 = 2 * kb - 4 - 2 * B
                if not edge:
                    s0 = 4 - d0
                    assert 0 <= s0 and s0 + 1 < 10, s0
                    for hb in range(2):
                        rb = t2v[:, s0:s0 + 2, hb * 4:(hb + 1) * 4, :].rearrange("p q h c -> p h q c")
                        P.mm(scb[hb].v(), identb.v(), rb, start=True, stop=False)
                        for h4 in range(4):
                            h = hb * 4 + h4
                            P.mm(scb[hb][:, h4 * 128:(h4 + 1) * 128], knaT[:, h, kb * 128:(kb + 1) * 128],
                                 qnaT[:, h, B * 128:(B + 1) * 128], start=False, stop=(h4 == 3))
                        P.act(PTn[:, i, hb * 4:(hb + 1) * 4, :], scb[hb].v().rearrange("p (h q) -> p h q", h=4), AF.Exp,
                              scale=32.0 ** -0.5)
                    continue
                for h in range(8):
                    P.mm(scb[h // 4][:, (h % 4) * 128:(h % 4 + 1) * 128], knaT[:, h, kb * 128:(kb + 1) * 128],
                         qnaT[:, h, B * 128:(B + 1) * 128])
                for qo in range(2):
                    d = d0 - qo
                    slot = 4 - d if d <= 2 else 10 + (d - 3)
                    assert 0 <= slot < 14, (slot, d)
                    for hb in range(2):
                        sc3 = scb[hb].v().rearrange("p (h q) -> p h q", h=4)
                        P.tt(sbn[:, hb * 4:(hb + 1) * 4, qo * 64:(qo + 1) * 64], sc3[:, :, qo * 64:(qo + 1) * 64],
                             t2v[:, slot, hb * 4:(hb + 1) * 4, :], ALU.add)
                P.act(PTn[:, i], sbn.v(), AF.Exp, scale=32.0 ** -0.5)
            pv = pb[7]
            for h in range(8):
                for i, kb in enumerate(kbs):
                    P.mm(pv[:, h * 33:(h + 1) * 33], PTn[:, i, h, :], vna[:, kb, h, :], start=(i == 0), stop=(i == len(kbs) - 1))
            pv3 = pv[:, 0:264].rearrange("p (h e) -> p h e", h=8)
            if debug and tile_i == NT - 1 and B == 0:
                dmisc(0, qnaT[:, 0, :])
                dmisc(512, knaT[:, 0, 0:512])
                dmisc(1024, knaT[:, 0, 512:1024])
                dmisc(2048, pv[:, 0:264])
                dmisc(2312, vna[:, 2].rearrange("p h e -> p (h e)"))
            P.recip(rdn.v(), pv3[:, :, 32])
            P.tt(hna_tok.v(), pv3[:, :, 0:32], bc_last(rdn.v(), [128, 8, 32]), ALU.mult)
            pt = pbf(5)
            hna2 = hna_tok.v().rearrange("p h d -> p (h d)")
            for k2 in range(2):
                P.transpose(pt[:, k2 * 128:(k2 + 1) * 128], hna2[:, k2 * 128:(k2 + 1) * 128], identb.v())
            P.copy(hnaT[:, :, B * 128:(B + 1) * 128], pt[:, 0:256].rearrange("p (k t) -> p k t", k=2))
        if debug:
            for k2 in range(2):
                dump("d_hna", slice(k2 * 128, (k2 + 1) * 128), tile_i, hnaT[:, k2, :])

        Wqk = next_w("qk_m")[:, 0:4096].rearrange("p (kc n) -> p kc n", kc=8)
        cwm = cf("convw_m", 64)
        for g in range(8):
            zq_, accq_ = zqs[g % 2], accqs[g % 2]
            for sg in range(2):
                ps = nb()
                for kc in range(8):
                    P.mm(ps[0:64, 0:258], Wqk[:, kc, g * 64:(g + 1) * 64], hT[:, kc, sg * 258:(sg + 1) * 258],
                         start=(kc == 0), stop=(kc == 7))
                P.copy(zq_[:, sg * 258:(sg + 1) * 258], ps[0:64, 0:258], eng="act")
            P.ts(accq_.v(), zq_[:, 0:512], cwm[:, g * 5:g * 5 + 1], ALU.mult)
            for j in range(1, 5):
                P.stt(accq_.v(), zq_[:, j:j + 512], cwm[:, g * 5 + j:g * 5 + j + 1], accq_.v(), ALU.mult, ALU.add)
            P.act(qkT[:, g, :], accq_.v(), AF.Silu, bias=cf("convb_m", 64)[:, g:g + 1])
        Wv = next_w("v_m")[:, 0:4096].rearrange("p (kc n) -> p kc n", kc=8)
        P.memset(vaug[:, :, :, 128:129], 1.0)
        for c in range(NCH):
            ps = nb()
            for kc in range(8):
                P.mm(ps[0:64, :], hT[:, kc, 2 + c * 64:2 + (c + 1) * 64], Wv[:, kc, :], start=(kc == 0), stop=(kc == 7))
            P.copy(vaug[:, c, :, 0:128], ps[0:64, :].rearrange("p (h e) -> p h e", h=4), eng="act")
        Wo = next_w("o_m")[:, 0:4096].rearrange("p (kc n) -> p kc n", kc=8)
        for c in range(NCH):
            ps = nb()
            for kc in range(8):
                P.mm(ps[0:64, :], hT[:, kc, 2 + c * 64:2 + (c + 1) * 64], Wo[:, kc, :], start=(kc == 0), stop=(kc == 7))
            P.act(osig[:, c, :], ps[0:64, :], AF.Sigmoid)
        Wg = next_w("g16")[:, 0:128].rearrange("p (kc n) -> p kc n", kc=8)
        for c in range(NCH):
            for kc in range(8):
                P.mm(pb[6][0:64, c * 16:(c + 1) * 16], hT[:, kc, 2 + c * 64:2 + (c + 1) * 64], Wg[:, kc, :], start=(kc == 0), stop=(kc == 7))
        gb = cf("gbias", 64)
        for c in range(NCH):
            P.tt(Gt[:, c, :], pb[6][0:64, c * 16:(c + 1) * 16], gb, ALU.add)
        gate_math(2)
        for c in range(NCH):
            pt = pbf(4)
            for h in range(4):
                P.transpose(pt[0:64, h * 64:(h + 1) * 64], qkT[:, 4 + h, c * 64:(c + 1) * 64], identb[0:64, 0:64])
            P.copy(ktok[:, c, :], pt[0:64, 0:256])
        P.dma(Cst[0].v(), cstart_d[tile_i].v(), q="pool")
        P.copy(Cb[0].v(), Cst[0].v(), eng="act")

        vu2 = zqs[1].v().bitcast(BF16)

        def vubuf(x, par):
            if par == 0:
                return vus[x].v()
            return vu2[:, x * 516:(x + 1) * 516].rearrange("p (h e) -> p h e", h=4)

        def make_vu(c, x, par):
            P.tt(vubuf(x, par), vaug[:, c], bc_last(Ut[:, c, x * 4:(x + 1) * 4], [64, 4, 129]), ALU.mult, eng="pool")

        def st_mm(c, x):
            cs = slice(c * 64, (c + 1) * 64)
            st = pb[0] if x == 0 else pb[3]
            for h in range(4):
                P.mm(st[0:64, h * 64:(h + 1) * 64], qkT[:, 4 + h, cs], qkT[:, h, cs])

        def pm_op(c, x):
            st = pb[0] if x == 0 else pb[3]
            P.tt(PMs[x].v(), st[0:64, 0:256], cf("mask4A" if x == 0 else "mask4B", 64), ALU.mult)

        def front_a(c, x):
            st_mm(c, x)
            pm_op(c, x)

        def front_b(c, x, par):
            PM, vu = PMs[x], vubuf(x, par)
            cs = slice(c * 64, (c + 1) * 64)
            pob = [pb[1], pb[2]] if x == 0 else [pb[4], pb[6]]
            for h in range(4):
                po_h = pob[h // 2][0:64, (h % 2) * 129:(h % 2) * 129 + 129]
                P.mm(po_h, PM[:, h * 64:(h + 1) * 64], vu[:, h, :], start=True, stop=False)
                P.mm(po_h, qkT[:, h, cs], Cb[x][:, h * 129:(h + 1) * 129], start=False, stop=True)

        def norm_abs(c, x):
            t1 = t1s[x]
            pob = [pb[1], pb[2]] if x == 0 else [pb[4], pb[6]]
            for bk in range(2):
                po3 = pob[bk][0:64, 0:258].rearrange("p (h e) -> p h e", h=2)
                P.act(t1[:, bk * 2:(bk + 1) * 2].rearrange("p (h o) -> p h o", o=1), po3[:, :, 128:129], AF.Abs)

        def norm_dve(c, x):
            t1, rr = t1s[x], rrs[x]
            P.tt(t1.v(), t1.v(), IWt[:, c, x * 4:(x + 1) * 4], ALU.max)
            P.recip(rr.v(), t1.v())

        def norm_copy(c, x, first):
            rr, tmph = rrs[x], tmphs[x]
            pob = [pb[1], pb[2]] if x == 0 else [pb[4], pb[6]]
            h4 = hacc[:, c, :].rearrange("p (h e) -> p h e", h=4)
            for bk in range(2):
                for h2 in range(2):
                    h = bk * 2 + h2
                    src = pob[bk][0:64, h2 * 129:h2 * 129 + 128]
                    if first:
                        P.act(h4[:, h, :], src, AF.Copy, scale=rr[:, h:h + 1])
                    else:
                        P.act(tmph[:, h2, :], src, AF.Copy, scale=rr[:, h:h + 1])
                if not first:
                    P.tt(h4[:, bk * 2:(bk + 1) * 2, :], h4[:, bk * 2:(bk + 1) * 2, :], tmph.v(), ALU.add, eng="pool")

        def state_mm(c, x, par):
            vu = vubuf(x, par)
            pdv = pb[5] if x == 0 else pb[7]
            pdn = (pb[0] if x == 0 else pb[3])
            for h in range(4):
                P.mm(pdv[0:64, h * 128:(h + 1) * 128], ktok[:, c, h * 64:(h + 1) * 64], vu[:, h, 0:128])
            for h in range(4):
                P.mm(pdn[0:64, 256 + h:257 + h], ktok[:, c, h * 64:(h + 1) * 64], vu[:, h, 128:129])

        def state_dve(c, x):
            pdv = pb[5] if x == 0 else pb[7]
            pdn = (pb[0] if x == 0 else pb[3])
            C3 = Cst[x].v().rearrange("p (h e) -> p h e", h=4)
            P.tt(C3[:, :, 0:128], pdv[0:64, :].rearrange("p (h e) -> p h e", h=4), C3[:, :, 0:128], ALU.add)
            P.tt(C3[:, :, 128:129], pdn[0:64, 256:260].rearrange("p (h o) -> p h o", o=1), C3[:, :, 128:129], ALU.add)
            decb = bc_last(DECt[:, c, x * 4:(x + 1) * 4], [64, 4, 129])
            P.tt(Cb[x].v().rearrange("p (h e) -> p h e", h=4), C3, decb, ALU.mult)
            P.tt(C3, C3, decb, ALU.mult, eng="pool")

        def finalize(c, k):
            sqh_, hn_ = zqs[k % 2][:, 0:512], accqs[k % 2].v()
            hmt = hm_tok.v() if k % 2 == 0 else vus[0].v().rearrange("p h e -> p (h e)")[:, 0:512]
            ssh_ = t1s[k % 2]
            P.tt(sqh_, hacc[:, c, :], hacc[:, c, :], ALU.mult, eng="pool")
            P.reduce(ssh_.v(), sqh_.rearrange("p (h e) -> p h e", h=4), ALU.add)
            P.act(ssh_.v(), ssh_.v(), AF.Sqrt, bias=EPS, scale=1.0 / 128)
            P.recip(ssh_.v(), ssh_.v())
            P.tt(hn_.rearrange("p (h e) -> p h e", h=4), hacc[:, c, :].rearrange("p (h e) -> p h e", h=4),
                 bc_last(ssh_.v(), [64, 4, 128]), ALU.mult)
            P.tt(hn_, hn_, cf("g_mlstm", 64), ALU.mult)
            P.tt(hmt, hn_, osig[:, c, :], ALU.mult, eng="pool")
            pt = pbf(0 if k % 2 == 0 else 3)
            for k4 in range(4):
                P.transpose(pt[:, 768 + k4 * 64:768 + (k4 + 1) * 64], hmt[:, k4 * 128:(k4 + 1) * 128], identb[0:64, 0:64])
            P.copy(hmT[:, :, c * 64:(c + 1) * 64], pt[:, 768:1024].rearrange("p (k t) -> p k t", k=4), eng="act")

        make_vu(0, 0, 0)
        make_vu(NCH - 1, 1, 0)
        front_a(0, 0)
        front_a(NCH - 1, 1)
        for i_ in range(NCH):
            ca, cb = i_, NCH - 1 - i_
            par = i_ % 2
            first = i_ < NCH // 2
            if i_ + 1 < NCH:
                make_vu(ca + 1, 0, 1 - par)
                make_vu(cb - 1, 1, 1 - par)
            front_b(ca, 0, par)
            front_b(cb, 1, par)
            state_mm(ca, 0, par)
            state_mm(cb, 1, par)
            if i_ + 1 < NCH:
                st_mm(ca + 1, 0)
                st_mm(cb - 1, 1)
            norm_abs(ca, 0)
            norm_abs(cb, 1)
            norm_dve(ca, 0)
            norm_dve(cb, 1)
            if i_ + 1 < NCH:
                pm_op(ca + 1, 0)
                pm_op(cb - 1, 1)
            norm_copy(ca, 0, first)
            norm_copy(cb, 1, first)
            state_dve(ca, 0)
            state_dve(cb, 1)
        for k_, c_ in enumerate(range(NCH)):
            finalize(c_, k_)
        if debug:
            for k4 in range(4):
                dump("d_hm", slice(k4 * 128, (k4 + 1) * 128), tile_i, hmT[:, k4, :])

        for j in range(8):
            P.dma(x1Ts[j].v(), xo_v[:, j, c0:c0 + TILE], q="pool")

        for j in range(8):
            W = next_w("mrg%d" % j)
            Wgt = W[:, 0:3072].rearrange("p (kc n) -> p kc n", kc=8)
            Wpm = W[:, 3072:3584].rearrange("p (kc n) -> p kc n", kc=4)
            Wpn = W[:, 3584:3840].rearrange("p (kc n) -> p kc n", kc=2)
            Wpe = W[0:64, 3840:4352].rearrange("p (kc n) -> p kc n", kc=4)
            yacc = yaccs[j % 2]
            for b in range(3):
                gs = gss[(j * 3 + b) % 2]
                tmpy = tmpys[b % 2]
                pg = nb()
                for kc in range(8):
                    P.mm(pg.v(), Wgt[:, kc, b * 128:(b + 1) * 128], hc[:, kc, :], start=(kc == 0), stop=(kc == 7))
                P.act(gs.v(), pg.v(), AF.Sigmoid)
                pp = nb()
                if b == 0:
                    for k4 in range(4):
                        P.mm(pp.v(), Wpm[:, k4, :], hmT[:, k4, :], start=(k4 == 0), stop=(k4 == 3))
                    P.tt(yacc.v(), pp.v(), gs.v(), ALU.mult)
                elif b == 1:
                    for k2 in range(2):
                        P.mm(pp.v(), Wpn[:, k2, :], hnaT[:, k2, :], start=(k2 == 0), stop=(k2 == 1))
                    P.tt(tmpy.v(), pp.v(), gs.v(), ALU.mult)
                    P.tt(yacc.v(), yacc.v(), tmpy.v(), ALU.add)
                else:
                    for h in range(4):
                        P.mm(pp.v(), Wpe[:, h, :], hmemT[:, h, :], start=(h == 0), stop=(h == 3))
                    P.tt(tmpy.v(), pp.v(), gs.v(), ALU.mult)
                    P.tt(yTs[j].v(), yacc.v(), tmpy.v(), ALU.add)
            if debug:
                dump("d_y", slice(j * 128, (j + 1) * 128), tile_i, yTs[j].v())

        for j in range(8):
            if j % 4 == 0:
                Wout = next_w("out%d" % (j // 4))[:, 0:4096].rearrange("p (kc n) -> p kc n", kc=8)
            ps = nb()
            for kc in range(8):
                P.mm(ps.v(), Wout[:, kc, (j % 4) * 128:(j % 4 + 1) * 128], yTs[kc].v(), start=(kc == 0), stop=(kc == 7))
            P.tt(x1Ts[j].v(), x1Ts[j].v(), ps.v(), ALU.add)
            P.act(sqxs[j % 2].v(), x1Ts[j].v(), AF.Square)
            if j >= 1:
                P.mm(pb[7].v(), onesb.v(), sqxs[(j - 1) % 2].v(), start=(j == 1), stop=False)
            if debug:
                dump("d_x1", slice(j * 128, (j + 1) * 128), tile_i, x1Ts[j].v())
        P.mm(pb[7].v(), onesb.v(), sqxs[1].v(), start=False, stop=True)
        P.act(rstd.v(), pb[7].v(), AF.Sqrt, bias=EPS, scale=1.0 / D)
        P.recip(rstd.v(), rstd.v())
        for j in range(8):
            P.stt(hnTs[j].v(), x1Ts[j].v(), g_ffn8[:, j:j + 1], rstd.v(), ALU.mult, ALU.mult)

        nxt = (tile_i - 1 >= NT - n_main_tiles)
        if nxt:
            nbufs = center_load(tile_i - 1)

        for i in range(8):
            Wup = next_w("up%d" % i)[:, 0:4096].rearrange("p (kc n) -> p kc n", kc=8)
            for f4 in range(4):
                f = i * 4 + f4
                ps = nb()
                for kc in range(8):
                    P.mm(ps.v(), Wup[:, kc, f4 * 128:(f4 + 1) * 128], hnTs[kc].v(), start=(kc == 0), stop=(kc == 7))
                P.act(rls[f % 2].v(), ps.v(), AF.Relu)
                P.tt(aTs[i][:, f4, :], rls[f % 2].v(), rls[f % 2].v(), ALU.mult, eng="pool")
        if nxt:
            center_norm(tile_i - 1, nbufs)
        last = []
        for j in range(8):
            Wdn = next_w("dn%d" % j)[:, 0:4096].rearrange("p (f n) -> p f n", f=32)
            ps = nb()
            for f in range(32):
                P.mm(ps.v(), Wdn[:, f, :], aTs[f // 4][:, f % 4, :], start=(f == 0), stop=(f == 31))
            o_ = ost[j % 2]
            P.tt(o_.v(), ps.v(), x1Ts[j].v(), ALU.add)
            last.append(P.dma(outT_d[j * 128:(j + 1) * 128, tile_i * TILE:(tile_i + 1) * TILE], o_.v(), q="pool"))
        return last

    finals = []
    for tile_i in range(NT - 1, NT - 1 - n_main_tiles, -1):
        finals = main_tile(tile_i)
    fin = list(finals[-2:])
    for e in ("pool",):
        dmas = [op for op in P.ops[e] if op.is_dma]
        if dmas:
            fin.append(dmas[-1])
    for op in fin:
        op.needed = True
    P.emit(final_wait_ops=fin)
    P.close()
    return nc, P


_CACHE = {}


def kernel(**inputs):
    maps = _host_prep(inputs)
    if "nc" not in _CACHE:
        _CACHE["nc"] = build_program()[0]
    nc = _CACHE["nc"]
    res = run_bass_kernel_spmd(nc, maps, core_ids=list(range(8)))
    return _assemble(res.results)
```

```python
import math
from contextlib import ExitStack
import numpy as np
import concourse.bass as bass
import concourse.mybir as mybir
from concourse.bass_utils import run_bass_kernel_spmd

F32 = mybir.dt.float32
BF16 = mybir.dt.bfloat16
AF = mybir.ActivationFunctionType
ALU = mybir.AluOpType
AX = mybir.AxisListType

ENGS = ["pe", "act", "dve", "pool", "sp"]
SAME_DIST = 3
SEM_LIMIT = 30000


class V:
    __slots__ = ("t", "ap")

    def __init__(self, t, ap):
        self.t = t
        self.ap = ap

    def __getitem__(self, k):
        return V(self.t, self.ap[k])

    def rearrange(self, pattern_, **kw):
        return V(self.t, self.ap.rearrange(pattern_, **kw))

    def bc(self, shape):
        return V(self.t, self.ap.to_broadcast(list(shape)))

    def bitcast(self, dt):
        return V(self.t, self.ap.bitcast(dt))


class TT:
    def __init__(self, name, ap, lo=0, hi=0, region=None):
        self.name = name
        self.ap = ap
        self.w = None
        self.r = {}
        self.dsem = None
        self.dcount = 0
        self.overlaps = []
        self.lo, self.hi = lo, hi
        if region is not None:
            for o in region:
                if o.lo < hi and lo < o.hi:
                    o.overlaps.append(self)
                    self.overlaps.append(o)
            region.append(self)

    def __getitem__(self, k):
        return V(self, self.ap[k])

    def v(self):
        return V(self, self.ap)


class Op:
    __slots__ = ("eng", "idx", "fn", "waits", "needed", "is_dma", "semkey", "semval", "cnt")

    def __init__(self, eng, idx, fn, is_dma):
        self.eng = eng
        self.idx = idx
        self.fn = fn
        self.waits = []
        self.needed = False
        self.is_dma = is_dma
        self.semkey = None
        self.semval = None
        self.cnt = None


class Prog:
    def __init__(self, nc):
        self.nc = nc
        self.ops = {e: [] for e in ENGS}
        self.waited = {e: {} for e in ENGS}
        self.next_dsem = 0
        self.stack = ExitStack()

    def sbuf(self, name, shape, dt):
        h = self.stack.enter_context(self.nc.sbuf_tensor("sb_" + name, list(shape), dt))
        return TT(name, h[:])

    def psum(self, name, shape, dt):
        h = self.stack.enter_context(self.nc.psum_tensor("ps_" + name, list(shape), dt))
        return TT(name, h[:])

    def dram(self, name, ap):
        return TT("@" + name, ap)

    def _add(self, eng, fn, reads, writes, dma_tile=None):
        ops = self.ops[eng]
        op = Op(eng, len(ops), fn, dma_tile is not None)
        rt = [x.t for x in reads]
        wt = [x.t for x in writes]
        cand = []
        for t in rt:
            if t.w is not None:
                cand.append(t.w)
            for o in t.overlaps:
                if o.w is not None:
                    cand.append(o.w)
        for t in wt:
            if t.w is not None:
                cand.append(t.w)
            cand.extend(t.r.values())
            for o in t.overlaps:
                if o.w is not None:
                    cand.append(o.w)
                cand.extend(o.r.values())
        wd = self.waited[eng]
        deps = {}
        for d in cand:
            if d.is_dma:
                key, val = d.semkey, d.semval
            else:
                key, val = d.eng, d.idx
                if d.eng == eng and not op.is_dma and eng != "pool" and (eng == "pe" or op.idx - d.idx > SAME_DIST):
                    continue
            if wd.get(key, -1) >= val:
                continue
            if key not in deps or deps[key][1] < val:
                deps[key] = (d, val)
        for key, (d, val) in deps.items():
            wd[key] = val
            d.needed = True
            op.waits.append(d)
        if dma_tile is not None:
            if dma_tile.dsem is None:
                dma_tile.dsem = {}
                dma_tile.dcount = {}
            if eng not in dma_tile.dsem:
                dma_tile.dsem[eng] = self.next_dsem
                dma_tile.dcount[eng] = 0
                self.next_dsem += 1
            dma_tile.dcount[eng] += 1
            op.semkey = ("d", dma_tile.dsem[eng])
            op.semval = 16 * dma_tile.dcount[eng]
            rkey = op.semkey
        else:
            rkey = eng
        for t in rt:
            t.r[rkey] = op
        for t in wt:
            t.w = op
            t.r = {}
        ops.append(op)
        return op

    def mm(self, out, lhsT, rhs, start=True, stop=True):
        self._add("pe", lambda e: e.matmul(out.ap, lhsT.ap, rhs.ap, start=start, stop=stop),
                  [lhsT, rhs] + ([] if start else [out]), [out])

    def transpose(self, out, in_, ident):
        self._add("pe", lambda e: e.transpose(out.ap, in_.ap, ident.ap), [in_, ident], [out])

    def act(self, out, in_, func, bias=None, scale=None, accum_out=None):
        reads = [in_]
        kw = {}
        if bias is not None:
            if isinstance(bias, V):
                reads.append(bias)
                kw["bias"] = bias.ap
            else:
                kw["bias"] = bias
        if scale is not None:
            if isinstance(scale, V):
                reads.append(scale)
                kw["scale"] = scale.ap
            else:
                kw["scale"] = scale
        writes = [out]
        if accum_out is not None:
            writes.append(accum_out)
            kw["accum_out"] = accum_out.ap
        self._add("act", lambda e: e.activation(out.ap, in_.ap, func, **kw), reads, writes)

    def tt(self, out, in0, in1, op, eng="dve"):
        self._add(eng, lambda e: e.tensor_tensor(out.ap, in0.ap, in1.ap, op), [in0, in1], [out])

    def ts(self, out, in0, s1, op0, s2=None, op1=None, eng="dve"):
        reads = [in0]
        a1 = s1
        if isinstance(s1, V):
            reads.append(s1)
            a1 = s1.ap
        a2 = s2
        if isinstance(s2, V):
            reads.append(s2)
            a2 = s2.ap
        if op1 is None:
            self._add(eng, lambda e: e.tensor_scalar(out.ap, in0.ap, a1, None, op0), reads, [out])
        else:
            self._add(eng, lambda e: e.tensor_scalar(out.ap, in0.ap, a1, a2, op0, op1), reads, [out])

    def stt(self, out, in0, scalar, in1, op0, op1, eng="dve"):
        reads = [in0, in1]
        a = scalar
        if isinstance(scalar, V):
            reads.append(scalar)
            a = scalar.ap
        self._add(eng, lambda e: e.scalar_tensor_tensor(out.ap, in0.ap, a, in1.ap, op0, op1), reads, [out])

    def copy(self, out, in_, eng="dve"):
        if eng == "act":
            self._add(eng, lambda e: e.copy(out.ap, in_.ap), [in_], [out])
        else:
            self._add(eng, lambda e: e.tensor_copy(out.ap, in_.ap), [in_], [out])

    def memset(self, out, val, eng="dve"):
        self._add(eng, lambda e: e.memset(out.ap, val), [], [out])

    def reduce(self, out, in_, op, axis=AX.X, eng="dve"):
        self._add(eng, lambda e: e.tensor_reduce(out.ap, in_.ap, axis, op), [in_], [out])

    def recip(self, out, in_):
        self._add("dve", lambda e: e.reciprocal(out.ap, in_.ap), [in_], [out])

    def dma(self, out, in_, q="sp", sem_tile=None):
        if sem_tile is None:
            sem_tile = out.t if out.t.name[0] != "@" else in_.t
        op = self._add(q, None, [in_], [out], dma_tile=sem_tile)
        op.fn = (out.ap, in_.ap)
        return op

    def emit(self, final_wait_ops=()):
        nc = self.nc
        st = self.stack
        esems = {}
        for e in ENGS:
            c = 0
            for op in self.ops[e]:
                if not op.is_dma and op.needed:
                    c += 1
                    op.cnt = c
            esems[e] = [st.enter_context(nc.semaphore("s_%s_%d" % (e, i))) for i in range(c // SEM_LIMIT + 1)]
        dsems = [st.enter_context(nc.semaphore("d_%d" % i)) for i in range(self.next_dsem)]

        def semof(d):
            if d.is_dma:
                return dsems[d.semkey[1]], d.semval
            k = (d.cnt - 1) // SEM_LIMIT
            return esems[d.eng][k], d.cnt - k * SEM_LIMIT

        block = st.enter_context(nc.Block())
        engattr = {"pe": "tensor", "act": "scalar", "dve": "vector", "pool": "gpsimd", "sp": "sync"}
        prog = self

        def make(ename):
            def body(eng):
                for op in prog.ops[ename]:
                    for d in op.waits:
                        s, v = semof(d)
                        eng.wait_ge(s, v)
                    if op.is_dma:
                        o, i = op.fn
                        eng.dma_start(out=o, in_=i).then_inc(dsems[op.semkey[1]], 16)
                    else:
                        ins = op.fn(eng)
                        if op.needed:
                            s, v = semof(op)
                            ins.then_inc(s, 1)
                if ename == "sp":
                    for d in final_wait_ops:
                        s, v = semof(d)
                        eng.wait_ge(s, v)
            return body

        for e in ENGS:
            getattr(block, engattr[e])(make(e))

    def close(self):
        self.stack.close()


D = 1024
S = 8192
NB = 4
HALF = 4096
TILE = 512
NT = HALF // TILE
NCH = TILE // 64
DFF = 4096
EPS = 1e-6
NEG = -80.0
LN8 = math.log(0.125)

O_QK, O_V, O_O, O_I, O_F, O_NA, O_MQ, O_G = 0, 512, 1024, 1536, 1544, 1552, 2320, 2576
XO_W = 256 + HALF + 256
XP_W = HALF + 4

SLOT = 4352
NSLOT = 4


def _chunks():
    ch = [("memq", 8 * 256), ("na_q", 8 * 256), ("na_kv", 8 * 512), ("qk_m", 8 * 512), ("v_m", 8 * 512),
          ("o_m", 8 * 512), ("g16", 8 * 16)]
    ch += [("mrg%d" % j, SLOT) for j in range(8)]
    ch += [("out%d" % i, 8 * 512) for i in range(2)]
    ch += [("up%d" % i, 8 * 512) for i in range(8)]
    ch += [("dn%d" % j, 32 * 128) for j in range(8)]
    off = {}
    o = 0
    for n, l in ch:
        off[n] = (o, l)
        o += l
    return ch, off, o


CHUNKS, CHOFF, WTOT = _chunks()

CST_FIELDS = [("g_mix8", 8), ("g_ffn8", 8), ("g_mem8", 8), ("convw_m", 40), ("convb_m", 8),
              ("convw_po", 10), ("convw_pp", 10), ("convb_p", 2), ("gbias", 16), ("g_mlstm", 512),
              ("g_naqk", 512), ("g_memq", 1), ("g_memk", 1), ("ident", 128), ("ones", 128),
              ("triA", 64), ("triB", 64), ("mask4A", 256), ("mask4B", 256), ("sut128", 128)]
CSTOFF = {}
_o = 0
for _n, _l in CST_FIELDS:
    CSTOFF[_n] = (_o, _l)
    _o += _l
NCST = _o
NT2 = 14 * 8 * 64


def _kmaj(W):
    K, N = W.shape
    kc = K // 128
    return np.ascontiguousarray(W.reshape(kc, 128, N).transpose(1, 0, 2)).reshape(128, kc * N)


def _host_prep(inp):
    f32 = np.float32
    w_in = np.asarray(inp["w_in"][0], f32)
    maps = []
    wall0 = np.zeros((128, WTOT), f32)

    def put(name, arr):
        o, l = CHOFF[name]
        assert arr.shape == (128, l), (name, arr.shape, l)
        wall0[:, o:o + l] = arr

    put("memq", _kmaj(w_in[:, O_MQ:O_MQ + 256]))
    put("na_q", _kmaj(w_in[:, O_NA:O_NA + 256]))
    put("na_kv", _kmaj(w_in[:, O_NA + 256:O_NA + 768]))
    put("qk_m", _kmaj(w_in[:, O_QK:O_QK + 512]))
    put("v_m", _kmaj(w_in[:, O_V:O_V + 512]))
    put("o_m", _kmaj(w_in[:, O_O:O_O + 512]))
    wpm = np.asarray(inp["w_proj_mlstm"][0], f32)
    wpn = np.asarray(inp["w_proj_na"][0], f32)
    wpe = np.asarray(inp["w_proj_mem"][0], f32)
    for j in range(8):
        cols = np.concatenate([O_G + b * 1024 + j * 128 + np.arange(128) for b in range(3)])
        a = _kmaj(w_in[:, cols])
        b_ = _kmaj(wpm[:, j * 128:(j + 1) * 128])
        c_ = _kmaj(wpn[:, j * 128:(j + 1) * 128])
        d_ = np.zeros((128, 4 * 128), f32)
        d_[0:64] = wpe[:, j * 128:(j + 1) * 128].reshape(4, 64, 128).transpose(1, 0, 2).reshape(64, 512)
        put("mrg%d" % j, np.concatenate([a, b_, c_, d_], axis=1))
    w_out = np.asarray(inp["w_out"][0], f32)
    for i in range(2):
        put("out%d" % i, _kmaj(w_out[:, i * 512:(i + 1) * 512]))
    w_up = np.asarray(inp["w_up"][0], f32)
    for i in range(8):
        put("up%d" % i, _kmaj(w_up[:, i * 512:(i + 1) * 512]))
    w_dn = np.asarray(inp["w_down"][0], f32)
    for j in range(8):
        put("dn%d" % j, _kmaj(w_dn[:, j * 128:(j + 1) * 128]))

    conv_w = np.asarray(inp["conv_w"][0], f32)
    conv_b = np.asarray(inp["conv_b"][0], f32)
    b_i = np.asarray(inp["b_igate"][0], f32)
    b_f = np.asarray(inp["b_fgate"][0], f32)
    rpb = np.asarray(inp["rpb"][0], f32)
    ar = np.arange(64)
    tri = (ar[:, None] <= ar[None, :]).astype(f32)

    walls, csts, t2s = [], [], []
    for half in range(2):
        dA, dB = half, 1 - half
        wall = wall0.copy()
        gcols = np.concatenate([O_I + dA * 4 + np.arange(4), O_F + dA * 4 + np.arange(4),
                                O_I + dB * 4 + np.arange(4), O_F + dB * 4 + np.arange(4)])
        o, l = CHOFF["g16"]
        wall[:, o:o + l] = _kmaj(w_in[:, gcols])
        walls.append(wall)

        cst = np.zeros((128, NCST), f32)

        def cput(name, arr, rows=128):
            o, l = CSTOFF[name]
            cst[0:rows, o:o + l] = arr

        cput("g_mix8", np.asarray(inp["g_mix"][0], f32).reshape(8, 128).T)
        cput("g_ffn8", np.asarray(inp["g_ffn"][0], f32).reshape(8, 128).T)
        cput("g_mem8", np.asarray(inp["g_mem"][0], f32).reshape(8, 128).T)
        cw_own = conv_w if half == 0 else conv_w[::-1]
        cw_par = cw_own[::-1]
        cput("convw_m", cw_own.T.reshape(8, 64, 5).transpose(1, 0, 2).reshape(64, 40), rows=64)
        cput("convb_m", conv_b.reshape(8, 64).T, rows=64)
        cput("convw_po", cw_own.T[256:512].reshape(2, 128, 5).transpose(1, 0, 2).reshape(128, 10))
        cput("convw_pp", cw_par.T[256:512].reshape(2, 128, 5).transpose(1, 0, 2).reshape(128, 10))
        cput("convb_p", conv_b[256:512].reshape(2, 128).T)
        gb = np.concatenate([b_i[dA], b_f[dA], b_i[dB], b_f[dB]])
        cput("gbias", np.tile(gb[None, :], (128, 1)))
        cput("g_mlstm", np.tile(np.asarray(inp["g_mlstm"][0], f32)[None, :], (128, 1)))
        gq = np.asarray(inp["g_na_q"][0], f32)
        gk = np.asarray(inp["g_na_k"][0], f32)
        cput("g_naqk", np.tile(np.concatenate([np.tile(gq, 8), np.tile(gk, 8)])[None, :], (128, 1)))
        cput("g_memq", np.asarray(inp["g_mem_q"][0], f32)[:, None], rows=64)
        cput("g_memk", np.asarray(inp["g_mem_k"][0], f32)[:, None], rows=64)
        cput("ident", np.eye(128, dtype=f32))
        cput("ones", np.ones((128, 128), f32))
        cput("triA", tri, rows=64)
        cput("triB", tri.T, rows=64)
        cput("mask4A", np.tile(tri, (1, 4)), rows=64)
        cput("mask4B", np.tile(tri.T, (1, 4)), rows=64)
        a128 = np.arange(128)
        cput("sut128", (a128[:, None] > a128[None, :]).astype(f32))
        csts.append(cst)

        t2 = np.full((128, 14, 8, 64), NEG, f32)
        kj = np.arange(64)[:, None]
        cq = np.arange(64)[None, :]
        if half == 0:
            cg, kg = cq, kj
        else:
            cg, kg = 63 - cq, 63 - kj
        cs = np.clip(cg - 8, 0, 48)
        colok = (kg >= cs) & (kg <= cs + 15)
        dcg = kg - cg
        for slot in range(14):
            if slot < 10:
                d = 4 - slot
                edge = False
            else:
                d = slot - 10 + 3
                edge = True
            for ko in range(2):
                dr = d + ko
                drg = dr if half == 0 else -dr
                if edge:
                    ok = abs(drg) <= 7
                else:
                    ok = (-4 <= drg <= 3)
                if not ok:
                    continue
                idx = np.clip(dcg + 15, 0, 30)
                for h in range(8):
                    vals = rpb[h, drg + 7][idx]
                    t2[ko * 64:(ko + 1) * 64, slot, h, :] = np.where(colok, vals, NEG)
        t2s.append(t2.reshape(128, NT2))

    x = np.asarray(inp["x"], f32)
    mem = np.asarray(inp["mem"], f32)
    for b in range(NB):
        memT = np.ascontiguousarray(mem[b].T)
        for half in range(2):
            seq = x[b] if half == 0 else x[b][::-1]
            xo = np.zeros((D, XO_W), f32)
            xo[:, 256:] = seq[0:HALF + 256].T
            xp = np.zeros((D, XP_W), f32)
            xp[:, 2:] = seq[::-1][0:HALF + 2].T
            maps.append({"xo": xo, "xp": xp, "memT": memT, "wall": walls[half], "cst": csts[half],
                         "t2": t2s[half], "w_mem_kv": np.asarray(inp["w_mem_kv"][0], f32)})
    return maps


def _assemble(results):
    out = np.zeros((NB, S, D), np.float32)
    for b in range(NB):
        for half in range(2):
            o = results[2 * b + half]["outT"]
            loc = o.T
            if half == 0:
                out[b, 0:HALF] = loc
            else:
                out[b, HALF:] = loc[::-1]
    return out


DEBUG = False
RBYTES = 63488


def build_program(n_main_tiles=NT, do_pre=True, debug=False):
    nc = bass.Bass("TRN2", target_bir_lowering=False)
    P = Prog(nc)

    def din(name, shape):
        return P.dram(name, nc.dram_tensor(name, shape, F32, kind="ExternalInput").ap())

    xo_d = din("xo", [D, XO_W])
    xp_d = din("xp", [D, XP_W])
    memT_d = din("memT", [D, 256])
    wall_d = din("wall", [128, WTOT])
    cst_d = din("cst", [128, NCST])
    t2_d = din("t2", [128, NT2])
    wmkv_d = din("w_mem_kv", [D, 512])
    outT_d = P.dram("outT", nc.dram_tensor("outT", [D, HALF], F32, kind="ExternalOutput").ap())
    wbf_ap = nc.dram_tensor("wbf", [128, WTOT], BF16).ap()
    NGRP = 4
    wbf_grp = [P.dram("wbf%d" % i, wbf_ap) for i in range(NGRP)]
    cstart_ap = nc.dram_tensor("cstart", [NT, 64, 516], F32).ap()
    cstart_d = [P.dram("cstart%d" % i, cstart_ap[i]) for i in range(NT)]
    dbg = {}
    if debug:
        for nm, rows in (("d_hmem", 256), ("d_hna", 256), ("d_hm", 512), ("d_y", 1024), ("d_x1", 1024)):
            dbg[nm] = P.dram(nm, nc.dram_tensor(nm, [rows, HALF], F32, kind="ExternalOutput").ap())
        dbg["d_c"] = P.dram("d_c", nc.dram_tensor("d_c", [64, 516], F32, kind="ExternalOutput").ap())
        dbg["d_misc"] = P.dram("d_misc", nc.dram_tensor("d_misc", [128, 4096], F32, kind="ExternalOutput").ap())

    xo_v = V(xo_d, xo_d.ap.rearrange("(kc p) t -> p kc t", p=128))
    xp_v = V(xp_d, xp_d.ap.rearrange("(kc p) t -> p kc t", p=128))

    cst = P.sbuf("cst", [128, NCST], F32)

    def cf(name, rows=128):
        o, l = CSTOFF[name]
        return cst[0:rows, o:o + l]

    identb = P.sbuf("identb", [128, 128], BF16)
    onesb = P.sbuf("onesb", [128, 128], BF16)
    t2 = P.sbuf("t2", [128, NT2], BF16)
    t2v = t2.v().rearrange("p (s h c) -> p s h c", s=14, h=8, c=64)
    ring = [P.sbuf("ring%d" % i, [128, SLOT], BF16) for i in range(NSLOT)]
    xs = [P.sbuf("xs%d" % i, [128, 8, 258], F32) for i in range(2)]
    sq = P.sbuf("sq", [128, 8, 258], BF16)
    rstd = P.sbuf("rstd", [128, 512], F32)
    hT = P.sbuf("hT", [128, 8, 516], BF16)
    hh = P.sbuf("hh", [128, 8, 256], BF16)
    kmT = P.sbuf("kmT", [64, 4, 256], BF16)
    vm_aug = P.sbuf("vm_aug", [128, 2, 4, 65], BF16)
    Cst = [P.sbuf("C%d" % i, [64, 516], F32) for i in range(2)]
    Cb = [P.sbuf("Cb%d" % i, [64, 516], BF16) for i in range(2)]
    hmT = P.sbuf("hmT", [128, 4, 512], BF16)
    hnaT = P.sbuf("hnaT", [128, 2, 512], BF16)
    hmemT = P.sbuf("hmemT", [64, 4, 512], BF16)
    yTs = [P.sbuf("yT%d" % i_, [128, 512], BF16) for i_ in range(8)]
    yaccs = [P.sbuf("yacc%d" % i_, [128, 512], F32) for i_ in range(2)]
    tmpys = [P.sbuf("tmpy%d" % i_, [128, 512], F32) for i_ in range(2)]
    gss = [P.sbuf("gs%d" % i_, [128, 512], F32) for i_ in range(2)]
    sqxs = [P.sbuf("sqx%d" % i_, [128, 512], BF16) for i_ in range(2)]
    rls = [P.sbuf("rl%d" % i_, [128, 512], F32) for i_ in range(2)]
    ost = [P.sbuf("ost%d" % i, [128, 512], F32) for i in range(2)]
    dstage = P.sbuf("dstage", [128, 512], F32) if debug else None
    R = P.sbuf("R", [128, RBYTES // 2], BF16)
    region = []

    def carve(name, off, parts, shape, dt):
        n = 1
        for s_ in shape:
            n *= s_
        nb = n * (4 if dt == F32 else 2)
        assert off % 4 == 0 and off + nb <= RBYTES, (name, off, nb)
        ap = R.ap[0:parts, off // 2:(off + nb) // 2]
        if dt == F32:
            ap = ap.bitcast(F32)
        if len(shape) == 2:
            ap = ap.rearrange("p (a b) -> p a b", a=shape[0], b=shape[1])
        elif len(shape) == 3:
            ap = ap.rearrange("p (a b c) -> p a b c", a=shape[0], b=shape[1], c=shape[2])
        return TT(name, ap, off, off + nb, region)

    class Lay:
        def __init__(self):
            self.o = 0

        def __call__(self, name, parts, shape, dt):
            t = carve(name, self.o, parts, shape, dt)
            self.o = t.hi + (-t.hi) % 4
            return t

    L = Lay()
    qnaT = L("qnaT", 32, [8, 512], BF16)
    knaT = L("knaT", 32, [8, 1024], BF16)
    vna = L("vna", 128, [8, 8, 33], BF16)
    sqns = [L("sqn%d" % i_, 128, [16, 32], F32) for i_ in range(2)]
    tmpns = [L("tmpn%d" % i_, 128, [16, 32], F32) for i_ in range(2)]
    qkns = [L("qkn%d" % i_, 128, [16, 32], BF16) for i_ in range(2)]
    ssns = [L("ssn%d" % i_, 128, [16], F32) for i_ in range(2)]
    hna_tok = L("hna_tok", 128, [8, 32], BF16)
    rdn = L("rdn", 128, [8], F32)
    mem_off = L.o
    PTn = L("PTn", 128, [5, 8, 128], BF16)
    sbns = [L("sbn%d" % i_, 128, [8, 128], F32) for i_ in range(2)]
    L = Lay()
    L.o = mem_off
    memq = L("memq", 64, [4, 512], BF16)
    PTm = L("PTm", 128, [4, 2, 512], BF16)
    hmem_toks = [L("hmem_tok%d" % i_, 128, [4, 64], BF16) for i_ in range(2)]
    sqfs = [L("sqf%d" % i_, 64, [512], F32) for i_ in range(2)]
    rss = [L("rs%d" % i_, 64, [512], F32) for i_ in range(2)]
    rdms = [L("rdm%d" % i_, 128, [4], F32) for i_ in range(2)]
    L = Lay()
    qkT = L("qkT", 64, [8, 512], BF16)
    ktok = L("ktok", 64, [8, 256], BF16)
    vaug = L("vaug", 64, [8, 4, 129], BF16)
    osig = L("osig", 64, [8, 512], BF16)
    hacc = L("hacc", 64, [8, 512], F32)
    zqs = [L("zq%d" % i_, 64, [516], F32) for i_ in range(2)]
    accqs = [L("accq%d" % i_, 64, [512], F32) for i_ in range(2)]
    zq, accq = zqs[0], accqs[0]
    Gt = L("Gt", 64, [8, 16], F32)
    Et = L("Et", 64, [8, 8], F32)
    Lt = L("Lt", 64, [8, 8], F32)
    tmpu = L("tmpu", 64, [8, 8], F32)
    Ut = L("Ut", 64, [8, 8], F32)
    Wt = L("Wt", 64, [8, 8], F32)
    DECt = L("DECt", 64, [8, 8], F32)
    IWt = L("IWt", 64, [8, 8], F32)
    PMs = [L("PM%d" % i_, 64, [256], BF16) for i_ in range(2)]
    vus = [L("vu%d" % i_, 64, [4, 129], BF16) for i_ in range(2)]
    t1s = [L("t1%d" % i_, 64, [4], F32) for i_ in range(2)]
    rrs = [L("rr%d" % i_, 64, [4], F32) for i_ in range(2)]
    tmphs = [L("tmph%d" % i_, 64, [2, 128], F32) for i_ in range(2)]
    ssh = L("ssh", 64, [4], F32)
    hm_tok = L("hm_tok", 64, [512], BF16)
    L = Lay()
    PB = []
    for i_ in range(2):
        PB.append(dict(
            hT=L("p_hT%d" % i_, 128, [8, 516], BF16),
            zk=L("zk%d" % i_, 128, [2, 516], F32),
            kTp=L("kTp%d" % i_, 128, [2, 512], BF16),
            accp=L("accp%d" % i_, 128, [512], F32),
            ktok128=L("ktok128%d" % i_, 128, [4, 256], BF16),
            vaug128=L("vaug128%d" % i_, 128, [4, 4, 129], BF16),
            vws=L("vws%d" % i_, 128, [4, 4, 129], BF16),
            Gp=L("Gp%d" % i_, 128, [4, 8], F32),
            Ep=L("Ep%d" % i_, 128, [4, 4], F32),
            Lp=L("Lp%d" % i_, 128, [4, 4], F32),
            WSp=L("WSp%d" % i_, 128, [4, 4], F32),
            DECp=L("DECp%d" % i_, 64, [4], F32)))
    L = Lay()
    aTs = [L("aT%d" % i_, 128, [4, 512], BF16) for i_ in range(8)]
    hnTs = [L("hnT%d" % i_, 128, [512], BF16) for i_ in range(8)]
    x1Ts = [L("x1T%d" % i_, 128, [512], F32) for i_ in range(8)]

    pb = [P.psum("pb%d" % i, [128, 512], F32) for i in range(8)]
    rot = [0]
    rotn = [2]

    def nb():
        rot[0] = (rot[0] + 1) % rotn[0]
        return pb[rot[0]]

    def pbf(i):
        return V(pb[i], pb[i].ap.bitcast(BF16))

    P.dma(cst.v(), cst_d.v(), q="sp")
    P.dma(t2.v(), t2_d.v(), q="pool")
    P.ts(t2.v(), t2.v(), 32.0 ** 0.5, ALU.mult)
    P.copy(identb.v(), cf("ident"))
    P.copy(onesb.v(), cf("ones"))
    ones64f = cst[0:64, CSTOFF["ones"][0]:CSTOFF["ones"][0] + 64]
    triAf = cf("triA", 64)
    triBf = cf("triB", 64)
    g_mix8 = cf("g_mix8")
    g_ffn8 = cf("g_ffn8")
    g_mem8 = cf("g_mem8")

    seg_ctr = [0]

    xq = ["sp"]

    def rms_load(src, n):
        xsb = xs[seg_ctr[0] % 2]
        seg_ctr[0] += 1
        P.dma(xsb[:, :, 0:n], src, q=xq[0])
        return xsb

    def rmsnorm_seg(src, g8, dst, n, xsb=None):
        if xsb is None:
            xsb = rms_load(src, n)
        P.act(sq[:, :, 0:n], xsb[:, :, 0:n], AF.Square)
        for kc in range(8):
            P.mm(pb[7][:, 0:n], onesb.v(), sq[:, kc, 0:n], start=(kc == 0), stop=(kc == 7))
        P.act(rstd[:, 0:n], pb[7][:, 0:n], AF.Sqrt, bias=EPS, scale=1.0 / D)
        P.recip(rstd[:, 0:n], rstd[:, 0:n])
        for kc in range(8):
            P.stt(dst[:, kc, :], xsb[:, kc, 0:n], g8[:, kc:kc + 1], rstd[:, 0:n], ALU.mult, ALU.mult)

    def bc_last(v, shape):
        return V(v.t, v.ap.rearrange("p (a o) -> p a o", o=1).to_broadcast(list(shape)))

    memT_v = V(memT_d, memT_d.ap.rearrange("(kc p) t -> p kc t", p=128))
    rmsnorm_seg(memT_v, g_mem8, hh[:, :, 0:256], 256)
    wkv = ring[0][:, 0:4096].rearrange("p (kc n) -> p kc n", kc=8)
    P.dma(wkv, V(wmkv_d, wmkv_d.ap.rearrange("(kc p) n -> p kc n", p=128)), q="pool")
    for h in range(4):
        ps = nb()
        for kc in range(8):
            P.mm(ps[0:64, 0:256], wkv[:, kc, h * 64:(h + 1) * 64], hh[:, kc, :], start=(kc == 0), stop=(kc == 7))
        sqf, rs = sqfs[h % 2], rss[h % 2]
        P.act(sqf[:, 0:256], ps[0:64, 0:256], AF.Square)
        ps2 = nb()
        P.mm(ps2[0:64, 0:256], ones64f, sqf[:, 0:256])
        P.act(rs[:, 0:256], ps2[0:64, 0:256], AF.Sqrt, bias=EPS, scale=1.0 / 64)
        P.recip(rs[:, 0:256], rs[:, 0:256])
        P.stt(kmT[:, h, :], ps[0:64, 0:256], cf("g_memk", 64), rs[:, 0:256], ALU.mult, ALU.mult)
    P.memset(vm_aug.v(), 1.0)
    for mc in range(2):
        ps = nb()
        for kc in range(8):
            P.mm(ps[:, 0:256], hh[:, kc, mc * 128:(mc + 1) * 128], wkv[:, kc, 256:512], start=(kc == 0), stop=(kc == 7))
        P.copy(vm_aug[:, mc, :, 0:64], ps[:, 0:256].rearrange("p (h d) -> p h d", h=4))

    def conv_chunks():
        for i, (nm, ln) in enumerate(CHUNKS):
            o, l = CHOFF[nm]
            g = wbf_grp[min(NGRP - 1, i * NGRP // len(CHUNKS))]
            P.dma(V(g, wbf_ap[:, o:o + l]), wall_d[:, o:o + l], q="pool", sem_tile=g)

    def chunk_group(nm):
        i = [c[0] for c in CHUNKS].index(nm)
        return wbf_grp[min(NGRP - 1, i * NGRP // len(CHUNKS))]

    wseq = []
    for _t in range(n_main_tiles):
        wseq += [c[0] for c in CHUNKS]
    wstate = {"next_load": 0, "next_use": 0, "released": 0}

    def _pump():
        while wstate["next_load"] < len(wseq) and wstate["next_load"] - NSLOT < wstate["released"]:
            i = wstate["next_load"]
            nm = wseq[i]
            o, l = CHOFF[nm]
            g = chunk_group(nm)
            P.dma(ring[i % NSLOT][:, 0:l], V(g, wbf_ap[:, o:o + l]), q="sp")
            wstate["next_load"] = i + 1

    def next_w(nm, hold=False):
        i = wstate["next_use"]
        assert wseq[i] == nm, (wseq[i], nm)
        if not hold:
            wstate["released"] = i
        _pump()
        assert wstate["next_load"] > i
        wstate["next_use"] = i + 1
        return ring[i % NSLOT]

    def gate_math(ndir):
        G4 = Gt.v().rearrange("p c (d e) -> p c d e", d=2, e=8)
        E4 = Et.v().rearrange("p c (d e) -> p c d e", d=2, e=4)
        L4 = Lt.v().rearrange("p c (d e) -> p c d e", d=2, e=4)
        U4 = tmpu.v().rearrange("p c (d e) -> p c d e", d=2, e=4)
        nd = ndir
        P.act(E4[:, :, 0:nd, :], G4[:, :, 0:nd, 4:8], AF.Exp, scale=-1.0)
        P.act(L4[:, :, 0:nd, :], E4[:, :, 0:nd, :], AF.Ln, bias=1.0)
        psb = pb[5]
        for c in range(NCH):
            P.mm(psb[0:64, c * 24:c * 24 + 4], triAf, Lt[:, c, 0:4])
            if nd == 2:
                P.mm(psb[0:64, c * 24 + 4:c * 24 + 8], triBf, Lt[:, c, 4:8])
            P.mm(psb[0:64, c * 24 + 8:c * 24 + 8 + 4 * nd], ones64f, Lt[:, c, 0:4 * nd])
        psv = psb[0:64, 0:NCH * 24].rearrange("p (c e) -> p c e", c=NCH)
        bp4 = psv[:, :, 0:8].rearrange("p c (d e) -> p c d e", d=2)
        P.tt(U4[:, :, 0:nd, :], G4[:, :, 0:nd, 0:4], bp4[:, :, 0:nd, :], ALU.add)
        P.act(Ut[:, :, 0:4 * nd], tmpu[:, :, 0:4 * nd], AF.Exp)
        P.act(Wt[:, :, 0:4 * nd], psv[:, :, 0:4 * nd], AF.Exp, scale=-1.0, bias=LN8)
        if nd == 2:
            P.act(IWt.v(), psv[:, :, 0:8], AF.Exp, scale=1.0, bias=-LN8)
        P.act(DECt[:, :, 0:4 * nd], psv[:, :, 8:8 + 4 * nd], AF.Exp, scale=-1.0)

    def state_update(c, x, pdb):
        Cx = Cst[x]
        for bk in range(2):
            P.tt(Cx[:, bk * 258:(bk + 1) * 258], pdb[bk][0:64, 0:258], Cx[:, bk * 258:(bk + 1) * 258], ALU.add)
        C3 = Cx.v().rearrange("p (h e) -> p h e", h=4)
        P.tt(C3, C3, bc_last(DECt[:, c, x * 4:(x + 1) * 4], [64, 4, 129]), ALU.mult)

    pre_ctr = [0]

    def dmisc0(col, src):
        rows, n = src.ap.shape[0], src.ap.shape[1]
        P.copy(dstage[0:rows, 0:n], src)
        P.dma(dbg["d_misc"][0:rows, col:col + n], dstage[0:rows, 0:n], q="pool")

    def gen_pre(ch, src_v, col0s, convw, gsel, Cx, save, wq, wv, wg):
        B_ = PB[ch]
        hT, zk, kTp, accp, ktok128, vaug128, vws = (B_[k_] for k_ in ("hT", "zk", "kTp", "accp", "ktok128", "vaug128", "vws"))
        Gp, Ep, Lp, WSp, DECp = (B_[k_] for k_ in ("Gp", "Ep", "Lp", "WSp", "DECp"))
        mybanks = [pb[2 * ch], pb[2 * ch + 1]]
        cnt = [0]

        def mb():
            cnt[0] += 1
            return mybanks[cnt[0] % 2]

        gcol = ch * 64
        for ti, col0 in enumerate(col0s):
            if save:
                P.dma(cstart_d[ti].v(), Cx.v(), q="sp")
            for sg in range(2):
                rmsnorm_seg(src_v[:, :, col0 + sg * 258:col0 + (sg + 1) * 258], g_mix8, hT[:, :, sg * 258:(sg + 1) * 258], 258)
                yield
            for blk in range(2):
                for sg in range(2):
                    ps = mb()
                    for kc in range(8):
                        P.mm(ps[:, 0:258], wq[:, kc, 256 + blk * 128:256 + (blk + 1) * 128], hT[:, kc, sg * 258:(sg + 1) * 258],
                             start=(kc == 0), stop=(kc == 7))
                    P.copy(zk[:, blk, sg * 258:(sg + 1) * 258], ps[:, 0:258], eng="act")
                yield
                P.ts(accp.v(), zk[:, blk, 0:512], convw[:, blk * 5:blk * 5 + 1], ALU.mult)
                for j in range(1, 5):
                    P.stt(accp.v(), zk[:, blk, j:j + 512], convw[:, blk * 5 + j:blk * 5 + j + 1], accp.v(), ALU.mult, ALU.add)
                P.act(kTp[:, blk, :], accp.v(), AF.Silu, bias=cf("convb_p")[:, blk:blk + 1])
                yield
            for tb in range(4):
                ps = mb()
                for kc in range(8):
                    P.mm(ps.v(), hT[:, kc, 2 + tb * 128:2 + (tb + 1) * 128], wv[:, kc, :], start=(kc == 0), stop=(kc == 7))
                P.copy(vaug128[:, tb, :, 0:128], ps.v().rearrange("p (h e) -> p h e", h=4), eng="act")
                for kc in range(8):
                    P.mm(pb[6][:, gcol + tb * 16:gcol + (tb + 1) * 16], hT[:, kc, 2 + tb * 128:2 + (tb + 1) * 128], wg[:, kc, :],
                         start=(kc == 0), stop=(kc == 7))
                pt = pbf(4 + ch)
                for blk in range(2):
                    P.transpose(pt[:, (tb % 2) * 256 + blk * 128:(tb % 2) * 256 + (blk + 1) * 128], kTp[:, blk, tb * 128:(tb + 1) * 128], identb.v())
                P.copy(ktok128[:, tb, :], pt[:, (tb % 2) * 256:(tb % 2) * 256 + 256])
                yield
            gb = cf("gbias")
            for tb in range(4):
                P.tt(Gp[:, tb, :], pb[6][:, gcol + tb * 16 + gsel * 8:gcol + tb * 16 + gsel * 8 + 8], gb[:, gsel * 8:gsel * 8 + 8], ALU.add)
            P.act(Ep.v(), Gp[:, :, 4:8], AF.Exp, scale=-1.0)
            P.act(Lp.v(), Ep.v(), AF.Ln, bias=1.0)
            yield
            pss_ = pb[7]
            pc = 32 * ch + 320
            sut = cf("sut128")
            onesf = cf("ones")
            for tb in range(4):
                P.mm(pss_[:, pc + tb * 4:pc + (tb + 1) * 4], sut, Lp[:, tb, :], start=True, stop=(tb == 3))
                for t2_ in range(tb + 1, 4):
                    P.mm(pss_[:, pc + tb * 4:pc + (tb + 1) * 4], onesf, Lp[:, t2_, :], start=False, stop=(t2_ == 3))
            for tb in range(4):
                P.mm(pss_[0:64, pc + 16:pc + 20], onesf[:, 0:64], Lp[:, tb, :], start=(tb == 0), stop=(tb == 3))
            P.tt(WSp.v(), Gp[:, :, 0:4], pss_[:, pc:pc + 16].rearrange("p (t h) -> p t h", t=4), ALU.subtract)
            P.act(WSp.v(), WSp.v(), AF.Exp)
            P.act(DECp.v(), pss_[0:64, pc + 16:pc + 20], AF.Exp, scale=-1.0)
            yield
            for tb in range(4):
                P.tt(vws[:, tb], vaug128[:, tb], bc_last(WSp[:, tb, :], [128, 4, 129]), ALU.mult)
            yield
            pdb = mybanks
            for h in range(4):
                for tb in range(4):
                    P.mm(pdb[h // 2][0:64, (h % 2) * 129:(h % 2) * 129 + 129], ktok128[:, tb, h * 64:(h + 1) * 64], vws[:, tb, h, :],
                         start=(tb == 0), stop=(tb == 3))
            C3 = Cx.v().rearrange("p (h e) -> p h e", h=4)
            P.tt(C3, C3, bc_last(DECp.v(), [64, 4, 129]), ALU.mult)
            for bk in range(2):
                P.tt(Cx[:, bk * 258:(bk + 1) * 258], pdb[bk][0:64, 0:258], Cx[:, bk * 258:(bk + 1) * 258], ALU.add)
            yield
        if save:
            P.dma(cstart_d[len(col0s)].v(), Cx.v(), q="sp")

    if do_pre:
        o_, l_ = CHOFF["qk_m"]
        P.dma(ring[1][:, 0:l_], wall_d[:, o_:o_ + l_], q="pool")
        o_, l_ = CHOFF["v_m"]
        P.dma(ring[2][:, 0:l_], wall_d[:, o_:o_ + l_], q="pool")
        o_, l_ = CHOFF["g16"]
        P.dma(ring[3][:, 0:l_], wall_d[:, o_:o_ + l_], q="pool")
    conv_chunks()
    if do_pre:
        wq_p = ring[1][:, 0:4096].rearrange("p (kc n) -> p kc n", kc=8)
        wv_p = ring[2][:, 0:4096].rearrange("p (kc n) -> p kc n", kc=8)
        wg_p = ring[3][:, 0:128].rearrange("p (kc n) -> p kc n", kc=8)
        for B_ in PB:
            P.memset(B_["vaug128"][:, :, :, 128:129], 1.0)
        P.memset(Cst[0].v(), 0.0)
        P.memset(Cst[1].v(), 0.0)
        gens = [gen_pre(0, xp_v, [t * TILE for t in range(NT)], cf("convw_pp"), 1, Cst[1], False, wq_p, wv_p, wg_p),
                gen_pre(1, xo_v, [256 + t * TILE - 2 for t in range(NT - 1)], cf("convw_po"), 0, Cst[0], True, wq_p, wv_p, wg_p)]
        while gens:
            for g_ in list(gens):
                try:
                    next(g_)
                except StopIteration:
                    gens.remove(g_)
        if debug:
            P.dma(dbg["d_c"].v(), Cst[1].v(), q="pool")
    else:
        P.memset(Cst[1].v(), 0.0)
        P.memset(Cst[0].v(), 0.0)
        for t in range(NT):
            P.dma(cstart_d[t].v(), Cst[0].v(), q="pool")
    P.copy(Cb[1].v(), Cst[1].v())
    xq[0] = "pool"
    rotn[0] = 4

    def dump(nm, view_rows, tile_i, src):
        rows = src.ap.shape[0]
        P.copy(dstage[0:rows, :], src)
        P.dma(dbg[nm][view_rows, tile_i * TILE:(tile_i + 1) * TILE], dstage[0:rows, :], q="pool")

    def dmisc(col, src):
        rows, n = src.ap.shape[0], src.ap.shape[1]
        P.copy(dstage[0:rows, 0:n], src)
        P.dma(dbg["d_misc"][0:rows, col:col + n], dstage[0:rows, 0:n], q="pool")

    hT_for = [None]

    def center_load(tile_i):
        c0 = 256 + tile_i * TILE
        return [rms_load(xo_v[:, :, c0 - 2 + sg * 258:c0 - 2 + (sg + 1) * 258], 258) for sg in range(2)]

    def center_norm(tile_i, bufs=None):
        c0 = 256 + tile_i * TILE
        for sg in range(2):
            rmsnorm_seg(xo_v[:, :, c0 - 2 + sg * 258:c0 - 2 + (sg + 1) * 258], g_mix8, hT[:, :, sg * 258:(sg + 1) * 258], 258,
                        xsb=(bufs[sg] if bufs else None))
        hT_for[0] = tile_i

    def main_tile(tile_i):
        c0 = 256 + tile_i * TILE
        hc = hT[:, :, 2:514]
        if hT_for[0] != tile_i:
            center_norm(tile_i)

        Wq = next_w("memq")[:, 0:2048].rearrange("p (kc n) -> p kc n", kc=8)
        Wnq = next_w("na_q", hold=True)[:, 0:2048].rearrange("p (kc n) -> p kc n", kc=8)
        Wnkv = next_w("na_kv", hold=True)[:, 0:4096].rearrange("p (kc n) -> p kc n", kc=8)

        def gen_mem():
            mcnt = [0]

            def mbk():
                mcnt[0] += 1
                return pb[mcnt[0] % 2]

            for h in range(4):
                sqf, rs = sqfs[h % 2], rss[h % 2]
                ps = mbk()
                for kc in range(8):
                    P.mm(ps[0:64, :], Wq[:, kc, h * 64:(h + 1) * 64], hc[:, kc, :], start=(kc == 0), stop=(kc == 7))
                P.act(sqf.v(), ps[0:64, :], AF.Square)
                ps2 = mbk()
                P.mm(ps2[0:64, :], ones64f, sqf.v())
                P.act(rs.v(), ps2[0:64, :], AF.Sqrt, bias=EPS, scale=1.0 / 64)
                P.recip(rs.v(), rs.v())
                P.stt(memq[:, h, :], ps[0:64, :], cf("g_memq", 64), rs.v(), ALU.mult, ALU.mult)
                yield
                for mc in range(2):
                    ps3 = mbk()
                    P.mm(ps3.v(), kmT[:, h, mc * 128:(mc + 1) * 128], memq[:, h, :])
                    P.act(PTm[:, h, mc, :], ps3.v(), AF.Exp, scale=0.125)
                yield
            for tb in range(4):
                hmem_tok, rdm = hmem_toks[tb % 2], rdms[tb % 2]
                pv = pb[6]
                for h in range(4):
                    for mc in range(2):
                        P.mm(pv[:, h * 65:(h + 1) * 65], PTm[:, h, mc, tb * 128:(tb + 1) * 128], vm_aug[:, mc, h, :],
                             start=(mc == 0), stop=(mc == 1))
                pv3 = pv[:, 0:260].rearrange("p (h e) -> p h e", h=4)
                P.recip(rdm.v(), pv3[:, :, 64])
                P.tt(hmem_tok.v(), pv3[:, :, 0:64], bc_last(rdm.v(), [128, 4, 64]), ALU.mult)
                pt = pbf(7)
                for h in range(4):
                    P.transpose(pt[0:64, h * 128:(h + 1) * 128], hmem_tok[:, h, :], identb.v())
                P.copy(hmemT[:, :, tb * 128:(tb + 1) * 128], pt[0:64, 0:512].rearrange("p (h t) -> p h t", h=4), eng="act")
                yield

        gq3 = cf("g_naqk").rearrange("p (h d) -> p h d", h=16)

        def gen_na(par):
            sqn, tmpn, qkn, ssn = sqns[par], tmpns[par], qkns[par], ssns[par]
            for jb in (2 + par, 4 + par, 0 + par, 6 + par):
                own = 2 <= jb <= 5
                if par == 0 and jb == 0:
                    rmsnorm_seg(xo_v[:, :, c0 - 256:c0], g_mix8, hh.v(), 256)
                if par == 0 and jb == 6:
                    rmsnorm_seg(xo_v[:, :, c0 + 512:c0 + 768], g_mix8, hh.v(), 256)
                if own:
                    hsrc = hT[:, :, 2 + (jb - 2) * 128:2 + (jb - 1) * 128]
                else:
                    hsrc = hh[:, :, (jb % 2) * 128:(jb % 2 + 1) * 128]
                pkv = pb[2 + par]
                for kc in range(8):
                    P.mm(pkv.v(), hsrc[:, kc, :], Wnkv[:, kc, :], start=(kc == 0), stop=(kc == 7))
                P.act(sqn[:, 0:8, :], pkv[:, 0:256].rearrange("p (h d) -> p h d", h=8), AF.Square)
                nh = 8
                if own:
                    pq = pb[4][:, par * 256:(par + 1) * 256]
                    for kc in range(8):
                        P.mm(pq, hsrc[:, kc, :], Wnq[:, kc, :], start=(kc == 0), stop=(kc == 7))
                    P.act(sqn[:, 8:16, :], pq.rearrange("p (h d) -> p h d", h=8), AF.Square)
                    nh = 16
                P.reduce(ssn[:, 0:nh], sqn[:, 0:nh, :], ALU.add)
                P.act(ssn[:, 0:nh], ssn[:, 0:nh], AF.Sqrt, bias=EPS, scale=1.0 / 32)
                P.recip(ssn[:, 0:nh], ssn[:, 0:nh])
                yield
                P.tt(tmpn[:, 0:8, :], pkv[:, 0:256].rearrange("p (h d) -> p h d", h=8), bc_last(ssn[:, 0:8], [128, 8, 32]), ALU.mult)
                P.tt(qkn[:, 0:8, :], tmpn[:, 0:8, :], gq3[:, 8:16, :], ALU.mult)
                if own:
                    P.tt(tmpn[:, 8:16, :], pq.rearrange("p (h d) -> p h d", h=8), bc_last(ssn[:, 8:16], [128, 8, 32]), ALU.mult)
                    P.tt(qkn[:, 8:16, :], tmpn[:, 8:16, :], gq3[:, 0:8, :], ALU.mult)
                P.copy(vna[:, jb, :, 0:32], pkv[:, 256:512].rearrange("p (h d) -> p h d", h=8), eng="act")
                yield
                pt = pbf(5 if par == 0 else 7)
                for h in range(8):
                    P.transpose(pt[0:32, h * 128:(h + 1) * 128], qkn[:, h, :], identb.v())
                P.copy(knaT[:, :, jb * 128:(jb + 1) * 128], pt[0:32, 0:1024].rearrange("p (h t) -> p h t", h=8))
                if own:
                    for h in range(8):
                        P.transpose(pt[0:32, h * 128:(h + 1) * 128], qkn[:, 8 + h, :], identb.v())
                    P.copy(qnaT[:, :, (jb - 2) * 128:(jb - 1) * 128], pt[0:32, 0:1024].rearrange("p (h t) -> p h t", h=8), eng="act")
                yield

        P.memset(vna[:, :, :, 32:33], 1.0)
        gens = [gen_mem(), gen_na(0), gen_na(1)]
        while gens:
            for g_ in list(gens):
                try:
                    next(g_)
                except StopIteration:
                    gens.remove(g_)
        if debug:
            for h in range(4):
                dump("d_hmem", slice(h * 64, (h + 1) * 64), tile_i, hmemT[:, h, :])

        for B in range(4):
            edge = (tile_i == 0 and B < 2)
            kbs = [2, 3, 4, 5] if edge else [B, B + 1, B + 2, B + 3, B + 4]
            for i, kb in enumerate(kbs):
                scb = [pb[4], pb[6]] if i % 2 == 0 else [pb[2], pb[3]]
                sbn = sbns[i % 2]
                d0 = 2 * kb - 4 - 2 * B
                if not edge:
                    s0 = 4 - d0
                    assert 0 <= s0 and s0 + 1 < 10, s0
                    for hb in range(2):
                        rb = t2v[:, s0:s0 + 2, hb * 4:(hb + 1) * 4, :].rearrange("p q h c -> p h q c")
                        P.mm(scb[hb].v(), identb.v(), rb, start=True, stop=False)
                        for h4 in range(4):
                            h = hb * 4 + h4
                            P.mm(scb[hb][:, h4 * 128:(h4 + 1) * 128], knaT[:, h, kb * 128:(kb + 1) * 128],
                                 qnaT[:, h, B * 128:(B + 1) * 128], start=False, stop=(h4 == 3))
                        P.act(PTn[:, i, hb * 4:(hb + 1) * 4, :], scb[hb].v().rearrange("p (h q) -> p h q", h=4), AF.Exp,
                              scale=32.0 ** -0.5)
                    continue
                for h in range(8):
                    P.mm(scb[h // 4][:, (h % 4) * 128:(h % 4 + 1) * 128], knaT[:, h, kb * 128:(kb + 1) * 128],
                         qnaT[:, h, B * 128:(B + 1) * 128])
                for qo in range(2):
                    d = d0 - qo
                    slot = 4 - d if d <= 2 else 10 + (d - 3)
                    assert 0 <= slot < 14, (slot, d)
                    for hb in range(2):
                        sc3 = scb[hb].v().rearrange("p (h q) -> p h q", h=4)
                        P.tt(sbn[:, hb * 4:(hb + 1) * 4, qo * 64:(qo + 1) * 64], sc3[:, :, qo * 64:(qo + 1) * 64],
                             t2v[:, slot, hb * 4:(hb + 1) * 4, :], ALU.add)
                P.act(PTn[:, i], sbn.v(), AF.Exp, scale=32.0 ** -0.5)
            pv = pb[7]
            for h in range(8):
                for i, kb in enumerate(kbs):
                    P.mm(pv[:, h * 33:(h + 1) * 33], PTn[:, i, h, :], vna[:, kb, h, :], start=(i == 0), stop=(i == len(kbs) - 1))
            pv3 = pv[:, 0:264].rearrange("p (h e) -> p h e", h=8)
            if debug and tile_i == NT - 1 and B == 0:
                dmisc(0, qnaT[:, 0, :])
                dmisc(512, knaT[:, 0, 0:512])
                dmisc(1024, knaT[:, 0, 512:1024])
                dmisc(2048, pv[:, 0:264])
                dmisc(2312, vna[:, 2].rearrange("p h e -> p (h e)"))
            P.recip(rdn.v(), pv3[:, :, 32])
            P.tt(hna_tok.v(), pv3[:, :, 0:32], bc_last(rdn.v(), [128, 8, 32]), ALU.mult)
            pt = pbf(5)
            hna2 = hna_tok.v().rearrange("p h d -> p (h d)")
            for k2 in range(2):
                P.transpose(pt[:, k2 * 128:(k2 + 1) * 128], hna2[:, k2 * 128:(k2 + 1) * 128], identb.v())
            P.copy(hnaT[:, :, B * 128:(B + 1) * 128], pt[:, 0:256].rearrange("p (k t) -> p k t", k=2))
        if debug:
            for k2 in range(2):
                dump("d_hna", slice(k2 * 128, (k2 + 1) * 128), tile_i, hnaT[:, k2, :])

        Wqk = next_w("qk_m")[:, 0:4096].rearrange("p (kc n) -> p kc n", kc=8)
        Wv = next_w("v_m", hold=True)[:, 0:4096].rearrange("p (kc n) -> p kc n", kc=8)
        P.memset(vaug[:, :, :, 128:129], 1.0)
        cwm = cf("convw_m", 64)
        for g in range(8):
            zq_, accq_ = zqs[g % 2], accqs[g % 2]
            for sg in range(2):
                ps = nb()
                for kc in range(8):
                    P.mm(ps[0:64, 0:258], Wqk[:, kc, g * 64:(g + 1) * 64], hT[:, kc, sg * 258:(sg + 1) * 258],
                         start=(kc == 0), stop=(kc == 7))
                P.copy(zq_[:, sg * 258:(sg + 1) * 258], ps[0:64, 0:258], eng="act")
            P.ts(accq_.v(), zq_[:, 0:512], cwm[:, g * 5:g * 5 + 1], ALU.mult)
            for j in range(1, 5):
                P.stt(accq_.v(), zq_[:, j:j + 512], cwm[:, g * 5 + j:g * 5 + j + 1], accq_.v(), ALU.mult, ALU.add)
            P.act(qkT[:, g, :], accq_.v(), AF.Silu, bias=cf("convb_m", 64)[:, g:g + 1])
            c = g
            ps = nb()
            for kc in range(8):
                P.mm(ps[0:64, :], hT[:, kc, 2 + c * 64:2 + (c + 1) * 64], Wv[:, kc, :], start=(kc == 0), stop=(kc == 7))
            P.copy(vaug[:, c, :, 0:128], ps[0:64, :].rearrange("p (h e) -> p h e", h=4), eng="act")
        Wo = next_w("o_m")[:, 0:4096].rearrange("p (kc n) -> p kc n", kc=8)
        for c in range(NCH):
            ps = nb()
            for kc in range(8):
                P.mm(ps[0:64, :], hT[:, kc, 2 + c * 64:2 + (c + 1) * 64], Wo[:, kc, :], start=(kc == 0), stop=(kc == 7))
            P.act(osig[:, c, :], ps[0:64, :], AF.Sigmoid)
        Wg = next_w("g16")[:, 0:128].rearrange("p (kc n) -> p kc n", kc=8)
        for c in range(NCH):
            for kc in range(8):
                P.mm(pb[6][0:64, c * 16:(c + 1) * 16], hT[:, kc, 2 + c * 64:2 + (c + 1) * 64], Wg[:, kc, :], start=(kc == 0), stop=(kc == 7))
        gb = cf("gbias", 64)
        for c in range(NCH):
            P.tt(Gt[:, c, :], pb[6][0:64, c * 16:(c + 1) * 16], gb, ALU.add)
        gate_math(2)
        for c in range(NCH):
            pt = pbf(4)
            for h in range(4):
                P.transpose(pt[0:64, h * 64:(h + 1) * 64], qkT[:, 4 + h, c * 64:(c + 1) * 64], identb[0:64, 0:64])
            P.copy(ktok[:, c, :], pt[0:64, 0:256])
        P.dma(Cst[0].v(), cstart_d[tile_i].v(), q="pool")
        P.copy(Cb[0].v(), Cst[0].v(), eng="act")

        vu2 = zqs[1].v().bitcast(BF16)

        def vubuf(x, par):
            if par == 0:
                return vus[x].v()
            return vu2[:, x * 516:(x + 1) * 516].rearrange("p (h e) -> p h e", h=4)

        def make_vu(c, x, par):
            P.tt(vubuf(x, par), vaug[:, c], bc_last(Ut[:, c, x * 4:(x + 1) * 4], [64, 4, 129]), ALU.mult, eng="pool")

        def st_mm(c, x):
            cs = slice(c * 64, (c + 1) * 64)
            st = pb[0] if x == 0 else pb[3]
            for h in range(4):
                P.mm(st[0:64, h * 64:(h + 1) * 64], qkT[:, 4 + h, cs], qkT[:, h, cs])

        def pm_op(c, x):
            st = pb[0] if x == 0 else pb[3]
            P.tt(PMs[x].v(), st[0:64, 0:256], cf("mask4A" if x == 0 else "mask4B", 64), ALU.mult)

        def front_a(c, x):
            st_mm(c, x)
            pm_op(c, x)

        def front_b(c, x, par):
            PM, vu = PMs[x], vubuf(x, par)
            cs = slice(c * 64, (c + 1) * 64)
            pob = [pb[1], pb[2]] if x == 0 else [pb[4], pb[6]]
            for h in range(4):
                po_h = pob[h // 2][0:64, (h % 2) * 129:(h % 2) * 129 + 129]
                P.mm(po_h, PM[:, h * 64:(h + 1) * 64], vu[:, h, :], start=True, stop=False)
                P.mm(po_h, qkT[:, h, cs], Cb[x][:, h * 129:(h + 1) * 129], start=False, stop=True)

        def norm_abs(c, x):
            t1 = t1s[x]
            pob = [pb[1], pb[2]] if x == 0 else [pb[4], pb[6]]
            for bk in range(2):
                po3 = pob[bk][0:64, 0:258].rearrange("p (h e) -> p h e", h=2)
                P.act(t1[:, bk * 2:(bk + 1) * 2].rearrange("p (h o) -> p h o", o=1), po3[:, :, 128:129], AF.Abs)

        def norm_dve(c, x):
            t1, rr = t1s[x], rrs[x]
            P.tt(t1.v(), t1.v(), IWt[:, c, x * 4:(x + 1) * 4], ALU.max)
            P.recip(rr.v(), t1.v())

        def norm_copy(c, x, first):
            rr, tmph = rrs[x], tmphs[x]
            pob = [pb[1], pb[2]] if x == 0 else [pb[4], pb[6]]
            h4 = hacc[:, c, :].rearrange("p (h e) -> p h e", h=4)
            for bk in range(2):
                for h2 in range(2):
                    h = bk * 2 + h2
                    src = pob[bk][0:64, h2 * 129:h2 * 129 + 128]
                    if first:
                        P.act(h4[:, h, :], src, AF.Copy, scale=rr[:, h:h + 1])
                    else:
                        P.act(tmph[:, h2, :], src, AF.Copy, scale=rr[:, h:h + 1])
                if not first:
                    P.tt(h4[:, bk * 2:(bk + 1) * 2, :], h4[:, bk * 2:(bk + 1) * 2, :], tmph.v(), ALU.add, eng="pool")

        def state_mm(c, x, par):
            vu = vubuf(x, par)
            pdv = pb[5] if x == 0 else pb[7]
            pdn = (pb[0] if x == 0 else pb[3])
            for h in range(4):
                P.mm(pdv[0:64, h * 128:(h + 1) * 128], ktok[:, c, h * 64:(h + 1) * 64], vu[:, h, 0:128])
            for h in range(4):
                P.mm(pdn[0:64, 256 + h:257 + h], ktok[:, c, h * 64:(h + 1) * 64], vu[:, h, 128:129])

        def state_dve(c, x):
            pdv = pb[5] if x == 0 else pb[7]
            pdn = (pb[0] if x == 0 else pb[3])
            C3 = Cst[x].v().rearrange("p (h e) -> p h e", h=4)
            P.tt(C3[:, :, 0:128], pdv[0:64, :].rearrange("p (h e) -> p h e", h=4), C3[:, :, 0:128], ALU.add)
            P.tt(C3[:, :, 128:129], pdn[0:64, 256:260].rearrange("p (h o) -> p h o", o=1), C3[:, :, 128:129], ALU.add)
            decb = bc_last(DECt[:, c, x * 4:(x + 1) * 4], [64, 4, 129])
            P.tt(Cb[x].v().rearrange("p (h e) -> p h e", h=4), C3, decb, ALU.mult)
            P.tt(C3, C3, decb, ALU.mult, eng="pool")

        def finalize(c, k):
            sqh_, hn_ = zqs[k % 2][:, 0:512], accqs[k % 2].v()
            hmt = hm_tok.v() if k % 2 == 0 else vus[0].v().rearrange("p h e -> p (h e)")[:, 0:512]
            ssh_ = t1s[k % 2]
            P.tt(sqh_, hacc[:, c, :], hacc[:, c, :], ALU.mult, eng="pool")
            P.reduce(ssh_.v(), sqh_.rearrange("p (h e) -> p h e", h=4), ALU.add)
            P.act(ssh_.v(), ssh_.v(), AF.Sqrt, bias=EPS, scale=1.0 / 128)
            P.recip(ssh_.v(), ssh_.v())
            P.tt(hn_.rearrange("p (h e) -> p h e", h=4), hacc[:, c, :].rearrange("p (h e) -> p h e", h=4),
                 bc_last(ssh_.v(), [64, 4, 128]), ALU.mult)
            P.tt(hn_, hn_, cf("g_mlstm", 64), ALU.mult)
            P.tt(hmt, hn_, osig[:, c, :], ALU.mult, eng="pool")
            pt = pbf(0 if k % 2 == 0 else 3)
            for k4 in range(4):
                P.transpose(pt[:, 768 + k4 * 64:768 + (k4 + 1) * 64], hmt[:, k4 * 128:(k4 + 1) * 128], identb[0:64, 0:64])
            P.copy(hmT[:, :, c * 64:(c + 1) * 64], pt[:, 768:1024].rearrange("p (k t) -> p k t", k=4), eng="act")

        make_vu(0, 0, 0)
        make_vu(NCH - 1, 1, 0)
        front_a(0, 0)
        front_a(NCH - 1, 1)
        for i_ in range(NCH):
            ca, cb = i_, NCH - 1 - i_
            par = i_ % 2
            first = i_ < NCH // 2
            if i_ + 1 < NCH:
                make_vu(ca + 1, 0, 1 - par)
                make_vu(cb - 1, 1, 1 - par)
            front_b(ca, 0, par)
            front_b(cb, 1, par)
            state_mm(ca, 0, par)
            state_mm(cb, 1, par)
            if i_ + 1 < NCH:
                st_mm(ca + 1, 0)
                st_mm(cb - 1, 1)
            norm_abs(ca, 0)
            norm_abs(cb, 1)
            norm_dve(ca, 0)
            norm_dve(cb, 1)
            if i_ + 1 < NCH:
                pm_op(ca + 1, 0)
                pm_op(cb - 1, 1)
            norm_copy(ca, 0, first)
            norm_copy(cb, 1, first)
            state_dve(ca, 0)
            state_dve(cb, 1)
        for k_, c_ in enumerate(range(NCH)):
            finalize(c_, k_)
        if debug:
            for k4 in range(4):
                dump("d_hm", slice(k4 * 128, (k4 + 1) * 128), tile_i, hmT[:, k4, :])

        for j in range(8):
            P.dma(x1Ts[j].v(), xo_v[:, j, c0:c0 + TILE], q="pool")

        for j in range(8):
            W = next_w("mrg%d" % j)
            Wgt = W[:, 0:3072].rearrange("p (kc n) -> p kc n", kc=8)
            Wpm = W[:, 3072:3584].rearrange("p (kc n) -> p kc n", kc=4)
            Wpn = W[:, 3584:3840].rearrange("p (kc n) -> p kc n", kc=2)
            Wpe = W[0:64, 3840:4352].rearrange("p (kc n) -> p kc n", kc=4)
            yacc = yaccs[j % 2]
            for b in range(3):
                gs = gss[(j * 3 + b) % 2]
                tmpy = tmpys[b % 2]
                pg = nb()
                for kc in range(8):
                    P.mm(pg.v(), Wgt[:, kc, b * 128:(b + 1) * 128], hc[:, kc, :], start=(kc == 0), stop=(kc == 7))
                P.act(gs.v(), pg.v(), AF.Sigmoid)
                pp = nb()
                if b == 0:
                    for k4 in range(4):
                        P.mm(pp.v(), Wpm[:, k4, :], hmT[:, k4, :], start=(k4 == 0), stop=(k4 == 3))
                    P.tt(yacc.v(), pp.v(), gs.v(), ALU.mult)
                elif b == 1:
                    for k2 in range(2):
                        P.mm(pp.v(), Wpn[:, k2, :], hnaT[:, k2, :], start=(k2 == 0), stop=(k2 == 1))
                    P.tt(tmpy.v(), pp.v(), gs.v(), ALU.mult)
                    P.tt(yacc.v(), yacc.v(), tmpy.v(), ALU.add)
                else:
                    for h in range(4):
                        P.mm(pp.v(), Wpe[:, h, :], hmemT[:, h, :], start=(h == 0), stop=(h == 3))
                    P.tt(tmpy.v(), pp.v(), gs.v(), ALU.mult)
                    P.tt(yTs[j].v(), yacc.v(), tmpy.v(), ALU.add)
            if debug:
                dump("d_y", slice(j * 128, (j + 1) * 128), tile_i, yTs[j].v())

        for j in range(8):
            if j % 4 == 0:
                Wout = next_w("out%d" % (j // 4))[:, 0:4096].rearrange("p (kc n) -> p kc n", kc=8)
            ps = nb()
            for kc in range(8):
                P.mm(ps.v(), Wout[:, kc, (j % 4) * 128:(j % 4 + 1) * 128], yTs[kc].v(), start=(kc == 0), stop=(kc == 7))
            P.tt(x1Ts[j].v(), x1Ts[j].v(), ps.v(), ALU.add)
            P.act(sqxs[j % 2].v(), x1Ts[j].v(), AF.Square)
            if j >= 1:
                P.mm(pb[7].v(), onesb.v(), sqxs[(j - 1) % 2].v(), start=(j == 1), stop=False)
            if debug:
                dump("d_x1", slice(j * 128, (j + 1) * 128), tile_i, x1Ts[j].v())
        P.mm(pb[7].v(), onesb.v(), sqxs[1].v(), start=False, stop=True)
        P.act(rstd.v(), pb[7].v(), AF.Sqrt, bias=EPS, scale=1.0 / D)
        P.recip(rstd.v(), rstd.v())
        for j in range(8):
            P.stt(hnTs[j].v(), x1Ts[j].v(), g_ffn8[:, j:j + 1], rstd.v(), ALU.mult, ALU.mult)

        nxt = (tile_i - 1 >= NT - n_main_tiles)
        if nxt:
            nbufs = center_load(tile_i - 1)

        for i in range(8):
            Wup = next_w("up%d" % i)[:, 0:4096].rearrange("p (kc n) -> p kc n", kc=8)
            for f4 in range(4):
                f = i * 4 + f4
                ps = nb()
                for kc in range(8):
                    P.mm(ps.v(), Wup[:, kc, f4 * 128:(f4 + 1) * 128], hnTs[kc].v(), start=(kc == 0), stop=(kc == 7))
                P.act(rls[f % 2].v(), ps.v(), AF.Relu)
                P.tt(aTs[i][:, f4, :], rls[f % 2].v(), rls[f % 2].v(), ALU.mult, eng="pool")
        if nxt:
            center_norm(tile_i - 1, nbufs)
        last = []
        for j in range(8):
            Wdn = next_w("dn%d" % j)[:, 0:4096].rearrange("p (f n) -> p f n", f=32)
            ps = nb()
            for f in range(32):
                P.mm(ps.v(), Wdn[:, f, :], aTs[f // 4][:, f % 4, :], start=(f == 0), stop=(f == 31))
            o_ = ost[j % 2]
            P.tt(o_.v(), ps.v(), x1Ts[j].v(), ALU.add)
            last.append(P.dma(outT_d[j * 128:(j + 1) * 128, tile_i * TILE:(tile_i + 1) * TILE], o_.v(), q="pool"))
        return last

    finals = []
    for tile_i in range(NT - 1, NT - 1 - n_main_tiles, -1):
        finals = main_tile(tile_i)
    fin = list(finals[-2:])
    for e in ("pool",):
        dmas = [op for op in P.ops[e] if op.is_dma]
        if dmas:
            fin.append(dmas[-1])
    for op in fin:
        op.needed = True
    P.emit(final_wait_ops=fin)
    P.close()
    return nc, P


_CACHE = {}


def kernel(**inputs):
    maps = _host_prep(inputs)
    if "nc" not in _CACHE:
        _CACHE["nc"] = build_program()[0]
    nc = _CACHE["nc"]
    res = run_bass_kernel_spmd(nc, maps, core_ids=list(range(8)))
    return _assemble(res.results)
```

```python
import math
from contextlib import ExitStack
import numpy as np
import concourse.bass as bass
import concourse.mybir as mybir
from concourse.bass_utils import run_bass_kernel_spmd

F32 = mybir.dt.float32
BF16 = mybir.dt.bfloat16
AF = mybir.ActivationFunctionType
ALU = mybir.AluOpType
AX = mybir.AxisListType

ENGS = ["pe", "act", "dve", "pool", "sp"]
SAME_DIST = 3
SEM_LIMIT = 30000


class V:
    __slots__ = ("t", "ap")

    def __init__(self, t, ap):
        self.t = t
        self.ap = ap

    def __getitem__(self, k):
        return V(self.t, self.ap[k])

    def rearrange(self, pattern_, **kw):
        return V(self.t, self.ap.rearrange(pattern_, **kw))

    def bc(self, shape):
        return V(self.t, self.ap.to_broadcast(list(shape)))

    def bitcast(self, dt):
        return V(self.t, self.ap.bitcast(dt))


class TT:
    def __init__(self, name, ap, lo=0, hi=0, region=None):
        self.name = name
        self.ap = ap
        self.w = None
        self.r = {}
        self.dsem = None
        self.dcount = 0
        self.overlaps = []
        self.lo, self.hi = lo, hi
        if region is not None:
            for o in region:
                if o.lo < hi and lo < o.hi:
                    o.overlaps.append(self)
                    self.overlaps.append(o)
            region.append(self)

    def __getitem__(self, k):
        return V(self, self.ap[k])

    def v(self):
        return V(self, self.ap)


class Op:
    __slots__ = ("eng", "idx", "fn", "waits", "needed", "is_dma", "semkey", "semval", "cnt")

    def __init__(self, eng, idx, fn, is_dma):
        self.eng = eng
        self.idx = idx
        self.fn = fn
        self.waits = []
        self.needed = False
        self.is_dma = is_dma
        self.semkey = None
        self.semval = None
        self.cnt = None


class Prog:
    def __init__(self, nc):
        self.nc = nc
        self.ops = {e: [] for e in ENGS}
        self.waited = {e: {} for e in ENGS}
        self.next_dsem = 0
        self.stack = ExitStack()

    def sbuf(self, name, shape, dt):
        h = self.stack.enter_context(self.nc.sbuf_tensor("sb_" + name, list(shape), dt))
        return TT(name, h[:])

    def psum(self, name, shape, dt):
        h = self.stack.enter_context(self.nc.psum_tensor("ps_" + name, list(shape), dt))
        return TT(name, h[:])

    def dram(self, name, ap):
        return TT("@" + name, ap)

    def _add(self, eng, fn, reads, writes, dma_tile=None):
        ops = self.ops[eng]
        op = Op(eng, len(ops), fn, dma_tile is not None)
        rt = [x.t for x in reads]
        wt = [x.t for x in writes]
        cand = []
        for t in rt:
            if t.w is not None:
                cand.append(t.w)
            for o in t.overlaps:
                if o.w is not None:
                    cand.append(o.w)
        for t in wt:
            if t.w is not None:
                cand.append(t.w)
            cand.extend(t.r.values())
            for o in t.overlaps:
                if o.w is not None:
                    cand.append(o.w)
                cand.extend(o.r.values())
        wd = self.waited[eng]
        deps = {}
        for d in cand:
            if d.is_dma:
                key, val = d.semkey, d.semval
            else:
                key, val = d.eng, d.idx
                if d.eng == eng and not op.is_dma and eng != "pool" and (eng == "pe" or op.idx - d.idx > SAME_DIST):
                    continue
            if wd.get(key, -1) >= val:
                continue
            if key not in deps or deps[key][1] < val:
                deps[key] = (d, val)
        for key, (d, val) in deps.items():
            wd[key] = val
            d.needed = True
            op.waits.append(d)
        if dma_tile is not None:
            if dma_tile.dsem is None:
                dma_tile.dsem = {}
                dma_tile.dcount = {}
            if eng not in dma_tile.dsem:
                dma_tile.dsem[eng] = self.next_dsem
                dma_tile.dcount[eng] = 0
                self.next_dsem += 1
            dma_tile.dcount[eng] += 1
            op.semkey = ("d", dma_tile.dsem[eng])
            op.semval = 16 * dma_tile.dcount[eng]
            rkey = op.semkey
        else:
            rkey = eng
        for t in rt:
            t.r[rkey] = op
        for t in wt:
            t.w = op
            t.r = {}
        ops.append(op)
        return op

    def mm(self, out, lhsT, rhs, start=True, stop=True):
        self._add("pe", lambda e: e.matmul(out.ap, lhsT.ap, rhs.ap, start=start, stop=stop),
                  [lhsT, rhs] + ([] if start else [out]), [out])

    def transpose(self, out, in_, ident):
        self._add("pe", lambda e: e.transpose(out.ap, in_.ap, ident.ap), [in_, ident], [out])

    def act(self, out, in_, func, bias=None, scale=None, accum_out=None):
        reads = [in_]
        kw = {}
        if bias is not None:
            if isinstance(bias, V):
                reads.append(bias)
                kw["bias"] = bias.ap
            else:
                kw["bias"] = bias
        if scale is not None:
            if isinstance(scale, V):
                reads.append(scale)
                kw["scale"] = scale.ap
            else:
                kw["scale"] = scale
        writes = [out]
        if accum_out is not None:
            writes.append(accum_out)
            kw["accum_out"] = accum_out.ap
        self._add("act", lambda e: e.activation(out.ap, in_.ap, func, **kw), reads, writes)

    def tt(self, out, in0, in1, op, eng="dve"):
        self._add(eng, lambda e: e.tensor_tensor(out.ap, in0.ap, in1.ap, op), [in0, in1], [out])

    def ts(self, out, in0, s1, op0, s2=None, op1=None, eng="dve"):
        reads = [in0]
        a1 = s1
        if isinstance(s1, V):
            reads.append(s1)
            a1 = s1.ap
        a2 = s2
        if isinstance(s2, V):
            reads.append(s2)
            a2 = s2.ap
        if op1 is None:
            self._add(eng, lambda e: e.tensor_scalar(out.ap, in0.ap, a1, None, op0), reads, [out])
        else:
            self._add(eng, lambda e: e.tensor_scalar(out.ap, in0.ap, a1, a2, op0, op1), reads, [out])

    def stt(self, out, in0, scalar, in1, op0, op1, eng="dve"):
        reads = [in0, in1]
        a = scalar
        if isinstance(scalar, V):
            reads.append(scalar)
            a = scalar.ap
        self._add(eng, lambda e: e.scalar_tensor_tensor(out.ap, in0.ap, a, in1.ap, op0, op1), reads, [out])

    def copy(self, out, in_, eng="dve"):
        if eng == "act":
            self._add(eng, lambda e: e.copy(out.ap, in_.ap), [in_], [out])
        else:
            self._add(eng, lambda e: e.tensor_copy(out.ap, in_.ap), [in_], [out])

    def memset(self, out, val, eng="dve"):
        self._add(eng, lambda e: e.memset(out.ap, val), [], [out])

    def reduce(self, out, in_, op, axis=AX.X, eng="dve"):
        self._add(eng, lambda e: e.tensor_reduce(out.ap, in_.ap, axis, op), [in_], [out])

    def recip(self, out, in_):
        self._add("dve", lambda e: e.reciprocal(out.ap, in_.ap), [in_], [out])

    def dma(self, out, in_, q="sp", sem_tile=None):
        if sem_tile is None:
            sem_tile = out.t if out.t.name[0] != "@" else in_.t
        op = self._add(q, None, [in_], [out], dma_tile=sem_tile)
        op.fn = (out.ap, in_.ap)
        return op

    def emit(self, final_wait_ops=()):
        nc = self.nc
        st = self.stack
        esems = {}
        for e in ENGS:
            c = 0
            for op in self.ops[e]:
                if not op.is_dma and op.needed:
                    c += 1
                    op.cnt = c
            esems[e] = [st.enter_context(nc.semaphore("s_%s_%d" % (e, i))) for i in range(c // SEM_LIMIT + 1)]
        dsems = [st.enter_context(nc.semaphore("d_%d" % i)) for i in range(self.next_dsem)]

        def semof(d):
            if d.is_dma:
                return dsems[d.semkey[1]], d.semval
            k = (d.cnt - 1) // SEM_LIMIT
            return esems[d.eng][k], d.cnt - k * SEM_LIMIT

        block = st.enter_context(nc.Block())
        engattr = {"pe": "tensor", "act": "scalar", "dve": "vector", "pool": "gpsimd", "sp": "sync"}
        prog = self

        def make(ename):
            def body(eng):
                for op in prog.ops[ename]:
                    for d in op.waits:
                        s, v = semof(d)
                        eng.wait_ge(s, v)
                    if op.is_dma:
                        o, i = op.fn
                        eng.dma_start(out=o, in_=i).then_inc(dsems[op.semkey[1]], 16)
                    else:
                        ins = op.fn(eng)
                        if op.needed:
                            s, v = semof(op)
                            ins.then_inc(s, 1)
                if ename == "sp":
                    for d in final_wait_ops:
                        s, v = semof(d)
                        eng.wait_ge(s, v)
            return body

        for e in ENGS:
            getattr(block, engattr[e])(make(e))

    def close(self):
        self.stack.close()


D = 1024
S = 8192
NB = 4
HALF = 4096
TILE = 512
NT = HALF // TILE
NCH = TILE // 64
DFF = 4096
EPS = 1e-6
NEG = -80.0
LN8 = math.log(0.125)

O_QK, O_V, O_O, O_I, O_F, O_NA, O_MQ, O_G = 0, 512, 1024, 1536, 1544, 1552, 2320, 2576
XO_W = 256 + HALF + 256
XP_W = HALF + 4

SLOT = 4352
NSLOT = 4


def _chunks():
    ch = [("memq", 8 * 256), ("na_q", 8 * 256), ("na_kv", 8 * 512), ("qk_m", 8 * 512), ("v_m", 8 * 512),
          ("o_m", 8 * 512), ("g16", 8 * 16)]
    ch += [("mrg%d" % j, SLOT) for j in range(8)]
    ch += [("out%d" % i, 8 * 512) for i in range(2)]
    ch += [("up%d" % i, 8 * 512) for i in range(8)]
    ch += [("dn%d" % j, 32 * 128) for j in range(8)]
    off = {}
    o = 0
    for n, l in ch:
        off[n] = (o, l)
        o += l
    return ch, off, o


CHUNKS, CHOFF, WTOT = _chunks()

CST_FIELDS = [("g_mix8", 8), ("g_ffn8", 8), ("g_mem8", 8), ("convw_m", 40), ("convb_m", 8),
              ("convw_po", 10), ("convw_pp", 10), ("convb_p", 2), ("gbias", 16), ("g_mlstm", 512),
              ("g_naqk", 512), ("g_memq", 1), ("g_memk", 1), ("ident", 128), ("ones", 128),
              ("triA", 64), ("triB", 64), ("mask4A", 256), ("mask4B", 256), ("sut128", 128)]
CSTOFF = {}
_o = 0
for _n, _l in CST_FIELDS:
    CSTOFF[_n] = (_o, _l)
    _o += _l
NCST = _o
NT2 = 14 * 8 * 64


def _kmaj(W):
    K, N = W.shape
    kc = K // 128
    return np.ascontiguousarray(W.reshape(kc, 128, N).transpose(1, 0, 2)).reshape(128, kc * N)


def _host_prep(inp):
    f32 = np.float32
    w_in = np.asarray(inp["w_in"][0], f32)
    maps = []
    wall0 = np.zeros((128, WTOT), f32)

    def put(name, arr):
        o, l = CHOFF[name]
        assert arr.shape == (128, l), (name, arr.shape, l)
        wall0[:, o:o + l] = arr

    put("memq", _kmaj(w_in[:, O_MQ:O_MQ + 256]))
    put("na_q", _kmaj(w_in[:, O_NA:O_NA + 256]))
    put("na_kv", _kmaj(w_in[:, O_NA + 256:O_NA + 768]))
    put("qk_m", _kmaj(w_in[:, O_QK:O_QK + 512]))
    put("v_m", _kmaj(w_in[:, O_V:O_V + 512]))
    put("o_m", _kmaj(w_in[:, O_O:O_O + 512]))
    wpm = np.asarray(inp["w_proj_mlstm"][0], f32)
    wpn = np.asarray(inp["w_proj_na"][0], f32)
    wpe = np.asarray(inp["w_proj_mem"][0], f32)
    for j in range(8):
        cols = np.concatenate([O_G + b * 1024 + j * 128 + np.arange(128) for b in range(3)])
        a = _kmaj(w_in[:, cols])
        b_ = _kmaj(wpm[:, j * 128:(j + 1) * 128])
        c_ = _kmaj(wpn[:, j * 128:(j + 1) * 128])
        d_ = np.zeros((128, 4 * 128), f32)
        d_[0:64] = wpe[:, j * 128:(j + 1) * 128].reshape(4, 64, 128).transpose(1, 0, 2).reshape(64, 512)
        put("mrg%d" % j, np.concatenate([a, b_, c_, d_], axis=1))
    w_out = np.asarray(inp["w_out"][0], f32)
    for i in range(2):
        put("out%d" % i, _kmaj(w_out[:, i * 512:(i + 1) * 512]))
    w_up = np.asarray(inp["w_up"][0], f32)
    for i in range(8):
        put("up%d" % i, _kmaj(w_up[:, i * 512:(i + 1) * 512]))
    w_dn = np.asarray(inp["w_down"][0], f32)
    for j in range(8):
        put("dn%d" % j, _kmaj(w_dn[:, j * 128:(j + 1) * 128]))

    conv_w = np.asarray(inp["conv_w"][0], f32)
    conv_b = np.asarray(inp["conv_b"][0], f32)
    b_i = np.asarray(inp["b_igate"][0], f32)
    b_f = np.asarray(inp["b_fgate"][0], f32)
    rpb = np.asarray(inp["rpb"][0], f32)
    ar = np.arange(64)
    tri = (ar[:, None] <= ar[None, :]).astype(f32)

    walls, csts, t2s = [], [], []
    for half in range(2):
        dA, dB = half, 1 - half
        wall = wall0.copy()
        gcols = np.concatenate([O_I + dA * 4 + np.arange(4), O_F + dA * 4 + np.arange(4),
                                O_I + dB * 4 + np.arange(4), O_F + dB * 4 + np.arange(4)])
        o, l = CHOFF["g16"]
        wall[:, o:o + l] = _kmaj(w_in[:, gcols])
        walls.append(wall)

        cst = np.zeros((128, NCST), f32)

        def cput(name, arr, rows=128):
            o, l = CSTOFF[name]
            cst[0:rows, o:o + l] = arr

        cput("g_mix8", np.asarray(inp["g_mix"][0], f32).reshape(8, 128).T)
        cput("g_ffn8", np.asarray(inp["g_ffn"][0], f32).reshape(8, 128).T)
        cput("g_mem8", np.asarray(inp["g_mem"][0], f32).reshape(8, 128).T)
        cw_own = conv_w if half == 0 else conv_w[::-1]
        cw_par = cw_own[::-1]
        cput("convw_m", cw_own.T.reshape(8, 64, 5).transpose(1, 0, 2).reshape(64, 40), rows=64)
        cput("convb_m", conv_b.reshape(8, 64).T, rows=64)
        cput("convw_po", cw_own.T[256:512].reshape(2, 128, 5).transpose(1, 0, 2).reshape(128, 10))
        cput("convw_pp", cw_par.T[256:512].reshape(2, 128, 5).transpose(1, 0, 2).reshape(128, 10))
        cput("convb_p", conv_b[256:512].reshape(2, 128).T)
        gb = np.concatenate([b_i[dA], b_f[dA], b_i[dB], b_f[dB]])
        cput("gbias", np.tile(gb[None, :], (128, 1)))
        cput("g_mlstm", np.tile(np.asarray(inp["g_mlstm"][0], f32)[None, :], (128, 1)))
        gq = np.asarray(inp["g_na_q"][0], f32)
        gk = np.asarray(inp["g_na_k"][0], f32)
        cput("g_naqk", np.tile(np.concatenate([np.tile(gq, 8), np.tile(gk, 8)])[None, :], (128, 1)))
        cput("g_memq", np.asarray(inp["g_mem_q"][0], f32)[:, None], rows=64)
        cput("g_memk", np.asarray(inp["g_mem_k"][0], f32)[:, None], rows=64)
        cput("ident", np.eye(128, dtype=f32))
        cput("ones", np.ones((128, 128), f32))
        cput("triA", tri, rows=64)
        cput("triB", tri.T, rows=64)
        cput("mask4A", np.tile(tri, (1, 4)), rows=64)
        cput("mask4B", np.tile(tri.T, (1, 4)), rows=64)
        a128 = np.arange(128)
        cput("sut128", (a128[:, None] > a128[None, :]).astype(f32))
        csts.append(cst)

        t2 = np.full((128, 14, 8, 64), NEG, f32)
        kj = np.arange(64)[:, None]
        cq = np.arange(64)[None, :]
        if half == 0:
            cg, kg = cq, kj
        else:
            cg, kg = 63 - cq, 63 - kj
        cs = np.clip(cg - 8, 0, 48)
        colok = (kg >= cs) & (kg <= cs + 15)
        dcg = kg - cg
        for slot in range(14):
            if slot < 10:
                d = 4 - slot
                edge = False
            else:
                d = slot - 10 + 3
                edge = True
            for ko in range(2):
                dr = d + ko
                drg = dr if half == 0 else -dr
                if edge:
                    ok = abs(drg) <= 7
                else:
                    ok = (-4 <= drg <= 3)
                if not ok:
                    continue
                idx = np.clip(dcg + 15, 0, 30)
                for h in range(8):
                    vals = rpb[h, drg + 7][idx]
                    t2[ko * 64:(ko + 1) * 64, slot, h, :] = np.where(colok, vals, NEG)
        t2s.append(t2.reshape(128, NT2))

    x = np.asarray(inp["x"], f32)
    mem = np.asarray(inp["mem"], f32)
    for b in range(NB):
        memT = np.ascontiguousarray(mem[b].T)
        for half in range(2):
            seq = x[b] if half == 0 else x[b][::-1]
            xo = np.zeros((D, XO_W), f32)
            xo[:, 256:] = seq[0:HALF + 256].T
            xp = np.zeros((D, XP_W), f32)
            xp[:, 2:] = seq[::-1][0:HALF + 2].T
            maps.append({"xo": xo, "xp": xp, "memT": memT, "wall": walls[half], "cst": csts[half],
                         "t2": t2s[half], "w_mem_kv": np.asarray(inp["w_mem_kv"][0], f32)})
    return maps


def _assemble(results):
    out = np.zeros((NB, S, D), np.float32)
    for b in range(NB):
        for half in range(2):
            o = results[2 * b + half]["outT"]
            loc = o.T
            if half == 0:
                out[b, 0:HALF] = loc
            else:
                out[b, HALF:] = loc[::-1]
    return out


DEBUG = False
RBYTES = 63488


def build_program(n_main_tiles=NT, do_pre=True, debug=False):
    nc = bass.Bass("TRN2", target_bir_lowering=False)
    P = Prog(nc)

    def din(name, shape):
        return P.dram(name, nc.dram_tensor(name, shape, F32, kind="ExternalInput").ap())

    xo_d = din("xo", [D, XO_W])
    xp_d = din("xp", [D, XP_W])
    memT_d = din("memT", [D, 256])
    wall_d = din("wall", [128, WTOT])
    cst_d = din("cst", [128, NCST])
    t2_d = din("t2", [128, NT2])
    wmkv_d = din("w_mem_kv", [D, 512])
    outT_d = P.dram("outT", nc.dram_tensor("outT", [D, HALF], F32, kind="ExternalOutput").ap())
    wbf_ap = nc.dram_tensor("wbf", [128, WTOT], BF16).ap()
    NGRP = 4
    wbf_grp = [P.dram("wbf%d" % i, wbf_ap) for i in range(NGRP)]
    cstart_ap = nc.dram_tensor("cstart", [NT, 64, 516], F32).ap()
    cstart_d = [P.dram("cstart%d" % i, cstart_ap[i]) for i in range(NT)]
    dbg = {}
    if debug:
        for nm, rows in (("d_hmem", 256), ("d_hna", 256), ("d_hm", 512), ("d_y", 1024), ("d_x1", 1024)):
            dbg[nm] = P.dram(nm, nc.dram_tensor(nm, [rows, HALF], F32, kind="ExternalOutput").ap())
        dbg["d_c"] = P.dram("d_c", nc.dram_tensor("d_c", [64, 516], F32, kind="ExternalOutput").ap())
        dbg["d_misc"] = P.dram("d_misc", nc.dram_tensor("d_misc", [128, 4096], F32, kind="ExternalOutput").ap())

    xo_v = V(xo_d, xo_d.ap.rearrange("(kc p) t -> p kc t", p=128))
    xp_v = V(xp_d, xp_d.ap.rearrange("(kc p) t -> p kc t", p=128))

    cst = P.sbuf("cst", [128, NCST], F32)

    def cf(name, rows=128):
        o, l = CSTOFF[name]
        return cst[0:rows, o:o + l]

    identb = P.sbuf("identb", [128, 128], BF16)
    onesb = P.sbuf("onesb", [128, 128], BF16)
    t2 = P.sbuf("t2", [128, NT2], BF16)
    t2v = t2.v().rearrange("p (s h c) -> p s h c", s=14, h=8, c=64)
    ring = [P.sbuf("ring%d" % i, [128, SLOT], BF16) for i in range(NSLOT)]
    xs = [P.sbuf("xs%d" % i, [128, 8, 258], F32) for i in range(2)]
    sq = P.sbuf("sq", [128, 8, 258], BF16)
    rstd = P.sbuf("rstd", [128, 512], F32)
    hT = P.sbuf("hT", [128, 8, 516], BF16)
    hh = P.sbuf("hh", [128, 8, 256], BF16)
    kmT = P.sbuf("kmT", [64, 4, 256], BF16)
    vm_aug = P.sbuf("vm_aug", [128, 2, 4, 65], BF16)
    Cst = [P.sbuf("C%d" % i, [64, 516], F32) for i in range(2)]
    Cb = [P.sbuf("Cb%d" % i, [64, 516], BF16) for i in range(2)]
    hmT = P.sbuf("hmT", [128, 4, 512], BF16)
    hnaT = P.sbuf("hnaT", [128, 2, 512], BF16)
    hmemT = P.sbuf("hmemT", [64, 4, 512], BF16)
    yTs = [P.sbuf("yT%d" % i_, [128, 512], BF16) for i_ in range(8)]
    yaccs = [P.sbuf("yacc%d" % i_, [128, 512], F32) for i_ in range(2)]
    tmpys = [P.sbuf("tmpy%d" % i_, [128, 512], F32) for i_ in range(2)]
    gss = [P.sbuf("gs%d" % i_, [128, 512], F32) for i_ in range(2)]
    sqxs = [P.sbuf("sqx%d" % i_, [128, 512], BF16) for i_ in range(2)]
    rls = [P.sbuf("rl%d" % i_, [128, 512], F32) for i_ in range(2)]
    ost = [P.sbuf("ost%d" % i, [128, 512], F32) for i in range(2)]
    dstage = P.sbuf("dstage", [128, 512], F32) if debug else None
    R = P.sbuf("R", [128, RBYTES // 2], BF16)
    region = []

    def carve(name, off, parts, shape, dt):
        n = 1
        for s_ in shape:
            n *= s_
        nb = n * (4 if dt == F32 else 2)
        assert off % 4 == 0 and off + nb <= RBYTES, (name, off, nb)
        ap = R.ap[0:parts, off // 2:(off + nb) // 2]
        if dt == F32:
            ap = ap.bitcast(F32)
        if len(shape) == 2:
            ap = ap.rearrange("p (a b) -> p a b", a=shape[0], b=shape[1])
        elif len(shape) == 3:
            ap = ap.rearrange("p (a b c) -> p a b c", a=shape[0], b=shape[1], c=shape[2])
        return TT(name, ap, off, off + nb, region)

    class Lay:
        def __init__(self):
            self.o = 0

        def __call__(self, name, parts, shape, dt):
            t = carve(name, self.o, parts, shape, dt)
            self.o = t.hi + (-t.hi) % 4
            return t

    L = Lay()
    qnaT = L("qnaT", 32, [8, 512], BF16)
    knaT = L("knaT", 32, [8, 1024], BF16)
    vna = L("vna", 128, [8, 8, 33], BF16)
    sqns = [L("sqn%d" % i_, 128, [16, 32], F32) for i_ in range(2)]
    tmpns = [L("tmpn%d" % i_, 128, [16, 32], F32) for i_ in range(2)]
    qkns = [L("qkn%d" % i_, 128, [16, 32], BF16) for i_ in range(2)]
    ssns = [L("ssn%d" % i_, 128, [16], F32) for i_ in range(2)]
    hna_tok = L("hna_tok", 128, [8, 32], BF16)
    rdn = L("rdn", 128, [8], F32)
    mem_off = L.o
    PTn = L("PTn", 128, [5, 8, 128], BF16)
    sbns = [L("sbn%d" % i_, 128, [8, 128], F32) for i_ in range(2)]
    L = Lay()
    L.o = mem_off
    memq = L("memq", 64, [4, 512], BF16)
    PTm = L("PTm", 128, [4, 2, 512], BF16)
    hmem_toks = [L("hmem_tok%d" % i_, 128, [4, 64], BF16) for i_ in range(2)]
    sqfs = [L("sqf%d" % i_, 64, [512], F32) for i_ in range(2)]
    rss = [L("rs%d" % i_, 64, [512], F32) for i_ in range(2)]
    rdms = [L("rdm%d" % i_, 128, [4], F32) for i_ in range(2)]
    L = Lay()
    qkT = L("qkT", 64, [8, 512], BF16)
    ktok = L("ktok", 64, [8, 256], BF16)
    vaug = L("vaug", 64, [8, 4, 129], BF16)
    osig = L("osig", 64, [8, 512], BF16)
    hacc = L("hacc", 64, [8, 512], F32)
    zqs = [L("zq%d" % i_, 64, [516], F32) for i_ in range(2)]
    accqs = [L("accq%d" % i_, 64, [512], F32) for i_ in range(2)]
    zq, accq = zqs[0], accqs[0]
    Gt = L("Gt", 64, [8, 16], F32)
    Et = L("Et", 64, [8, 8], F32)
    Lt = L("Lt", 64, [8, 8], F32)
    tmpu = L("tmpu", 64, [8, 8], F32)
    Ut = L("Ut", 64, [8, 8], F32)
    Wt = L("Wt", 64, [8, 8], F32)
    DECt = L("DECt", 64, [8, 8], F32)
    IWt = L("IWt", 64, [8, 8], F32)
    PMs = [L("PM%d" % i_, 64, [256], BF16) for i_ in range(2)]
    vus = [L("vu%d" % i_, 64, [4, 129], BF16) for i_ in range(2)]
    t1s = [L("t1%d" % i_, 64, [4], F32) for i_ in range(2)]
    rrs = [L("rr%d" % i_, 64, [4], F32) for i_ in range(2)]
    tmphs = [L("tmph%d" % i_, 64, [2, 128], F32) for i_ in range(2)]
    ssh = L("ssh", 64, [4], F32)
    hm_tok = L("hm_tok", 64, [512], BF16)
    L = Lay()
    PB = []
    for i_ in range(2):
        PB.append(dict(
            hT=L("p_hT%d" % i_, 128, [8, 516], BF16),
            zk=L("zk%d" % i_, 128, [2, 516], F32),
            kTp=L("kTp%d" % i_, 128, [2, 512], BF16),
            accp=L("accp%d" % i_, 128, [512], F32),
            ktok128=L("ktok128%d" % i_, 128, [4, 256], BF16),
            vaug128=L("vaug128%d" % i_, 128, [4, 4, 129], BF16),
            vws=L("vws%d" % i_, 128, [4, 4, 129], BF16),
            Gp=L("Gp%d" % i_, 128, [4, 8], F32),
            Ep=L("Ep%d" % i_, 128, [4, 4], F32),
            Lp=L("Lp%d" % i_, 128, [4, 4], F32),
            WSp=L("WSp%d" % i_, 128, [4, 4], F32),
            DECp=L("DECp%d" % i_, 64, [4], F32)))
    L = Lay()
    aTs = [L("aT%d" % i_, 128, [4, 512], BF16) for i_ in range(8)]
    hnTs = [L("hnT%d" % i_, 128, [512], BF16) for i_ in range(8)]
    x1Ts = [L("x1T%d" % i_, 128, [512], F32) for i_ in range(8)]

    pb = [P.psum("pb%d" % i, [128, 512], F32) for i in range(8)]
    rot = [0]
    rotn = [2]

    def nb():
        rot[0] = (rot[0] + 1) % rotn[0]
        return pb[rot[0]]

    def pbf(i):
        return V(pb[i], pb[i].ap.bitcast(BF16))

    P.dma(cst.v(), cst_d.v(), q="sp")
    P.dma(t2.v(), t2_d.v(), q="pool")
    P.ts(t2.v(), t2.v(), 32.0 ** 0.5, ALU.mult)
    P.copy(identb.v(), cf("ident"))
    P.copy(onesb.v(), cf("ones"))
    ones64f = cst[0:64, CSTOFF["ones"][0]:CSTOFF["ones"][0] + 64]
    triAf = cf("triA", 64)
    triBf = cf("triB", 64)
    g_mix8 = cf("g_mix8")
    g_ffn8 = cf("g_ffn8")
    g_mem8 = cf("g_mem8")

    seg_ctr = [0]

    xq = ["sp"]

    def rms_load(src, n):
        xsb = xs[seg_ctr[0] % 2]
        seg_ctr[0] += 1
        P.dma(xsb[:, :, 0:n], src, q=xq[0])
        return xsb

    def rmsnorm_seg(src, g8, dst, n, xsb=None):
        if xsb is None:
            xsb = rms_load(src, n)
        P.act(sq[:, :, 0:n], xsb[:, :, 0:n], AF.Square)
        for kc in range(8):
            P.mm(pb[7][:, 0:n], onesb.v(), sq[:, kc, 0:n], start=(kc == 0), stop=(kc == 7))
        P.act(rstd[:, 0:n], pb[7][:, 0:n], AF.Sqrt, bias=EPS, scale=1.0 / D)
        P.recip(rstd[:, 0:n], rstd[:, 0:n])
        for kc in range(8):
            P.stt(dst[:, kc, :], xsb[:, kc, 0:n], g8[:, kc:kc + 1], rstd[:, 0:n], ALU.mult, ALU.mult)

    def bc_last(v, shape):
        return V(v.t, v.ap.rearrange("p (a o) -> p a o", o=1).to_broadcast(list(shape)))

    memT_v = V(memT_d, memT_d.ap.rearrange("(kc p) t -> p kc t", p=128))
    rmsnorm_seg(memT_v, g_mem8, hh[:, :, 0:256], 256)
    wkv = ring[0][:, 0:4096].rearrange("p (kc n) -> p kc n", kc=8)
    P.dma(wkv, V(wmkv_d, wmkv_d.ap.rearrange("(kc p) n -> p kc n", p=128)), q="pool")
    for h in range(4):
        ps = nb()
        for kc in range(8):
            P.mm(ps[0:64, 0:256], wkv[:, kc, h * 64:(h + 1) * 64], hh[:, kc, :], start=(kc == 0), stop=(kc == 7))
        sqf, rs = sqfs[h % 2], rss[h % 2]
        P.act(sqf[:, 0:256], ps[0:64, 0:256], AF.Square)
        ps2 = nb()
        P.mm(ps2[0:64, 0:256], ones64f, sqf[:, 0:256])
        P.act(rs[:, 0:256], ps2[0:64, 0:256], AF.Sqrt, bias=EPS, scale=1.0 / 64)
        P.recip(rs[:, 0:256], rs[:, 0:256])
        P.stt(kmT[:, h, :], ps[0:64, 0:256], cf("g_memk", 64), rs[:, 0:256], ALU.mult, ALU.mult)
    P.memset(vm_aug.v(), 1.0)
    for mc in range(2):
        ps = nb()
        for kc in range(8):
            P.mm(ps[:, 0:256], hh[:, kc, mc * 128:(mc + 1) * 128], wkv[:, kc, 256:512], start=(kc == 0), stop=(kc == 7))
        P.copy(vm_aug[:, mc, :, 0:64], ps[:, 0:256].rearrange("p (h d) -> p h d", h=4))

    def conv_chunks():
        for i, (nm, ln) in enumerate(CHUNKS):
            o, l = CHOFF[nm]
            g = wbf_grp[min(NGRP - 1, i * NGRP // len(CHUNKS))]
            P.dma(V(g, wbf_ap[:, o:o + l]), wall_d[:, o:o + l], q="pool", sem_tile=g)

    def chunk_group(nm):
        i = [c[0] for c in CHUNKS].index(nm)
        return wbf_grp[min(NGRP - 1, i * NGRP // len(CHUNKS))]

    wseq = []
    for _t in range(n_main_tiles):
        wseq += [c[0] for c in CHUNKS]
    wstate = {"next_load": 0, "next_use": 0, "released": 0}

    def _pump():
        while wstate["next_load"] < len(wseq) and wstate["next_load"] - NSLOT < wstate["released"]:
            i = wstate["next_load"]
            nm = wseq[i]
            o, l = CHOFF[nm]
            g = chunk_group(nm)
            P.dma(ring[i % NSLOT][:, 0:l], V(g, wbf_ap[:, o:o + l]), q="sp")
            wstate["next_load"] = i + 1

    def next_w(nm, hold=False):
        i = wstate["next_use"]
        assert wseq[i] == nm, (wseq[i], nm)
        if not hold:
            wstate["released"] = i
        _pump()
        assert wstate["next_load"] > i
        wstate["next_use"] = i + 1
        return ring[i % NSLOT]

    def gate_math(ndir):
        G4 = Gt.v().rearrange("p c (d e) -> p c d e", d=2, e=8)
        E4 = Et.v().rearrange("p c (d e) -> p c d e", d=2, e=4)
        L4 = Lt.v().rearrange("p c (d e) -> p c d e", d=2, e=4)
        U4 = tmpu.v().rearrange("p c (d e) -> p c d e", d=2, e=4)
        nd = ndir
        P.act(E4[:, :, 0:nd, :], G4[:, :, 0:nd, 4:8], AF.Exp, scale=-1.0)
        P.act(L4[:, :, 0:nd, :], E4[:, :, 0:nd, :], AF.Ln, bias=1.0)
        psb = pb[5]
        for c in range(NCH):
            P.mm(psb[0:64, c * 24:c * 24 + 4], triAf, Lt[:, c, 0:4])
            if nd == 2:
                P.mm(psb[0:64, c * 24 + 4:c * 24 + 8], triBf, Lt[:, c, 4:8])
            P.mm(psb[0:64, c * 24 + 8:c * 24 + 8 + 4 * nd], ones64f, Lt[:, c, 0:4 * nd])
        psv = psb[0:64, 0:NCH * 24].rearrange("p (c e) -> p c e", c=NCH)
        bp4 = psv[:, :, 0:8].rearrange("p c (d e) -> p c d e", d=2)
        P.tt(U4[:, :, 0:nd, :], G4[:, :, 0:nd, 0:4], bp4[:, :, 0:nd, :], ALU.add)
        P.act(Ut[:, :, 0:4 * nd], tmpu[:, :, 0:4 * nd], AF.Exp)
        P.act(Wt[:, :, 0:4 * nd], psv[:, :, 0:4 * nd], AF.Exp, scale=-1.0, bias=LN8)
        if nd == 2:
            P.act(IWt.v(), psv[:, :, 0:8], AF.Exp, scale=1.0, bias=-LN8)
        P.act(DECt[:, :, 0:4 * nd], psv[:, :, 8:8 + 4 * nd], AF.Exp, scale=-1.0)

    def state_update(c, x, pdb):
        Cx = Cst[x]
        for bk in range(2):
            P.tt(Cx[:, bk * 258:(bk + 1) * 258], pdb[bk][0:64, 0:258], Cx[:, bk * 258:(bk + 1) * 258], ALU.add)
        C3 = Cx.v().rearrange("p (h e) -> p h e", h=4)
        P.tt(C3, C3, bc_last(DECt[:, c, x * 4:(x + 1) * 4], [64, 4, 129]), ALU.mult)

    pre_ctr = [0]

    def dmisc0(col, src):
        rows, n = src.ap.shape[0], src.ap.shape[1]
        P.copy(dstage[0:rows, 0:n], src)
        P.dma(dbg["d_misc"][0:rows, col:col + n], dstage[0:rows, 0:n], q="pool")

    def gen_pre(ch, src_v, col0s, convw, gsel, Cx, save, wq, wv, wg):
        B_ = PB[ch]
        hT, zk, kTp, accp, ktok128, vaug128, vws = (B_[k_] for k_ in ("hT", "zk", "kTp", "accp", "ktok128", "vaug128", "vws"))
        Gp, Ep, Lp, WSp, DECp = (B_[k_] for k_ in ("Gp", "Ep", "Lp", "WSp", "DECp"))
        mybanks = [pb[2 * ch], pb[2 * ch + 1]]
        cnt = [0]

        def mb():
            cnt[0] += 1
            return mybanks[cnt[0] % 2]

        gcol = ch * 64
        for ti, col0 in enumerate(col0s):
            if save:
                P.dma(cstart_d[ti].v(), Cx.v(), q="sp")
            for sg in range(2):
                rmsnorm_seg(src_v[:, :, col0 + sg * 258:col0 + (sg + 1) * 258], g_mix8, hT[:, :, sg * 258:(sg + 1) * 258], 258)
                yield
            for blk in range(2):
                for sg in range(2):
                    ps = mb()
                    for kc in range(8):
                        P.mm(ps[:, 0:258], wq[:, kc, 256 + blk * 128:256 + (blk + 1) * 128], hT[:, kc, sg * 258:(sg + 1) * 258],
                             start=(kc == 0), stop=(kc == 7))
                    P.copy(zk[:, blk, sg * 258:(sg + 1) * 258], ps[:, 0:258], eng="act")
                yield
                P.ts(accp.v(), zk[:, blk, 0:512], convw[:, blk * 5:blk * 5 + 1], ALU.mult)
                for j in range(1, 5):
                    P.stt(accp.v(), zk[:, blk, j:j + 512], convw[:, blk * 5 + j:blk * 5 + j + 1], accp.v(), ALU.mult, ALU.add)
                P.act(kTp[:, blk, :], accp.v(), AF.Silu, bias=cf("convb_p")[:, blk:blk + 1])
                yield
            for tb in range(4):
                ps = mb()
                for kc in range(8):
                    P.mm(ps.v(), hT[:, kc, 2 + tb * 128:2 + (tb + 1) * 128], wv[:, kc, :], start=(kc == 0), stop=(kc == 7))
                P.copy(vaug128[:, tb, :, 0:128], ps.v().rearrange("p (h e) -> p h e", h=4), eng="act")
                for kc in range(8):
                    P.mm(pb[6][:, gcol + tb * 16:gcol + (tb + 1) * 16], hT[:, kc, 2 + tb * 128:2 + (tb + 1) * 128], wg[:, kc, :],
                         start=(kc == 0), stop=(kc == 7))
                pt = pbf(4 + ch)
                for blk in range(2):
                    P.transpose(pt[:, (tb % 2) * 256 + blk * 128:(tb % 2) * 256 + (blk + 1) * 128], kTp[:, blk, tb * 128:(tb + 1) * 128], identb.v())
                P.copy(ktok128[:, tb, :], pt[:, (tb % 2) * 256:(tb % 2) * 256 + 256])
                yield
            gb = cf("gbias")
            for tb in range(4):
                P.tt(Gp[:, tb, :], pb[6][:, gcol + tb * 16 + gsel * 8:gcol + tb * 16 + gsel * 8 + 8], gb[:, gsel * 8:gsel * 8 + 8], ALU.add)
            P.act(Ep.v(), Gp[:, :, 4:8], AF.Exp, scale=-1.0)
            P.act(Lp.v(), Ep.v(), AF.Ln, bias=1.0)
            yield
            pss_ = pb[7]
            pc = 32 * ch + 320
            sut = cf("sut128")
            onesf = cf("ones")
            for tb in range(4):
                P.mm(pss_[:, pc + tb * 4:pc + (tb + 1) * 4], sut, Lp[:, tb, :], start=True, stop=(tb == 3))
                for t2_ in range(tb + 1, 4):
                    P.mm(pss_[:, pc + tb * 4:pc + (tb + 1) * 4], onesf, Lp[:, t2_, :], start=False, stop=(t2_ == 3))
            for tb in range(4):
                P.mm(pss_[0:64, pc + 16:pc + 20], onesf[:, 0:64], Lp[:, tb, :], start=(tb == 0), stop=(tb == 3))
            P.tt(WSp.v(), Gp[:, :, 0:4], pss_[:, pc:pc + 16].rearrange("p (t h) -> p t h", t=4), ALU.subtract)
            P.act(WSp.v(), WSp.v(), AF.Exp)
            P.act(DECp.v(), pss_[0:64, pc + 16:pc + 20], AF.Exp, scale=-1.0)
            yield
            for tb in range(4):
                P.tt(vws[:, tb], vaug128[:, tb], bc_last(WSp[:, tb, :], [128, 4, 129]), ALU.mult)
            yield
            pdb = mybanks
            for h in range(4):
                for tb in range(4):
                    P.mm(pdb[h // 2][0:64, (h % 2) * 129:(h % 2) * 129 + 129], ktok128[:, tb, h * 64:(h + 1) * 64], vws[:, tb, h, :],
                         start=(tb == 0), stop=(tb == 3))
            C3 = Cx.v().rearrange("p (h e) -> p h e", h=4)
            P.tt(C3, C3, bc_last(DECp.v(), [64, 4, 129]), ALU.mult)
            for bk in range(2):
                P.tt(Cx[:, bk * 258:(bk + 1) * 258], pdb[bk][0:64, 0:258], Cx[:, bk * 258:(bk + 1) * 258], ALU.add)
            yield
        if save:
            P.dma(cstart_d[len(col0s)].v(), Cx.v(), q="sp")

    if do_pre:
        o_, l_ = CHOFF["qk_m"]
        P.dma(ring[1][:, 0:l_], wall_d[:, o_:o_ + l_], q="pool")
        o_, l_ = CHOFF["v_m"]
        P.dma(ring[2][:, 0:l_], wall_d[:, o_:o_ + l_], q="pool")
        o_, l_ = CHOFF["g16"]
        P.dma(ring[3][:, 0:l_], wall_d[:, o_:o_ + l_], q="pool")
    conv_chunks()
    if do_pre:
        wq_p = ring[1][:, 0:4096].rearrange("p (kc n) -> p kc n", kc=8)
        wv_p = ring[2][:, 0:4096].rearrange("p (kc n) -> p kc n", kc=8)
        wg_p = ring[3][:, 0:128].rearrange("p (kc n) -> p kc n", kc=8)
        for B_ in PB:
            P.memset(B_["vaug128"][:, :, :, 128:129], 1.0)
        P.memset(Cst[0].v(), 0.0)
        P.memset(Cst[1].v(), 0.0)
        gens = [gen_pre(0, xp_v, [t * TILE for t in range(NT)], cf("convw_pp"), 1, Cst[1], False, wq_p, wv_p, wg_p),
                gen_pre(1, xo_v, [256 + t * TILE - 2 for t in range(NT - 1)], cf("convw_po"), 0, Cst[0], True, wq_p, wv_p, wg_p)]
        while gens:
            for g_ in list(gens):
                try:
                    next(g_)
                except StopIteration:
                    gens.remove(g_)
        if debug:
            P.dma(dbg["d_c"].v(), Cst[1].v(), q="pool")
    else:
        P.memset(Cst[1].v(), 0.0)
        P.memset(Cst[0].v(), 0.0)
        for t in range(NT):
            P.dma(cstart_d[t].v(), Cst[0].v(), q="pool")
    P.copy(Cb[1].v(), Cst[1].v())
    xq[0] = "pool"
    rotn[0] = 4

    def dump(nm, view_rows, tile_i, src):
        rows = src.ap.shape[0]
        P.copy(dstage[0:rows, :], src)
        P.dma(dbg[nm][view_rows, tile_i * TILE:(tile_i + 1) * TILE], dstage[0:rows, :], q="pool")

    def dmisc(col, src):
        rows, n = src.ap.shape[0], src.ap.shape[1]
        P.copy(dstage[0:rows, 0:n], src)
        P.dma(dbg["d_misc"][0:rows, col:col + n], dstage[0:rows, 0:n], q="pool")

    hT_for = [None]

    def center_load(tile_i):
        c0 = 256 + tile_i * TILE
        return [rms_load(xo_v[:, :, c0 - 2 + sg * 258:c0 - 2 + (sg + 1) * 258], 258) for sg in range(2)]

    def center_norm(tile_i, bufs=None):
        c0 = 256 + tile_i * TILE
        for sg in range(2):
            rmsnorm_seg(xo_v[:, :, c0 - 2 + sg * 258:c0 - 2 + (sg + 1) * 258], g_mix8, hT[:, :, sg * 258:(sg + 1) * 258], 258,
                        xsb=(bufs[sg] if bufs else None))
        hT_for[0] = tile_i

    def main_tile(tile_i):
        c0 = 256 + tile_i * TILE
        hc = hT[:, :, 2:514]
        if hT_for[0] != tile_i:
            center_norm(tile_i)

        Wq = next_w("memq")[:, 0:2048].rearrange("p (kc n) -> p kc n", kc=8)
        Wnq = next_w("na_q", hold=True)[:, 0:2048].rearrange("p (kc n) -> p kc n", kc=8)
        Wnkv = next_w("na_kv", hold=True)[:, 0:4096].rearrange("p (kc n) -> p kc n", kc=8)

        def gen_mem():
            mcnt = [0]

            def mbk():
                mcnt[0] += 1
                return pb[mcnt[0] % 2]

            for h in range(4):
                sqf, rs = sqfs[h % 2], rss[h % 2]
                ps = mbk()
                for kc in range(8):
                    P.mm(ps[0:64, :], Wq[:, kc, h * 64:(h + 1) * 64], hc[:, kc, :], start=(kc == 0), stop=(kc == 7))
                P.act(sqf.v(), ps[0:64, :], AF.Square)
                ps2 = mbk()
                P.mm(ps2[0:64, :], ones64f, sqf.v())
                P.act(rs.v(), ps2[0:64, :], AF.Sqrt, bias=EPS, scale=1.0 / 64)
                P.recip(rs.v(), rs.v())
                P.stt(memq[:, h, :], ps[0:64, :], cf("g_memq", 64), rs.v(), ALU.mult, ALU.mult)
                yield
                for mc in range(2):
                    ps3 = mbk()
                    P.mm(ps3.v(), kmT[:, h, mc * 128:(mc + 1) * 128], memq[:, h, :])
                    P.act(PTm[:, h, mc, :], ps3.v(), AF.Exp, scale=0.125)
                yield
            for tb in range(4):
                hmem_tok, rdm = hmem_toks[tb % 2], rdms[tb % 2]
                pv = pb[6]
                for h in range(4):
                    for mc in range(2):
                        P.mm(pv[:, h * 65:(h + 1) * 65], PTm[:, h, mc, tb * 128:(tb + 1) * 128], vm_aug[:, mc, h, :],
                             start=(mc == 0), stop=(mc == 1))
                pv3 = pv[:, 0:260].rearrange("p (h e) -> p h e", h=4)
                P.recip(rdm.v(), pv3[:, :, 64])
                P.tt(hmem_tok.v(), pv3[:, :, 0:64], bc_last(rdm.v(), [128, 4, 64]), ALU.mult)
                pt = pbf(7)
                for h in range(4):
                    P.transpose(pt[0:64, h * 128:(h + 1) * 128], hmem_tok[:, h, :], identb.v())
                P.copy(hmemT[:, :, tb * 128:(tb + 1) * 128], pt[0:64, 0:512].rearrange("p (h t) -> p h t", h=4), eng="act")
                yield

        gq3 = cf("g_naqk").rearrange("p (h d) -> p h d", h=16)

        def gen_na(par):
            sqn, tmpn, qkn, ssn = sqns[par], tmpns[par], qkns[par], ssns[par]
            for jb in (2 + par, 4 + par, 0 + par, 6 + par):
                own = 2 <= jb <= 5
                if par == 0 and jb == 0:
                    rmsnorm_seg(xo_v[:, :, c0 - 256:c0], g_mix8, hh.v(), 256)
                if par == 0 and jb == 6:
                    rmsnorm_seg(xo_v[:, :, c0 + 512:c0 + 768], g_mix8, hh.v(), 256)
                if own:
                    hsrc = hT[:, :, 2 + (jb - 2) * 128:2 + (jb - 1) * 128]
                else:
                    hsrc = hh[:, :, (jb % 2) * 128:(jb % 2 + 1) * 128]
                pkv = pb[2 + par]
                for kc in range(8):
                    P.mm(pkv.v(), hsrc[:, kc, :], Wnkv[:, kc, :], start=(kc == 0), stop=(kc == 7))
                P.act(sqn[:, 0:8, :], pkv[:, 0:256].rearrange("p (h d) -> p h d", h=8), AF.Square)
                nh = 8
                if own:
                    pq = pb[4][:, par * 256:(par + 1) * 256]
                    for kc in range(8):
                        P.mm(pq, hsrc[:, kc, :], Wnq[:, kc, :], start=(kc == 0), stop=(kc == 7))
                    P.act(sqn[:, 8:16, :], pq.rearrange("p (h d) -> p h d", h=8), AF.Square)
                    nh = 16
                P.reduce(ssn[:, 0:nh], sqn[:, 0:nh, :], ALU.add)
                P.act(ssn[:, 0:nh], ssn[:, 0:nh], AF.Sqrt, bias=EPS, scale=1.0 / 32)
                P.recip(ssn[:, 0:nh], ssn[:, 0:nh])
                yield
                P.tt(tmpn[:, 0:8, :], pkv[:, 0:256].rearrange("p (h d) -> p h d", h=8), bc_last(ssn[:, 0:8], [128, 8, 32]), ALU.mult)
                P.tt(qkn[:, 0:8, :], tmpn[:, 0:8, :], gq3[:, 8:16, :], ALU.mult)
                if own:
                    P.tt(tmpn[:, 8:16, :], pq.rearrange("p (h d) -> p h d", h=8), bc_last(ssn[:, 8:16], [128, 8, 32]), ALU.mult)
                    P.tt(qkn[:, 8:16, :], tmpn[:, 8:16, :], gq3[:, 0:8, :], ALU.mult)
                P.copy(vna[:, jb, :, 0:32], pkv[:, 256:512].rearrange("p (h d) -> p h d", h=8), eng="act")
                yield
                pt = pbf(5 if par == 0 else 7)
                for h in range(8):
                    P.transpose(pt[0:32, h * 128:(h + 1) * 128], qkn[:, h, :], identb.v())
                P.copy(knaT[:, :, jb * 128:(jb + 1) * 128], pt[0:32, 0:1024].rearrange("p (h t) -> p h t", h=8))
                if own:
                    for h in range(8):
                        P.transpose(pt[0:32, h * 128:(h + 1) * 128], qkn[:, 8 + h, :], identb.v())
                    P.copy(qnaT[:, :, (jb - 2) * 128:(jb - 1) * 128], pt[0:32, 0:1024].rearrange("p (h t) -> p h t", h=8), eng="act")
                yield

        P.memset(vna[:, :, :, 32:33], 1.0)
        gens = [gen_mem(), gen_na(0), gen_na(1)]
        while gens:
            for g_ in list(gens):
                try:
                    next(g_)
                except StopIteration:
                    gens.remove(g_)
        if debug:
            for h in range(4):
                dump("d_hmem", slice(h * 64, (h + 1) * 64), tile_i, hmemT[:, h, :])

        for B in range(4):
            edge = (tile_i == 0 and B < 2)
            kbs = [2, 3, 4, 5] if edge else [B, B + 1, B + 2, B + 3, B + 4]
            for i, kb in enumerate(kbs):
                scb = [pb[4], pb[6]] if i % 2 == 0 else [pb[2], pb[3]]
                sbn = sbns[i % 2]
                d0 = 2 * kb - 4 - 2 * B
                if not edge:
                    s0 = 4 - d0
                    assert 0 <= s0 and s0 + 1 < 10, s0
                    for hb in range(2):
                        rb = t2v[:, s0:s0 + 2, hb * 4:(hb + 1) * 4, :].rearrange("p q h c -> p h q c")
                        P.mm(scb[hb].v(), identb.v(), rb, start=True, stop=False)
                        for h4 in range(4):
                            h = hb * 4 + h4
                            P.mm(scb[hb][:, h4 * 128:(h4 + 1) * 128], knaT[:, h, kb * 128:(kb + 1) * 128],
                                 qnaT[:, h, B * 128:(B + 1) * 128], start=False, stop=(h4 == 3))
                        P.act(PTn[:, i, hb * 4:(hb + 1) * 4, :], scb[hb].v().rearrange("p (h q) -> p h q", h=4), AF.Exp,
                              scale=32.0 ** -0.5)
                    continue
                for h in range(8):
                    P.mm(scb[h // 4][:, (h % 4) * 128:(h % 4 + 1) * 128], knaT[:, h, kb * 128:(kb + 1) * 128],
                         qnaT[:, h, B * 128:(B + 1) * 128])
                for qo in range(2):
                    d = d0 - qo
                    slot = 4 - d if d <= 2 else 10 + (d - 3)
                    assert 0 <= slot < 14, (slot, d)
                    for hb in range(2):
                        sc3 = scb[hb].v().rearrange("p (h q) -> p h q", h=4)
                        P.tt(sbn[:, hb * 4:(hb + 1) * 4, qo * 64:(qo + 1) * 64], sc3[:, :, qo * 64:(qo + 1) * 64],
                             t2v[:, slot, hb * 4:(hb + 1) * 4, :], ALU.add)
                P.act(PTn[:, i], sbn.v(), AF.Exp, scale=32.0 ** -0.5)
            pv = pb[7]
            for h in range(8):
                for i, kb in enumerate(kbs):
                    P.mm(pv[:, h * 33:(h + 1) * 33], PTn[:, i, h, :], vna[:, kb, h, :], start=(i == 0), stop=(i == len(kbs) - 1))
            pv3 = pv[:, 0:264].rearrange("p (h e) -> p h e", h=8)
            if debug and tile_i == NT - 1 and B == 0:
                dmisc(0, qnaT[:, 0, :])
                dmisc(512, knaT[:, 0, 0:512])
                dmisc(1024, knaT[:, 0, 512:1024])
                dmisc(2048, pv[:, 0:264])
                dmisc(2312, vna[:, 2].rearrange("p h e -> p (h e)"))
            P.recip(rdn.v(), pv3[:, :, 32])
            P.tt(hna_tok.v(), pv3[:, :, 0:32], bc_last(rdn.v(), [128, 8, 32]), ALU.mult)
            pt = pbf(5)
            hna2 = hna_tok.v().rearrange("p h d -> p (h d)")
            for k2 in range(2):
                P.transpose(pt[:, k2 * 128:(k2 + 1) * 128], hna2[:, k2 * 128:(k2 + 1) * 128], identb.v())
            P.copy(hnaT[:, :, B * 128:(B + 1) * 128], pt[:, 0:256].rearrange("p (k t) -> p k t", k=2))
        if debug:
            for k2 in range(2):
                dump("d_hna", slice(k2 * 128, (k2 + 1) * 128), tile_i, hnaT[:, k2, :])

        Wqk = next_w("qk_m")[:, 0:4096].rearrange("p (kc n) -> p kc n", kc=8)
        Wv = next_w("v_m", hold=True)[:, 0:4096].rearrange("p (kc n) -> p kc n", kc=8)
        Wo = next_w("o_m", hold=True)[:, 0:4096].rearrange("p (kc n) -> p kc n", kc=8)
        P.memset(vaug[:, :, :, 128:129], 1.0)
        cwm = cf("convw_m", 64)
        for g in range(8):
            zq_, accq_ = zqs[g % 2], accqs[g % 2]
            for sg in range(2):
                ps = nb()
                for kc in range(8):
                    P.mm(ps[0:64, 0:258], Wqk[:, kc, g * 64:(g + 1) * 64], hT[:, kc, sg * 258:(sg + 1) * 258],
                         start=(kc == 0), stop=(kc == 7))
                P.copy(zq_[:, sg * 258:(sg + 1) * 258], ps[0:64, 0:258], eng="act")
            P.ts(accq_.v(), zq_[:, 0:512], cwm[:, g * 5:g * 5 + 1], ALU.mult)
            for j in range(1, 5):
                P.stt(accq_.v(), zq_[:, j:j + 512], cwm[:, g * 5 + j:g * 5 + j + 1], accq_.v(), ALU.mult, ALU.add)
            P.act(qkT[:, g, :], accq_.v(), AF.Silu, bias=cf("convb_m", 64)[:, g:g + 1])
            c = g
            ps = nb()
            for kc in range(8):
                P.mm(ps[0:64, :], hT[:, kc, 2 + c * 64:2 + (c + 1) * 64], Wv[:, kc, :], start=(kc == 0), stop=(kc == 7))
            P.copy(vaug[:, c, :, 0:128], ps[0:64, :].rearrange("p (h e) -> p h e", h=4), eng="act")
            if g >= 4:
                for c2 in (2 * (g - 4), 2 * (g - 4) + 1):
                    ps = nb()
                    for kc in range(8):
                        P.mm(ps[0:64, :], hT[:, kc, 2 + c2 * 64:2 + (c2 + 1) * 64], Wo[:, kc, :], start=(kc == 0), stop=(kc == 7))
                    P.copy(hacc[:, c2, :], ps[0:64, :], eng="act")
        P.act(osig.v(), hacc.v(), AF.Sigmoid)
        Wg = next_w("g16")[:, 0:128].rearrange("p (kc n) -> p kc n", kc=8)
        for c in range(NCH):
            for kc in range(8):
                P.mm(pb[6][0:64, c * 16:(c + 1) * 16], hT[:, kc, 2 + c * 64:2 + (c + 1) * 64], Wg[:, kc, :], start=(kc == 0), stop=(kc == 7))
        gb = cf("gbias", 64)
        for c in range(NCH):
            P.tt(Gt[:, c, :], pb[6][0:64, c * 16:(c + 1) * 16], gb, ALU.add)
        gate_math(2)
        for c in range(NCH):
            pt = pbf(4)
            for h in range(4):
                P.transpose(pt[0:64, h * 64:(h + 1) * 64], qkT[:, 4 + h, c * 64:(c + 1) * 64], identb[0:64, 0:64])
            P.copy(ktok[:, c, :], pt[0:64, 0:256])
        P.dma(Cst[0].v(), cstart_d[tile_i].v(), q="pool")
        P.copy(Cb[0].v(), Cst[0].v(), eng="act")

        vu2 = zqs[1].v().bitcast(BF16)

        def vubuf(x, par):
            if par == 0:
                return vus[x].v()
            return vu2[:, x * 516:(x + 1) * 516].rearrange("p (h e) -> p h e", h=4)

        def make_vu(c, x, par):
            P.tt(vubuf(x, par), vaug[:, c], bc_last(Ut[:, c, x * 4:(x + 1) * 4], [64, 4, 129]), ALU.mult, eng="pool")

        def st_mm(c, x):
            cs = slice(c * 64, (c + 1) * 64)
            st = pb[0] if x == 0 else pb[3]
            for h in range(4):
                P.mm(st[0:64, h * 64:(h + 1) * 64], qkT[:, 4 + h, cs], qkT[:, h, cs])

        def pm_op(c, x):
            st = pb[0] if x == 0 else pb[3]
            P.tt(PMs[x].v(), st[0:64, 0:256], cf("mask4A" if x == 0 else "mask4B", 64), ALU.mult)

        def front_a(c, x):
            st_mm(c, x)
            pm_op(c, x)

        def front_b(c, x, par):
            PM, vu = PMs[x], vubuf(x, par)
            cs = slice(c * 64, (c + 1) * 64)
            pob = [pb[1], pb[2]] if x == 0 else [pb[4], pb[6]]
            for h in range(4):
                po_h = pob[h // 2][0:64, (h % 2) * 129:(h % 2) * 129 + 129]
                P.mm(po_h, PM[:, h * 64:(h + 1) * 64], vu[:, h, :], start=True, stop=False)
                P.mm(po_h, qkT[:, h, cs], Cb[x][:, h * 129:(h + 1) * 129], start=False, stop=True)

        def norm_abs(c, x):
            t1 = t1s[x]
            pob = [pb[1], pb[2]] if x == 0 else [pb[4], pb[6]]
            for bk in range(2):
                po3 = pob[bk][0:64, 0:258].rearrange("p (h e) -> p h e", h=2)
                P.act(t1[:, bk * 2:(bk + 1) * 2].rearrange("p (h o) -> p h o", o=1), po3[:, :, 128:129], AF.Abs)

        def norm_dve(c, x):
            t1, rr = t1s[x], rrs[x]
            P.tt(t1.v(), t1.v(), IWt[:, c, x * 4:(x + 1) * 4], ALU.max)
            P.recip(rr.v(), t1.v())

        def norm_copy(c, x, first):
            rr, tmph = rrs[x], tmphs[x]
            pob = [pb[1], pb[2]] if x == 0 else [pb[4], pb[6]]
            h4 = hacc[:, c, :].rearrange("p (h e) -> p h e", h=4)
            for bk in range(2):
                for h2 in range(2):
                    h = bk * 2 + h2
                    src = pob[bk][0:64, h2 * 129:h2 * 129 + 128]
                    if first:
                        P.act(h4[:, h, :], src, AF.Copy, scale=rr[:, h:h + 1])
                    else:
                        P.act(tmph[:, h2, :], src, AF.Copy, scale=rr[:, h:h + 1])
                if not first:
                    P.tt(h4[:, bk * 2:(bk + 1) * 2, :], h4[:, bk * 2:(bk + 1) * 2, :], tmph.v(), ALU.add, eng="pool")

        def state_mm(c, x, par):
            vu = vubuf(x, par)
            pdv = pb[5] if x == 0 else pb[7]
            pdn = (pb[0] if x == 0 else pb[3])
            for h in range(4):
                P.mm(pdv[0:64, h * 128:(h + 1) * 128], ktok[:, c, h * 64:(h + 1) * 64], vu[:, h, 0:128])
            for h in range(4):
                P.mm(pdn[0:64, 256 + h:257 + h], ktok[:, c, h * 64:(h + 1) * 64], vu[:, h, 128:129])

        def state_dve(c, x):
            pdv = pb[5] if x == 0 else pb[7]
            pdn = (pb[0] if x == 0 else pb[3])
            C3 = Cst[x].v().rearrange("p (h e) -> p h e", h=4)
            P.tt(C3[:, :, 0:128], pdv[0:64, :].rearrange("p (h e) -> p h e", h=4), C3[:, :, 0:128], ALU.add)
            P.tt(C3[:, :, 128:129], pdn[0:64, 256:260].rearrange("p (h o) -> p h o", o=1), C3[:, :, 128:129], ALU.add)
            decb = bc_last(DECt[:, c, x * 4:(x + 1) * 4], [64, 4, 129])
            P.tt(Cb[x].v().rearrange("p (h e) -> p h e", h=4), C3, decb, ALU.mult)
            P.tt(C3, C3, decb, ALU.mult, eng="pool")

        def finalize(c, k):
            sqh_, hn_ = zqs[k % 2][:, 0:512], accqs[k % 2].v()
            hmt = hm_tok.v() if k % 2 == 0 else vus[0].v().rearrange("p h e -> p (h e)")[:, 0:512]
            ssh_ = t1s[k % 2]
            P.tt(sqh_, hacc[:, c, :], hacc[:, c, :], ALU.mult, eng="pool")
            P.reduce(ssh_.v(), sqh_.rearrange("p (h e) -> p h e", h=4), ALU.add)
            P.act(ssh_.v(), ssh_.v(), AF.Sqrt, bias=EPS, scale=1.0 / 128)
            P.recip(ssh_.v(), ssh_.v())
            P.tt(hn_.rearrange("p (h e) -> p h e", h=4), hacc[:, c, :].rearrange("p (h e) -> p h e", h=4),
                 bc_last(ssh_.v(), [64, 4, 128]), ALU.mult)
            P.tt(hn_, hn_, cf("g_mlstm", 64), ALU.mult)
            P.tt(hmt, hn_, osig[:, c, :], ALU.mult, eng="pool")
            pt = pbf(0 if k % 2 == 0 else 3)
            for k4 in range(4):
                P.transpose(pt[:, 768 + k4 * 64:768 + (k4 + 1) * 64], hmt[:, k4 * 128:(k4 + 1) * 128], identb[0:64, 0:64])
            P.copy(hmT[:, :, c * 64:(c + 1) * 64], pt[:, 768:1024].rearrange("p (k t) -> p k t", k=4), eng="act")

        make_vu(0, 0, 0)
        make_vu(NCH - 1, 1, 0)
        front_a(0, 0)
        front_a(NCH - 1, 1)
        for i_ in range(NCH):
            ca, cb = i_, NCH - 1 - i_
            par = i_ % 2
            first = i_ < NCH // 2
            if i_ + 1 < NCH:
                make_vu(ca + 1, 0, 1 - par)
                make_vu(cb - 1, 1, 1 - par)
            front_b(ca, 0, par)
            front_b(cb, 1, par)
            state_mm(ca, 0, par)
            state_mm(cb, 1, par)
            if i_ + 1 < NCH:
                st_mm(ca + 1, 0)
                st_mm(cb - 1, 1)
            norm_abs(ca, 0)
            norm_abs(cb, 1)
            norm_dve(ca, 0)
            norm_dve(cb, 1)
            if i_ + 1 < NCH:
                pm_op(ca + 1, 0)
                pm_op(cb - 1, 1)
            norm_copy(ca, 0, first)
            norm_copy(cb, 1, first)
            state_dve(ca, 0)
            state_dve(cb, 1)
        for k_, c_ in enumerate(range(NCH)):
            finalize(c_, k_)
        if debug:
            for k4 in range(4):
                dump("d_hm", slice(k4 * 128, (k4 + 1) * 128), tile_i, hmT[:, k4, :])

        for j in range(8):
            P.dma(x1Ts[j].v(), xo_v[:, j, c0:c0 + TILE], q="pool")

        for j in range(8):
            W = next_w("mrg%d" % j)
            Wgt = W[:, 0:3072].rearrange("p (kc n) -> p kc n", kc=8)
            Wpm = W[:, 3072:3584].rearrange("p (kc n) -> p kc n", kc=4)
            Wpn = W[:, 3584:3840].rearrange("p (kc n) -> p kc n", kc=2)
            Wpe = W[0:64, 3840:4352].rearrange("p (kc n) -> p kc n", kc=4)
            yacc = yaccs[j % 2]
            for b in range(3):
                gs = gss[(j * 3 + b) % 2]
                tmpy = tmpys[b % 2]
                pg = nb()
                for kc in range(8):
                    P.mm(pg.v(), Wgt[:, kc, b * 128:(b + 1) * 128], hc[:, kc, :], start=(kc == 0), stop=(kc == 7))
                P.act(gs.v(), pg.v(), AF.Sigmoid)
                pp = nb()
                if b == 0:
                    for k4 in range(4):
                        P.mm(pp.v(), Wpm[:, k4, :], hmT[:, k4, :], start=(k4 == 0), stop=(k4 == 3))
                    P.tt(yacc.v(), pp.v(), gs.v(), ALU.mult)
                elif b == 1:
                    for k2 in range(2):
                        P.mm(pp.v(), Wpn[:, k2, :], hnaT[:, k2, :], start=(k2 == 0), stop=(k2 == 1))
                    P.tt(tmpy.v(), pp.v(), gs.v(), ALU.mult)
                    P.tt(yacc.v(), yacc.v(), tmpy.v(), ALU.add)
                else:
                    for h in range(4):
                        P.mm(pp.v(), Wpe[:, h, :], hmemT[:, h, :], start=(h == 0), stop=(h == 3))
                    P.tt(tmpy.v(), pp.v(), gs.v(), ALU.mult)
                    P.tt(yTs[j].v(), yacc.v(), tmpy.v(), ALU.add)
            if debug:
                dump("d_y", slice(j * 128, (j + 1) * 128), tile_i, yTs[j].v())

        for j in range(8):
            if j % 4 == 0:
                Wout = next_w("out%d" % (j // 4))[:, 0:4096].rearrange("p (kc n) -> p kc n", kc=8)
            ps = nb()
            for kc in range(8):
                P.mm(ps.v(), Wout[:, kc, (j % 4) * 128:(j % 4 + 1) * 128], yTs[kc].v(), start=(kc == 0), stop=(kc == 7))
            P.tt(x1Ts[j].v(), x1Ts[j].v(), ps.v(), ALU.add)
            P.act(sqxs[j % 2].v(), x1Ts[j].v(), AF.Square)
            if j >= 1:
                P.mm(pb[7].v(), onesb.v(), sqxs[(j - 1) % 2].v(), start=(j == 1), stop=False)
            if debug:
                dump("d_x1", slice(j * 128, (j + 1) * 128), tile_i, x1Ts[j].v())
        P.mm(pb[7].v(), onesb.v(), sqxs[1].v(), start=False, stop=True)
        P.act(rstd.v(), pb[7].v(), AF.Sqrt, bias=EPS, scale=1.0 / D)
        P.recip(rstd.v(), rstd.v())
        for j in range(8):
            P.stt(hnTs[j].v(), x1Ts[j].v(), g_ffn8[:, j:j + 1], rstd.v(), ALU.mult, ALU.mult)

        nxt = (tile_i - 1 >= NT - n_main_tiles)
        if nxt:
            nbufs = center_load(tile_i - 1)

        for i in range(8):
            Wup = next_w("up%d" % i)[:, 0:4096].rearrange("p (kc n) -> p kc n", kc=8)
            for f4 in range(4):
                f = i * 4 + f4
                ps = nb()
                for kc in range(8):
                    P.mm(ps.v(), Wup[:, kc, f4 * 128:(f4 + 1) * 128], hnTs[kc].v(), start=(kc == 0), stop=(kc == 7))
                P.act(rls[f % 2].v(), ps.v(), AF.Relu)
                P.tt(aTs[i][:, f4, :], rls[f % 2].v(), rls[f % 2].v(), ALU.mult, eng="pool")
        if nxt:
            center_norm(tile_i - 1, nbufs)
        last = []
        for j in range(8):
            Wdn = next_w("dn%d" % j)[:, 0:4096].rearrange("p (f n) -> p f n", f=32)
            ps = nb()
            for f in range(32):
                P.mm(ps.v(), Wdn[:, f, :], aTs[f // 4][:, f % 4, :], start=(f == 0), stop=(f == 31))
            o_ = ost[j % 2]
            P.tt(o_.v(), ps.v(), x1Ts[j].v(), ALU.add)
            last.append(P.dma(outT_d[j * 128:(j + 1) * 128, tile_i * TILE:(tile_i + 1) * TILE], o_.v(), q="pool"))
        return last

    finals = []
    for tile_i in range(NT - 1, NT - 1 - n_main_tiles, -1):
        finals = main_tile(tile_i)
    fin = list(finals[-2:])
    for e in ("pool",):
        dmas = [op for op in P.ops[e] if op.is_dma]
        if dmas:
            fin.append(dmas[-1])
    for op in fin:
        op.needed = True
    P.emit(final_wait_ops=fin)
    P.close()
    return nc, P


_CACHE = {}


def kernel(**inputs):
    maps = _host_prep(inputs)
    if "nc" not in _CACHE:
        _CACHE["nc"] = build_program()[0]
    nc = _CACHE["nc"]
    res = run_bass_kernel_spmd(nc, maps, core_ids=list(range(8)))
    return _assemble(res.results)
```
